# Optimizing a Trainium2 kernel written in Bass

```python
import math
import jax, jax.numpy as jnp
from jax import lax
import numpy as np

D_MODEL = 1024
BATCH = 16
SEQ = 256
DEPTH = 1
DEC_BATCH = 8
DEC_SEQ = 1024
PAST_LEN = 256

GRID_W = 64
MIX_W = D_MODEL
N_DIR = 2
SSD_W = MIX_W // 2
SSD_HEAD_DIM = 64
SSD_HEADS = SSD_W // SSD_HEAD_DIM
SSD_GROUPS = 2
SSD_STATE = 64
SSD_CHUNK = 128
SSD_CONV = 3
SSD_XBC = SSD_W + 2 * SSD_GROUPS * SSD_STATE
HY_W = MIX_W - SSD_W
HY_ORDER = 2
HY_CONV = 3
HY_EMB = 33
HY_BANDS = (HY_EMB - 1) // 2
HY_FILTER_HIDDEN = 64
HY_DECAY_TARGET = 1e-2
HY_FAST_PCT = 0.3
HY_SLOW_PCT = 1.5
HY_MIN_DECAY = math.log(HY_DECAY_TARGET) / HY_SLOW_PCT
HY_MAX_DECAY = math.log(HY_DECAY_TARGET) / HY_FAST_PCT
FFN_DIM = 2752
N_MOD = 9
RMS_EPS = 1e-6
IN_COLS = SSD_W + SSD_XBC + N_DIR * SSD_HEADS + (HY_ORDER + 1) * HY_W

kernel_name = 'hybrid_ssd_hyena_macaron_prefix_step'


def rmsnorm(x, g):
    xf = x.astype(jnp.float32)
    y = xf * lax.rsqrt(jnp.mean(xf * xf, axis=-1, keepdims=True) + RMS_EPS)
    return (y * g.astype(jnp.float32)).astype(x.dtype)


def modulate(h, shift, scale):
    return h * (1.0 + scale) + shift


def swiglu(h, w_gate, w_up, w_down):
    return jnp.dot(jax.nn.silu(jnp.dot(h, w_gate)) * jnp.dot(h, w_up), w_down)


def short_conv(x, w, b, n_rows):
    bsz, l, ch = x.shape
    k = w.shape[0]
    xr = x.reshape(bsz * n_rows, l // n_rows, ch)
    y = lax.conv_general_dilated(xr, w[:, None, :].astype(x.dtype), window_strides=(1,),
                                 padding=[((k - 1) // 2, k // 2)],
                                 dimension_numbers=('NWC', 'WIO', 'NWC'), feature_group_count=ch)
    return (y + b).reshape(bsz, l, ch)


def segsum(a):
    t = a.shape[-1]
    x = jnp.broadcast_to(a[..., :, None], a.shape + (t,))
    x = jnp.where(jnp.tril(jnp.ones((t, t), bool), -1), x, 0.0)
    s = jnp.cumsum(x, axis=-2)
    return jnp.where(jnp.tril(jnp.ones((t, t), bool)), s, -jnp.inf)


def ssd_chunked(xh, dt, a_coef, bh, ch, init):
    bsz, l, nh, hd = xh.shape
    nc = l // SSD_CHUNK

    def chunk(t):
        return t.reshape((bsz, nc, SSD_CHUNK) + t.shape[2:])

    xc = chunk(xh * dt[..., None])
    bc, cc = chunk(bh), chunk(ch)
    ac = chunk(dt * a_coef).transpose(0, 1, 3, 2)
    a_cum = jnp.cumsum(ac, axis=-1)
    scores = jnp.einsum('bclhn,bcshn->bchls', cc, bc) * jnp.exp(segsum(ac))
    y_diag = jnp.einsum('bchls,bcshp->bclhp', scores, xc)
    decay_states = jnp.exp(a_cum[..., -1:] - a_cum)
    states = jnp.einsum('bcshn,bchs,bcshp->bchpn', bc, decay_states, xc)
    states = jnp.concatenate([init[:, None], states], axis=1)
    a_chunk = jnp.pad(a_cum[..., -1], ((0, 0), (1, 0), (0, 0))).transpose(0, 2, 1)
    states = jnp.einsum('bhzc,bchpn->bzhpn', jnp.exp(segsum(a_chunk)), states)
    prev_states, final_state = states[:, :-1], states[:, -1]
    y_off = jnp.einsum('bclhn,bchpn,bchl->bclhp', cc, prev_states, jnp.exp(a_cum))
    return (y_diag + y_off).reshape(bsz, l, nh, hd), final_state


def ssd_mixer(z, xbc, dt_raw, init, n_rows, p):
    f32 = jnp.float32
    bsz, l, _ = z.shape
    xbc = jax.nn.silu(short_conv(xbc, p['ssd_conv_w'], p['ssd_conv_b'], n_rows)).astype(f32)
    xs, bm, cm = jnp.split(xbc, [SSD_W, SSD_W + SSD_GROUPS * SSD_STATE], axis=-1)
    hpg = SSD_HEADS // SSD_GROUPS
    xh = xs.reshape(bsz, l, SSD_HEADS, SSD_HEAD_DIM)
    bh = jnp.repeat(bm.reshape(bsz, l, SSD_GROUPS, SSD_STATE), hpg, axis=2)
    ch = jnp.repeat(cm.reshape(bsz, l, SSD_GROUPS, SSD_STATE), hpg, axis=2)
    dt = jax.nn.softplus(dt_raw.astype(f32).reshape(bsz, l, N_DIR, SSD_HEADS)
                         + p['ssd_dt_bias'].astype(f32))
    a_coef = -jnp.exp(p['ssd_a_log'].astype(f32))
    init = init.astype(f32)
    y_f, s_f = ssd_chunked(xh, dt[:, :, 0], a_coef[0], bh, ch, init[:, 0])
    y_b, s_b = ssd_chunked(jnp.flip(xh, 1), jnp.flip(dt[:, :, 1], 1), a_coef[1],
                           jnp.flip(bh, 1), jnp.flip(ch, 1), init[:, 1])
    y = y_f + jnp.flip(y_b, 1) + xh * p['ssd_d'].astype(f32)[:, None]
    y = y.reshape(bsz, l, SSD_W).astype(z.dtype) * jax.nn.silu(z)
    return rmsnorm(y, p['ssd_norm_w']), jnp.stack([s_f, s_b], axis=1)


def hyena_filters(l, p):
    f32 = jnp.float32
    t = jnp.linspace(0.0, 1.0, l, dtype=f32)[:, None]
    w = (2.0 * math.pi / l) * jnp.arange(l, dtype=f32)[:, None]
    f = jnp.linspace(1e-4, HY_BANDS - 1, HY_BANDS, dtype=f32)[None, :]
    feats = jnp.concatenate([t, jnp.cos(f * w), -jnp.sin(f * w)], axis=-1)
    freq = p['hy_freq'].astype(f32)
    h = jnp.sin(freq * (jnp.dot(feats, p['hy_w1'].astype(f32)) + p['hy_b1'].astype(f32)))
    h = jnp.sin(freq * (jnp.dot(h, p['hy_w2'].astype(f32)) + p['hy_b2'].astype(f32)))
    h = jnp.dot(h, p['hy_w3'].astype(f32)).reshape(l, N_DIR, HY_ORDER, HY_W)
    deltas = jnp.abs(jnp.linspace(HY_MIN_DECAY, HY_MAX_DECAY, HY_ORDER * HY_W, dtype=f32))
    window = jnp.exp(-t[:, :, None] * deltas.reshape(HY_ORDER, HY_W))
    return h * window[:, None]


def hyena_mixer(u, n_rows, p):
    bsz, l, _ = u.shape
    u = short_conv(u, p['hy_conv_w'], p['hy_conv_b'], n_rows).astype(jnp.float32)
    v, x1, x2 = jnp.split(u, 3, axis=-1)
    k = hyena_filters(l, p)
    k2 = jnp.concatenate([k[:, 0], jnp.zeros_like(k[:1, 0]), k[:0:-1, 1]], axis=0)
    kf = jnp.fft.rfft(k2, axis=0)
    skip = p['hy_skip'].astype(jnp.float32)
    zz = v
    for i, gate in enumerate((x1, x2)):
        sf = jnp.fft.rfft(zz, n=2 * l, axis=1)
        conv = jnp.fft.irfft(sf * kf[:, i], n=2 * l, axis=1)[:, :l]
        zz = gate * (conv + zz * skip[i])
    return zz


def token_mix(h, ssd_init, n_rows, p):
    proj = jnp.dot(h, p['w_in'])
    z, xbc, dt_raw, hy_in = jnp.split(
        proj, [SSD_W, SSD_W + SSD_XBC, SSD_W + SSD_XBC + N_DIR * SSD_HEADS], axis=-1)
    y_ssd, final_state = ssd_mixer(z, xbc, dt_raw, ssd_init, n_rows, p)
    y_hy = hyena_mixer(hy_in, n_rows, p).astype(h.dtype)
    return jnp.dot(jnp.concatenate([y_ssd, y_hy], axis=-1), p['w_out']), final_state


def trunk_layer(x, cond, ssd_init, n_rows, p):
    mod = jnp.dot(jax.nn.silu(cond), p['w_ada']) + p['b_ada']
    sh1, sc1, g1, sh2, sc2, g2, sh3, sc3, g3 = jnp.split(mod[:, None, :], N_MOD, axis=-1)
    h = modulate(rmsnorm(x, p['norm_ffn1']), sh1, sc1)
    x = x + 0.5 * g1 * swiglu(h, p['ffn1_w_gate'], p['ffn1_w_up'], p['ffn1_w_down'])
    h = modulate(rmsnorm(x, p['norm_mix']), sh2, sc2)
    mix, final_state = token_mix(h, ssd_init, n_rows, p)
    x = x + g2 * mix
    h = modulate(rmsnorm(x, p['norm_ffn2']), sh3, sc3)
    x = x + 0.5 * g3 * swiglu(h, p['ffn2_w_gate'], p['ffn2_w_up'], p['ffn2_w_down'])
    return x, final_state


def setup_inputs(seed: int = 0) -> dict:
    key = jax.random.key(seed)
    keys = jax.random.split(key, 34)
    f32 = jnp.float32

    def nrm(i, shape, scale):
        return scale * jax.random.normal(keys[i], shape, f32)

    def gain(i, shape):
        return 1.0 + nrm(i, shape, 0.1)

    L = DEPTH
    dt0 = jnp.exp(jax.random.uniform(keys[16], (L, N_DIR, SSD_HEADS), f32,
                                     minval=math.log(1e-3), maxval=math.log(1e-1)))
    dt_bias = dt0 + jnp.log(-jnp.expm1(-dt0))
    a_log = jnp.log(jax.random.uniform(keys[17], (L, N_DIR, SSD_HEADS), f32, minval=1.0, maxval=16.0))
    return {
        'x_prompt': nrm(0, (BATCH, SEQ, D_MODEL), 1.0),
        'x_sample': nrm(1, (DEC_BATCH, DEC_SEQ, D_MODEL), 1.0),
        'state_ssd': nrm(2, (DEC_BATCH, DEPTH, N_DIR, SSD_HEADS, SSD_HEAD_DIM, SSD_STATE), 0.5),
        'c': nrm(3, (DEC_BATCH, D_MODEL), 1.0),
        'c_ctx': nrm(4, (D_MODEL,), 1.0),
        'w_ada': nrm(5, (L, D_MODEL, N_MOD * D_MODEL), D_MODEL ** -0.5),
        'b_ada': nrm(6, (L, N_MOD * D_MODEL), 0.02),
        'norm_ffn1': gain(7, (L, D_MODEL)),
        'ffn1_w_gate': nrm(8, (L, D_MODEL, FFN_DIM), D_MODEL ** -0.5),
        'ffn1_w_up': nrm(9, (L, D_MODEL, FFN_DIM), D_MODEL ** -0.5),
        'ffn1_w_down': nrm(10, (L, FFN_DIM, D_MODEL), FFN_DIM ** -0.5),
        'norm_mix': gain(11, (L, D_MODEL)),
        'w_in': nrm(12, (L, D_MODEL, IN_COLS), D_MODEL ** -0.5),
        'w_out': nrm(13, (L, MIX_W, D_MODEL), MIX_W ** -0.5),
        'ssd_conv_w': nrm(14, (L, SSD_CONV, SSD_XBC), SSD_CONV ** -0.5),
        'ssd_conv_b': nrm(15, (L, SSD_XBC), 0.02),
        'ssd_dt_bias': dt_bias,
        'ssd_a_log': a_log,
        'ssd_d': gain(18, (L, SSD_HEADS)),
        'ssd_norm_w': gain(19, (L, SSD_W)),
        'hy_conv_w': nrm(20, (L, HY_CONV, (HY_ORDER + 1) * HY_W), HY_CONV ** -0.5),
        'hy_conv_b': nrm(21, (L, (HY_ORDER + 1) * HY_W), 0.02),
        'hy_w1': nrm(22, (L, HY_EMB, HY_FILTER_HIDDEN), HY_EMB ** -0.5),
        'hy_b1': nrm(23, (L, HY_FILTER_HIDDEN), 0.02),
        'hy_freq': gain(24, (L, HY_FILTER_HIDDEN)),
        'hy_w2': nrm(25, (L, HY_FILTER_HIDDEN, HY_FILTER_HIDDEN), HY_FILTER_HIDDEN ** -0.5),
        'hy_b2': nrm(26, (L, HY_FILTER_HIDDEN), 0.02),
        'hy_w3': nrm(27, (L, HY_FILTER_HIDDEN, N_DIR * HY_ORDER * HY_W), 0.1 * HY_FILTER_HIDDEN ** -0.5),
        'hy_skip': nrm(28, (L, HY_ORDER, HY_W), 0.5),
        'norm_ffn2': gain(29, (L, D_MODEL)),
        'ffn2_w_gate': nrm(30, (L, D_MODEL, FFN_DIM), D_MODEL ** -0.5),
        'ffn2_w_up': nrm(31, (L, D_MODEL, FFN_DIM), D_MODEL ** -0.5),
        'ffn2_w_down': nrm(32, (L, FFN_DIM, D_MODEL), FFN_DIM ** -0.5),
        'norm_final': gain(33, (D_MODEL,)),
    }


def reference(x_prompt, x_sample, state_ssd, c, c_ctx, w_ada, b_ada, norm_ffn1, ffn1_w_gate, ffn1_w_up,
              ffn1_w_down, norm_mix, w_in, w_out, ssd_conv_w, ssd_conv_b, ssd_dt_bias, ssd_a_log, ssd_d,
              ssd_norm_w, hy_conv_w, hy_conv_b, hy_w1, hy_b1, hy_freq, hy_w2, hy_b2, hy_w3, hy_skip,
              norm_ffn2, ffn2_w_gate, ffn2_w_up, ffn2_w_down, norm_final):
    rows = x_sample.shape[1] // GRID_W
    ctx_cond = c_ctx[None, :]
    h_ctx, h_lat = x_prompt, x_sample
    ctx_states = []
    for i in range(DEPTH):
        p = {
            'w_ada': w_ada[i], 'b_ada': b_ada[i], 'norm_ffn1': norm_ffn1[i],
            'ffn1_w_gate': ffn1_w_gate[i], 'ffn1_w_up': ffn1_w_up[i], 'ffn1_w_down': ffn1_w_down[i],
            'norm_mix': norm_mix[i], 'w_in': w_in[i], 'w_out': w_out[i],
            'ssd_conv_w': ssd_conv_w[i], 'ssd_conv_b': ssd_conv_b[i], 'ssd_dt_bias': ssd_dt_bias[i],
            'ssd_a_log': ssd_a_log[i], 'ssd_d': ssd_d[i], 'ssd_norm_w': ssd_norm_w[i],
            'hy_conv_w': hy_conv_w[i], 'hy_conv_b': hy_conv_b[i], 'hy_w1': hy_w1[i], 'hy_b1': hy_b1[i],
            'hy_freq': hy_freq[i], 'hy_w2': hy_w2[i], 'hy_b2': hy_b2[i], 'hy_w3': hy_w3[i],
            'hy_skip': hy_skip[i], 'norm_ffn2': norm_ffn2[i],
            'ffn2_w_gate': ffn2_w_gate[i], 'ffn2_w_up': ffn2_w_up[i], 'ffn2_w_down': ffn2_w_down[i],
        }
        ctx_init = jnp.zeros((x_prompt.shape[0], N_DIR, SSD_HEADS, SSD_HEAD_DIM, SSD_STATE), jnp.float32)
        h_ctx, ctx_final = trunk_layer(h_ctx, ctx_cond, ctx_init, 1, p)
        ctx_states.append(ctx_final)
        h_lat, _ = trunk_layer(h_lat, c, state_ssd[:, i], rows, p)
    y_prompt = rmsnorm(h_ctx, norm_final)
    y_sample = rmsnorm(h_lat, norm_final)
    new_state_ssd = jnp.stack(ctx_states, axis=1).astype(x_prompt.dtype)
    return (y_prompt, y_sample, new_state_ssd)
```

```python
import math
import contextlib
import numpy as np
import ml_dtypes
import concourse.bass as bass
import concourse.mybir as mybir
from concourse.bass_utils import run_bass_kernel_spmd

F32 = mybir.dt.float32
BF16 = mybir.dt.bfloat16
AF = mybir.ActivationFunctionType
ALU = mybir.AluOpType

ENGS = ('pe', 'act', 'dve', 'pool', 'sp')
D = 1024
FF = 2752
NTOK = 1536
INC = 2832
RMS_EPS = 1e-6
NFC = 22
HALVES = (list(range(0, 12)), list(range(12, 22)))
SEQS = ((0, 256, 1, 0), (256, 256, 1, 0), (512, 1024, 16, 1))
PI = math.pi


class Buf:
    __slots__ = ('name', 'w', 'rs')

    def __init__(self, name):
        self.name = name
        self.w = None
        self.rs = []


import os as _os
XLAT = float(_os.environ.get('MK_LAT', '1.0'))
_CS = float(_os.environ.get('MK_CS', '1.0'))
DEF_COST = {'pe': 0.15, 'act': 0.45 * _CS, 'dve': 0.55 * _CS, 'pool': 1.0 * _CS, 'sp': 0.06}


class Op:
    __slots__ = ('eng', 'fn', 'deps', 'inc', 'cnt', 'dma', 'dkey', 'dcnt', 'cost', 'idx', 'sdeps')

    def __init__(self, eng, fn, dma=False, dkey=None, cost=None):
        self.eng = eng
        self.fn = fn
        self.deps = []
        self.sdeps = []
        self.cost = cost if cost is not None else (2.5 if dma else DEF_COST[eng])
        self.idx = 0
        self.inc = False
        self.cnt = 0
        self.dma = dma
        self.dkey = dkey
        self.dcnt = 0


class Prog:
    def __init__(self, nc):
        self.nc = nc
        self.q = {e: [] for e in ENGS}
        self.all = []
        self.dkeys = {}
        self.groups = set()
        self.lastdma = {}

    def _add(self, op, reads, writes):
        deps = []
        for b in reads:
            if b.w is not None:
                deps.append(b.w)
        for b in writes:
            if b.w is not None:
                deps.append(b.w)
            deps.extend(b.rs)
        seen = set(id(d) for d in op.deps)
        for d in deps:
            if d is op or id(d) in seen:
                continue
            seen.add(id(d))
            if d.eng == op.eng and not d.dma and not op.dma and op.eng == 'pe':
                op.sdeps.append(d)
                continue
            op.deps.append(d)
        for b in reads:
            b.rs.append(op)
        for b in writes:
            b.w = op
            b.rs = []
        op.idx = len(self.all)
        self.all.append(op)
        return op

    def op(self, eng, fn, reads=(), writes=(), cost=None):
        return self._add(Op(eng, fn, cost=cost), list(reads), list(writes))

    def schedule(self):
        import heapq
        ops = self.all
        n = len(ops)
        succ = [[] for _ in range(n)]
        indeg = [0] * n
        for o in ops:
            ds = set(id(d) for d in o.deps) | set(id(d) for d in o.sdeps)
            o_all = {d.idx for d in o.deps} | {d.idx for d in o.sdeps}
            for di in o_all:
                succ[di].append(o.idx)
            indeg[o.idx] = len(o_all)
        ready = [0.0] * n
        fin = [0.0] * n
        efree = {e: 0.0 for e in ENGS}
        heap = [(0.0, o.idx) for o in ops if indeg[o.idx] == 0]
        heapq.heapify(heap)
        self.q = {e: [] for e in ENGS}
        done = 0
        while heap:
            r, i = heapq.heappop(heap)
            o = ops[i]
            start = max(r, efree[o.eng])
            if o.dma:
                efree[o.eng] = start + DEF_COST['sp']
                fin[i] = start + o.cost
            else:
                efree[o.eng] = start + o.cost
                fin[i] = start + o.cost
            self.q[o.eng].append(o)
            done += 1
            for j in succ[i]:
                lat = 0.05 if (ops[j].eng == o.eng and not o.dma) else XLAT
                t = fin[i] + lat
                if t > ready[j]:
                    ready[j] = t
                indeg[j] -= 1
                if indeg[j] == 0:
                    heapq.heappush(heap, (ready[j], j))
        assert done == n, (done, n)
        self.makespan = max(fin) if fin else 0.0

    def dma(self, eng, out, in_, reads=(), writes=(), dkey=None, group=False, cost=None):
        o = Op(eng, lambda e: e.dma_start(out=out, in_=in_), dma=True, dkey=dkey, cost=cost)
        self.dkeys[dkey] = self.dkeys.get(dkey, 0) + 1
        o.dcnt = self.dkeys[dkey] * 16
        if group:
            self.groups.add(dkey)
        else:
            prev = self.lastdma.get(dkey)
            if prev is not None:
                o.deps.append(prev)
            self.lastdma[dkey] = o
        return self._add(o, list(reads), list(writes))

    def emit(self, final_dkeys, sched=True):
        nc = self.nc
        if sched:
            self.schedule()
        else:
            self.q = {e: [] for e in ENGS}
            for o in self.all:
                self.q[o.eng].append(o)
        for e in ENGS:
            for o in self.q[e]:
                for d in o.deps:
                    if not d.dma:
                        d.inc = True
        for e in ENGS:
            c = 0
            for o in self.q[e]:
                if o.inc and not o.dma:
                    c += 1
                o.cnt = c
        with contextlib.ExitStack() as st:
            esem = {e: st.enter_context(nc.semaphore('s_' + e)) for e in ENGS if e != 'sp'}
            dsem = {k: st.enter_context(nc.semaphore('d_%d' % i)) for i, k in enumerate(self.dkeys)}
            block = st.enter_context(nc.Block())

            def run(engname, eng):
                known = {}
                for o in self.q[engname]:
                    need = {}
                    for d in o.deps:
                        if d.dma:
                            key, val = ('d', d.dkey), (self.dkeys[d.dkey] * 16 if d.dkey in self.groups else d.dcnt)
                        else:
                            key, val = ('e', d.eng), d.cnt
                        if val > need.get(key, 0):
                            need[key] = val
                    for key, val in need.items():
                        if known.get(key, 0) >= val:
                            continue
                        known[key] = val
                        eng.wait_ge(dsem[key[1]] if key[0] == 'd' else esem[key[1]], val)
                    ins = o.fn(eng)
                    if o.dma:
                        ins.then_inc(dsem[o.dkey], 16)
                    elif o.inc:
                        ins.then_inc(esem[engname], 1)
                if engname == 'sp':
                    for k in final_dkeys:
                        if k not in self.dkeys:
                            continue
                        eng.wait_ge(dsem[k], self.dkeys[k] * 16)

            @block.sync
            def _(e):
                run('sp', e)

            @block.tensor
            def _(e):
                run('pe', e)

            @block.scalar
            def _(e):
                run('act', e)

            @block.vector
            def _(e):
                run('dve', e)

            @block.gpsimd
            def _(e):
                run('pool', e)


class Arena:
    def __init__(self, nc, st, nbytes):
        self.t = st.enter_context(nc.sbuf_tensor("arena", [128, nbytes // 2], BF16))
        self.t32 = self.t.bitcast(F32)
        self.cap = nbytes
        self.top = 0
        self.hist = []
        self.peak = 0
        self.hw = 0

    def alloc(self, name, cols, dt, nbufs=1):
        esz = 4 if dt is F32 else 2
        start = (self.top + 31) // 32 * 32
        end = start + cols * esz
        assert end <= self.cap, (name, end, self.cap)
        self.top = end
        self.peak = max(self.peak, end)
        self.hw = max(self.hw, end)
        bufs = [Buf('%s%d' % (name, i)) for i in range(nbufs)]
        inh = []
        keep = []
        for (s, e, obs) in self.hist:
            if s < end and start < e:
                for ob in obs:
                    if ob.w is not None:
                        inh.append(ob.w)
                    inh.extend(ob.rs)
                if s >= start and e <= end:
                    continue
            keep.append((s, e, obs))
        self.hist = keep
        for b in bufs:
            b.rs = list(inh)
        self.hist.append((start, end, bufs))
        if dt is F32:
            v = self.t32[:, start // 4:start // 4 + cols]
        else:
            v = self.t[:, start // 2:start // 2 + cols]
        return v, bufs

    def mark(self):
        return self.top

    def release(self, m):
        self.top = m


def _dft_tables(L):
    N = 2 * L
    nT = L // 128
    s = np.arange(L, dtype=np.float64)[:, None]
    f = np.arange(L, dtype=np.float64)[None, :]
    C = np.cos(2 * np.pi * s * f / N)
    S = np.sin(2 * np.pi * s * f / N)
    S[:, 0] = (-1.0) ** np.arange(L)

    def blk(M):
        return np.ascontiguousarray(M.reshape(nT, 128, nT, 128).transpose(2, 1, 0, 3)).astype(ml_dtypes.bfloat16)
    return blk(C), blk(S), blk(S.T.copy())


def _filter_consts(L):
    f32 = np.float32
    t = np.linspace(0.0, 1.0, L, dtype=f32)[:, None]
    w = (f32(2.0 * math.pi / L)) * np.arange(L, dtype=f32)[:, None]
    f = np.linspace(1e-4, 16 - 1, 16, dtype=f32)[None, :]
    feats = np.concatenate([t, np.cos(f * w), -np.sin(f * w)], axis=-1).astype(f32)
    mn = math.log(1e-2) / 1.5
    mx = math.log(1e-2) / 0.3
    deltas = np.abs(np.linspace(mn, mx, 1024, dtype=f32))
    window = np.exp(-t[:, :, None] * deltas.reshape(2, 512)).astype(f32)
    wb = window.copy()
    wb[0] = 0.0
    return np.ascontiguousarray(feats.T), window, wb


def _masks():
    t = np.arange(128)
    le = (t[:, None] <= t[None, :])
    gt = (t[:, None] > t[None, :])
    ge = (t[:, None] >= t[None, :])
    lt = (t[:, None] < t[None, :])
    negf = -30000.0 * (t[None, :] < t[:, None])
    negb = -30000.0 * (t[None, :] > t[:, None])
    return np.stack([le, gt, ge, lt, negf, negb]).astype(np.float32).transpose(1, 0, 2).reshape(128, 768).copy()


PV_BADA = 0
PV_N1 = 72
PV_NM = 80
PV_N2 = 88
PV_NF = 96
PV_SCW = 104
PV_SCB = 122
PV_HCW = 128
PV_HCB = 164
PV_SNW = 176
PV_HB1 = 180
PV_HFR = 181
PV_HB2 = 182
PV_GM0 = 184
PV_GM1 = 185
NPV = 186


def build_program():
    nc = bass.Bass("TRN2", target_bir_lowering=False)

    def din(name, shape, dt=F32):
        return nc.dram_tensor(name, list(shape), dt, kind="ExternalInput").ap()

    def dout(name, shape):
        return nc.dram_tensor(name, list(shape), F32, kind="ExternalOutput").ap()

    xT_d = din("xT", [128, 8 * NTOK])
    condT_d = din("condT", [128, 16])
    stT_d = din("stT", [2, 128, 256])
    w_ada_d = din("w_ada", [D, 9 * D])
    wg_d = [din("ffn1_w_gate", [D, FF]), din("ffn2_w_gate", [D, FF])]
    wu_d = [din("ffn1_w_up", [D, FF]), din("ffn2_w_up", [D, FF])]
    wd_d = [din("ffn1_w_down", [FF, D]), din("ffn2_w_down", [FF, D])]
    w_in_d = din("w_in", [D, INC])
    w_out_d = din("w_out", [D, D])
    pv_d = din("pv", [128, NPV])
    dtb_d = din("dt_bias", [16])
    alog_d = din("a_log", [16])
    dsk_d = din("ssd_d", [8])
    skip_d = din("hy_skip", [1024])
    hw1_d = din("hy_w1", [33, 64])
    hw2_d = din("hy_w2", [64, 64])
    hw3_d = din("hy_w3", [64, 2048])
    ident_d = din("ident", [128, 128])
    masks_d = din("masks", [128, 768])
    tab_d = {}
    feats_d = {}
    win_d = {}
    for L in (256, 1024):
        nT = L // 128
        tab_d[L] = [din("tab%s_%d" % (n, L), [nT, 128, nT * 128], BF16) for n in ("C", "S", "ST")]
        feats_d[L] = din("featsT_%d" % L, [33, L])
        win_d[L] = din("win_%d" % L, [L, 2048])
    yT_d = dout("yT", [128, 8 * NTOK])
    nsT_d = dout("nsT", [2, 2, 128, 256])

    st = contextlib.ExitStack()
    with st:
        PS = [st.enter_context(nc.psum_tensor("ps%d" % i, [128, 512], F32)) for i in range(8)]
        PB = [Buf('ps%d' % i) for i in range(8)]
        A = Arena(nc, st, int(_os.environ.get("MK_CAP", "210944")))
        P = Prog(nc)
        psn = [0]

        ps_pool = [list(range(8))]

        def ps():
            pool = ps_pool[0]
            i = pool[psn[0] % len(pool)]
            psn[0] += 1
            return PS[i][:], PB[i]

        dk = [0]

        def newkey(p='k'):
            dk[0] += 1
            return '%s%d' % (p, dk[0])

        xT2, xB = A.alloc('xT', 8 * NTOK, F32, 24)
        xT = xT2.rearrange("p (k t) -> p k t", k=8)
        hT2, hB = A.alloc('hT', 8 * NTOK, BF16, 6)
        hT = hT2.rearrange("p (k t) -> p k t", k=8)
        ident, (identB,) = A.alloc('ident', 128, F32)
        masks, (masksB,) = A.alloc('masks', 768, F32)
        LE, GT, GE, LT, NEGF, NEGB = (masks[:, i * 128:(i + 1) * 128] for i in range(6))
        onesM, (onesMB,) = A.alloc('onesM', 128, BF16)
        ones32, (ones32B,) = A.alloc('ones32', 128, F32)
        pv, (pvB,) = A.alloc('pv', NPV, F32)
        mod, (modB,) = A.alloc('mod', 144, F32)
        Asc, (AscB,) = A.alloc('Asc', 48, F32)
        gsc, (gscB,) = A.alloc('gsc', 48, F32)
        dtb, (dtbB,) = A.alloc('dtb', 16, F32)
        acoef, (acoefB,) = A.alloc('acoef', 16, F32)
        dsk, (dskB,) = A.alloc('dsk', 8, F32)
        skip, (skipB,) = A.alloc('skip', 1024, F32)
        condT, (condTB,) = A.alloc('condT', 16, BF16)
        condf, (condfB,) = A.alloc('condf', 16, F32)
        hw1, (hw1B,) = A.alloc('hw1', 64, F32)
        hw2, (hw2B,) = A.alloc('hw2', 64, F32)
        hw3, (hw3B,) = A.alloc('hw3', 2048, F32)
        fb, (fbB,) = A.alloc('fb', 4, F32)
        negpi, (negpiB,) = A.alloc('negpi', 1, F32)
        epsc, (epscB,) = A.alloc('epsc', 1, F32)
        one_c, (one_cB,) = A.alloc('one_c', 1, F32)

        def xb(dc, tok0, n):
            t0 = tok0 // 512
            t1 = (tok0 + n - 1) // 512
            return [xB[dc * 3 + t] for t in range(t0, t1 + 1)]

        def hb(tok0, n):
            return [hB[i] for i in range(tok0 // 256, (tok0 + n - 1) // 256 + 1)]

        xd3 = xT_d.rearrange("p (k t) -> p k t", k=8)
        for kc in range(8):
            P.dma('sp', xT[:, kc, :], xd3[:, kc, :], writes=[xB[kc * 3 + t] for t in range(3)], dkey='xin', group=True)
        P.dma('sp', ident, ident_d, writes=[identB], dkey='c0', group=True)
        P.dma('sp', masks, masks_d, writes=[masksB], dkey='c0', group=True)
        P.dma('sp', pv, pv_d, writes=[pvB], dkey='c0', group=True)
        P.dma('sp', condf, condT_d, writes=[condfB], dkey='c0', group=True)
        P.dma('sp', dtb, dtb_d.partition_broadcast(128), writes=[dtbB], dkey='c0', group=True)
        P.dma('sp', acoef, alog_d.partition_broadcast(128), writes=[acoefB], dkey='c0', group=True)
        P.dma('sp', dsk, dsk_d.partition_broadcast(128), writes=[dskB], dkey='c0', group=True)
        P.dma('sp', skip, skip_d.partition_broadcast(128), writes=[skipB], dkey='c0', group=True)
        P.dma('sp', hw1[0:33, :], hw1_d, writes=[hw1B], dkey='c0', group=True)
        P.dma('sp', hw2[0:64, :], hw2_d, writes=[hw2B], dkey='c0', group=True)
        P.dma('sp', hw3[0:64, :], hw3_d, writes=[hw3B], dkey='c0', group=True)
        P.op('dve', lambda e: e.memset(onesM, 1.0 / D), writes=[onesMB])
        P.op('dve', lambda e: e.memset(ones32, 1.0), writes=[ones32B])
        P.op('dve', lambda e: e.memset(negpi, -PI), writes=[negpiB])
        P.op('dve', lambda e: e.memset(epsc, RMS_EPS), writes=[epscB])
        P.op('dve', lambda e: e.memset(one_c, 1.0), writes=[one_cB])
        P.op('act', lambda e: e.activation(acoef, acoef, AF.Exp), reads=[acoefB], writes=[acoefB])
        P.op('dve', lambda e: e.tensor_scalar(acoef, acoef, -1.0, None, ALU.mult), reads=[acoefB], writes=[acoefB])
        P.op('dve', lambda e: e.tensor_tensor(fb[0:64, 0:1], pv[0:64, PV_HFR:PV_HFR + 1], pv[0:64, PV_HB1:PV_HB1 + 1], ALU.mult),
             reads=[pvB], writes=[fbB])
        P.op('dve', lambda e: e.tensor_tensor(fb[0:64, 1:2], pv[0:64, PV_HFR:PV_HFR + 1], pv[0:64, PV_HB2:PV_HB2 + 1], ALU.mult),
             reads=[pvB], writes=[fbB])
        P.op('act', lambda e: e.activation(condT, condf, AF.Silu), reads=[condfB], writes=[condTB])

        m0 = A.mark()
        wab = [A.alloc('wab%d' % i, 8 * 512, BF16) for i in range(3)]
        modBs = [Buf('mod%d' % m) for m in range(9)]
        for m in range(9):
            pm, pmB = ps()
            for hb_ in range(2):
                blk = m * 2 + hb_
                wv, (wB,) = wab[blk % 3]
                wv3 = wv.rearrange("p (k n) -> p k n", k=8)
                P.dma('pool', wv3, w_ada_d[:, blk * 512:(blk + 1) * 512].rearrange("(k p) n -> p k n", p=128),
                      writes=[wB], dkey='wab%d' % (blk % 3), cost=5.0)
                for j in range(4):
                    cj = hb_ * 4 + j
                    for kc in range(8):
                        P.op('pe', lambda e, pm=pm, cj=cj, j=j, kc=kc, wv3=wv3: e.matmul(
                            pm[:, cj * 2:cj * 2 + 2], wv3[:, kc, j * 128:(j + 1) * 128], condT[:, kc * 2:kc * 2 + 2],
                            start=(kc == 0), stop=(kc == 7)), reads=[wB, condTB], writes=[pmB])
            P.op('dve', lambda e, pm=pm, m=m: e.tensor_tensor(mod[:, m * 16:(m + 1) * 16].rearrange("p (c o) -> p c o", o=2),
                                                            pm[:, 0:16].rearrange("p (c o) -> p c o", o=2),
                                                            pv[:, PV_BADA + m * 8:PV_BADA + (m + 1) * 8].unsqueeze(2).broadcast_to([128, 8, 2]), ALU.add),
                 reads=[pmB, pvB], writes=[modBs[m]])
        A.release(m0)

        def modap(m, dc, c):
            col = (m * 8 + dc) * 2 + c
            return mod[:, col:col + 1]

        AscBs = [Buf('Asc%d' % i) for i in range(3)]
        gscBs = [Buf('gsc%d' % i) for i in range(3)]
        for n, (pvo, msc, mg, gmul) in enumerate(((PV_N1, 1, 2, 0.5), (PV_NM, 4, 5, 1.0), (PV_N2, 7, 8, 0.5))):
            a3 = Asc[:, n * 16:(n + 1) * 16].rearrange("p (d c) -> p d c", c=2)
            m3 = mod[:, msc * 16:(msc + 1) * 16].rearrange("p (d c) -> p d c", c=2)
            P.op('dve', lambda e, a3=a3, m3=m3: e.tensor_scalar(a3, m3, 1.0, None, ALU.add), reads=[modBs[msc]], writes=[AscBs[n]])
            P.op('dve', lambda e, a3=a3, pvo=pvo: e.tensor_tensor(a3, a3, pv[:, pvo:pvo + 8].unsqueeze(2).broadcast_to([128, 8, 2]), ALU.mult),
                 reads=[AscBs[n], pvB], writes=[AscBs[n]])
            P.op('dve', lambda e, n=n, mg=mg, gmul=gmul: e.tensor_scalar(gsc[:, n * 16:(n + 1) * 16], mod[:, mg * 16:(mg + 1) * 16], gmul, None, ALU.mult),
                 reads=[modBs[mg]], writes=[gscBs[n]])

        def asc(n, dc, c):
            col = n * 16 + dc * 2 + c
            return Asc[:, col:col + 1]

        def gscap(n, dc, c):
            col = n * 16 + dc * 2 + c
            return gsc[:, col:col + 1]

        def norm_stage(n, final=False, outv=None, outB=None):
            m1 = A.mark()
            saved_pool = ps_pool[0]
            ps_pool[0] = [6, 7]
            sq = [A.alloc('sq%d' % i, 512, BF16) for i in range(3)]
            tmp = [A.alloc('nt%d' % i, 512, F32) for i in range(3)]
            rstd = [A.alloc('rstd%d' % i, 512, F32) for i in range(2)]
            cnt = 0
            for t in range(3):
                c = 0 if t == 0 else 1
                tk = slice(t * 512, (t + 1) * 512)
                pt, ptB = ps()
                for kc in range(8):
                    sv, (sB,) = sq[cnt % 3]
                    cnt += 1
                    P.op('act', lambda e, sv=sv, kc=kc, tk=tk: e.activation(sv, xT[:, kc, tk], AF.Square),
                         reads=[xB[kc * 3 + t]], writes=[sB])
                    P.op('pe', lambda e, sv=sv, kc=kc, pt=pt: e.matmul(pt, onesM, sv, start=(kc == 0), stop=(kc == 7)),
                         reads=[sB, onesMB], writes=[ptB], cost=0.215)
                rv, (rB,) = rstd[t % 2]
                P.op('act', lambda e, rv=rv, pt=pt: e.activation(rv, pt, AF.Sqrt, bias=epsc[:, 0:1]), reads=[ptB, epscB], writes=[rB])
                P.op('dve', lambda e, rv=rv: e.reciprocal(rv, rv), reads=[rB], writes=[rB])
                for kc in range(8):
                    tv, (tB,) = tmp[cnt % 3]
                    cnt += 1
                    P.op('dve', lambda e, tv=tv, kc=kc, tk=tk, rv=rv: e.tensor_tensor(tv, xT[:, kc, tk], rv, ALU.mult),
                         reads=[xB[kc * 3 + t], rB], writes=[tB])
                    if final:
                        P.op('act', lambda e, tv=tv, kc=kc, tk=tk: e.activation(outv[:, kc, tk], tv, AF.Identity, scale=pv[:, PV_NF + kc:PV_NF + kc + 1]),
                             reads=[tB, pvB], writes=[outB[kc * 3 + t]])
                    else:
                        P.op('act', lambda e, tv=tv, kc=kc, tk=tk, c=c: e.activation(hT[:, kc, tk], tv, AF.Identity,
                                                                                      bias=modap(3 * n, kc, c), scale=asc(n, kc, c)),
                             reads=[tB, modBs[3 * n], AscBs[n]], writes=hb(t * 512, 512))
            ps_pool[0] = saved_pool
            A.release(m1)

        def ffn(li, n):
            norm_stage(n)
            m1 = A.mark()
            ps_pool[0] = [0, 1, 2, 3, 4, 5]
            actv, actB = A.alloc('act', 12 * NTOK, BF16, 36)
            act3 = actv.rearrange("p (f t) -> p f t", f=12)
            wgb = [A.alloc('wg%d' % i, 8 * 256, BF16) for i in range(3)]
            wub = [A.alloc('wu%d' % i, 8 * 256, BF16) for i in range(3)]
            wdb = [A.alloc('wd%d' % i, 12 * 256, BF16) for i in range(4)]
            sgb = [A.alloc('sg%d' % i, 512, F32) for i in range(3)]
            kg = 'wg%d_' % li
            nblk = 0
            nsg = 0
            nwd = 0
            for hf, chunks in enumerate(HALVES):
                for i, fc in enumerate(chunks):
                    w = 128 if fc < 21 else 64
                    if fc % 2 == 0:
                        bw = 256 if fc < 20 else 192
                        gv, (gB,) = wgb[nblk % 3]
                        uv, (uB,) = wub[nblk % 3]
                        g3 = gv.rearrange("p (k n) -> p k n", k=8)
                        u3 = uv.rearrange("p (k n) -> p k n", k=8)
                        P.dma('pool', g3[:, :, 0:bw], wg_d[li][:, fc * 128:fc * 128 + bw].rearrange("(k p) n -> p k n", p=128),
                              writes=[gB], dkey=kg + 'g%d' % (nblk % 3), cost=4.0)
                        P.dma('pool', u3[:, :, 0:bw], wu_d[li][:, fc * 128:fc * 128 + bw].rearrange("(k p) n -> p k n", p=128),
                              writes=[uB], dkey=kg + 'u%d' % (nblk % 3), cost=4.0)
                        nblk += 1
                    off = (fc % 2) * 128
                    for t in range(3):
                        tk = slice(t * 512, (t + 1) * 512)
                        pg, pgB = ps()
                        pu, puB = ps()
                        for kc in range(8):
                            P.op('pe', lambda e, pg=pg, g3=g3, kc=kc, off=off, w=w, tk=tk: e.matmul(
                                pg[0:w, :], g3[:, kc, off:off + w], hT[:, kc, tk], start=(kc == 0), stop=(kc == 7)),
                                reads=[gB] + hb(t * 512, 512), writes=[pgB], cost=0.215)
                        for kc in range(8):
                            P.op('pe', lambda e, pu=pu, u3=u3, kc=kc, off=off, w=w, tk=tk: e.matmul(
                                pu[0:w, :], u3[:, kc, off:off + w], hT[:, kc, tk], start=(kc == 0), stop=(kc == 7)),
                                reads=[uB] + hb(t * 512, 512), writes=[puB], cost=0.215)
                        sv, (sB,) = sgb[nsg % 3]
                        nsg += 1
                        P.op('act', lambda e, sv=sv, pg=pg, w=w: e.activation(sv[0:w, :], pg[0:w, :], AF.Silu), reads=[pgB], writes=[sB])
                        P.op('dve', lambda e, sv=sv, pu=pu, w=w, i=i, tk=tk: e.tensor_tensor(act3[0:w, i, tk], sv[0:w, :], pu[0:w, :], ALU.mult),
                             reads=[sB, puB], writes=[actB[i * 3 + t]])
                nf = len(chunks)
                f0 = chunks[0]
                nfull = nf if chunks[-1] < 21 else nf - 1
                dblk = []
                for db in range(4):
                    dv, (dB,) = wdb[nwd % 4]
                    d3 = dv.rearrange("p (f n) -> p f n", f=12)
                    key = 'wd%d_%d' % (li, nwd % 4)
                    nwd += 1
                    P.dma('pool', d3[:, 0:nfull, :], wd_d[li][f0 * 128:(f0 + nfull) * 128, db * 256:(db + 1) * 256].rearrange("(f p) n -> p f n", p=128),
                          writes=[dB], dkey=key, cost=4.0)
                    if nfull < nf:
                        P.dma('pool', d3[0:64, nfull, :], wd_d[li][21 * 128:21 * 128 + 64, db * 256:(db + 1) * 256],
                              writes=[dB], dkey=key)
                    dblk.append((d3, dB))
                order = [(dc, t) for dc in range(8) for t in range(3)] if hf == 0 else [(dc, t) for t in range(3) for dc in range(8)]
                for dc, t in order:
                    d3, dB = dblk[dc // 2]
                    dcl = dc % 2
                    c = 0 if t == 0 else 1
                    tk = slice(t * 512, (t + 1) * 512)
                    po, poB = ps()
                    for i, fc in enumerate(chunks):
                        w = 128 if fc < 21 else 64
                        P.op('pe', lambda e, po=po, d3=d3, i=i, dcl=dcl, w=w, tk=tk, nf=nf: e.matmul(
                            po, d3[0:w, i, dcl * 128:(dcl + 1) * 128], act3[0:w, i, tk], start=(i == 0), stop=(i == nf - 1)),
                            reads=[dB, actB[i * 3 + t]], writes=[poB], cost=0.215)
                    P.op('dve', lambda e, po=po, dc=dc, tk=tk, c=c: e.scalar_tensor_tensor(
                        xT[:, dc, tk], po, gscap(n, dc, c), xT[:, dc, tk], ALU.mult, ALU.add),
                        reads=[poB, gscBs[n], xB[dc * 3 + t]], writes=[xB[dc * 3 + t]], cost=0.6)
            ps_pool[0] = list(range(8))
            A.release(m1)

        def load_w(view3, cols0, ncols, wB, key):
            P.dma('pool', view3[:, :, 0:ncols], w_in_d[:, cols0:cols0 + ncols].rearrange("(k p) n -> p k n", p=128),
                  writes=[wB], dkey=key)

        def inproj_row(w3, woff, wB, tok0, L, rawv, rawB, cvv=None, cvB=None, w1col=None, bcol=None):
            for t0 in range(0, L, 512):
                n = min(512, L - t0)
                pp, ppB = ps()
                for kc in range(8):
                    P.op('pe', lambda e, pp=pp, kc=kc, t0=t0, n=n: e.matmul(
                        pp[:, 0:n], w3[:, kc, woff:woff + 128], hT[:, kc, tok0 + t0:tok0 + t0 + n], start=(kc == 0), stop=(kc == 7)),
                        reads=[wB] + hb(tok0 + t0, n), writes=[ppB], cost=0.215 * n / 512 + 0.02)
                P.op('act', lambda e, pp=pp, t0=t0, n=n: e.copy(rawv[:, t0:t0 + n], pp[:, 0:n]), reads=[ppB], writes=[rawB])
                if cvv is not None:
                    P.op('act', lambda e, pp=pp, t0=t0, n=n: e.activation(cvv[:, t0:t0 + n], pp[:, 0:n], AF.Identity,
                                                                        bias=pv[:, bcol:bcol + 1], scale=pv[:, w1col:w1col + 1]),
                         reads=[ppB, pvB], writes=[cvB])

        def conv_row(rawv, rawB, cvv, cvB, L, n_rows, wcols, bcol):
            seg = L // n_rows
            r3 = rawv.rearrange("p (r s) -> p r s", r=n_rows)
            c3 = cvv.rearrange("p (r s) -> p r s", r=n_rows)
            P.op('dve', lambda e: e.scalar_tensor_tensor(c3[:, :, 1:seg], r3[:, :, 0:seg - 1], pv[:, wcols[0]:wcols[0] + 1], c3[:, :, 1:seg], ALU.mult, ALU.add),
                 reads=[rawB, pvB, cvB], writes=[cvB])
            P.op('dve', lambda e: e.scalar_tensor_tensor(c3[:, :, 0:seg - 1], r3[:, :, 1:seg], pv[:, wcols[2]:wcols[2] + 1], c3[:, :, 0:seg - 1], ALU.mult, ALU.add),
                 reads=[rawB, pvB, cvB], writes=[cvB])

        def ssd_seq(si, tok0, L, n_rows, wz3, wzB, yssd3, yssdB):
            nT = L // 128
            m1 = A.mark()
            BCT, (BCTB,) = A.alloc('BCT', 5 * L, BF16)
            BCT3 = BCT.rearrange("p (b l) -> p b l", b=5)
            xs_tok, xsB = A.alloc('xs_tok', nT * 512, F32, nT)
            xs3 = xs_tok.rearrange("p (c f) -> p c f", c=nT)
            Btok, BtB = A.alloc('Btok', nT * 128, BF16, nT)
            Bt3 = Btok.rearrange("p (c f) -> p c f", c=nT)
            a_all, aB = A.alloc('a_all', nT * 16, F32, nT)
            dt_all, dtB_ = A.alloc('dt_all', nT * 16, F32, nT)
            E_all, EB = A.alloc('E_all', nT * 32, F32, nT)
            et_all, etB = A.alloc('et_all', nT * 16, F32, nT)
            nC_all, nCB = A.alloc('nC_all', nT * 16, F32, nT)
            pf_all, pfB = A.alloc('pf_all', nT * 256, BF16, nT)
            pb_all, pbB = A.alloc('pb_all', nT * 256, BF16, nT)
            stf, (stfB,) = A.alloc('stf', 256, F32)
            stb, (stbB,) = A.alloc('stb', 256, F32)
            m2 = A.mark()
            wx, (wxB,) = A.alloc('wx', 8 * 768, BF16)
            wx3 = wx.rearrange("p (k n) -> p k n", k=8)
            load_w(wx3, 512, 768, wxB, newkey('wx'))
            raws = [A.alloc('raw%d' % i, L, F32) for i in range(3)]
            cvs = [A.alloc('cv%d' % i, L, F32) for i in range(3)]
            for j in range(6):
                rawv, (rawB,) = raws[j % 3]
                cvv, (cvB,) = cvs[j % 3]
                inproj_row(wx3, j * 128, wxB, tok0, L, rawv, rawB, cvv, cvB, PV_SCW + 6 + j, PV_SCB + j)
                conv_row(rawv, rawB, cvv, cvB, L, n_rows, [PV_SCW + k * 6 + j for k in range(3)], PV_SCB + j)
                if j < 4:
                    P.op('act', lambda e, cvv=cvv: e.activation(cvv, cvv, AF.Silu), reads=[cvB], writes=[cvB])
                    for c0 in range(0, nT, 4):
                        nn = min(4, nT - c0)
                        pp, ppB = ps()
                        for i in range(nn):
                            P.op('pe', lambda e, pp=pp, cvv=cvv, c0=c0, i=i: e.transpose(pp[:, i * 128:(i + 1) * 128], cvv[:, (c0 + i) * 128:(c0 + i + 1) * 128], ident),
                                 reads=[cvB, identB], writes=[ppB])
                        P.op('act', lambda e, pp=pp, c0=c0, nn=nn, j=j: e.copy(xs3[:, c0:c0 + nn, j * 128:(j + 1) * 128],
                                                                             pp[:, 0:nn * 128].rearrange("p (c f) -> p c f", c=nn)),
                             reads=[ppB], writes=[xsB[c0 + i] for i in range(nn)])
                else:
                    P.op('act', lambda e, cvv=cvv: e.activation(cvv, cvv, AF.Silu), reads=[cvB], writes=[cvB])
                    for g in range(2):
                        P.op('dve', lambda e, cvv=cvv, j=j, g=g: e.tensor_scalar(BCT3[:, (j - 4) * 2 + g, :], cvv, pv[:, PV_GM0 + g:PV_GM0 + g + 1], None, ALU.mult),
                             reads=[cvB, pvB], writes=[BCTB])
                    if j == 5:
                        P.op('act', lambda e, cvv=cvv: e.copy(BCT3[:, 4, :], cvv), reads=[cvB], writes=[BCTB])
                    if j == 4:
                        for c in range(nT):
                            pp, ppB = ps()
                            P.op('pe', lambda e, pp=pp, cvv=cvv, c=c: e.transpose(pp[:, 0:128], cvv[:, c * 128:(c + 1) * 128], ident),
                                 reads=[cvB, identB], writes=[ppB])
                            P.op('dve', lambda e, pp=pp, c=c: e.tensor_copy(Bt3[:, c, :], pp[:, 0:128]), reads=[ppB], writes=[BtB[c]])
            A.release(m2)
            import os
            sub = int(os.environ.get('MK_SUB', '9'))
            if sub <= 1:
                A.release(m1)
                return
            Sb_all, SbB = A.alloc('Sb_all', nT * 256, F32, nT)
            wdt, (wdtB,) = A.alloc('wdt', 8 * 16, BF16)
            wdt3 = wdt.rearrange("p (k n) -> p k n", k=8)
            load_w(wdt3, 1280, 16, wdtB, newkey('wdt'))
            smt = [A.alloc('smt%d' % i, 16, F32) for i in range(3)]
            xcd = [A.alloc('xcd%d' % i, 512, BF16) for i in range(6)]
            scf = [A.alloc('scf%d' % i, 16, F32) for i in range(3)]
            stmp = [A.alloc('stmp%d' % i, 256, F32) for i in range(2)]
            if si < 2:
                P.op('dve', lambda e: e.memset(stf, 0.0), writes=[stfB])
                P.op('dve', lambda e: e.memset(stb, 0.0), writes=[stbB])
            else:
                P.dma('sp', stf, stT_d[0], writes=[stfB], dkey=newkey('st'))
                P.dma('sp', stb, stT_d[1], writes=[stbB], dkey=newkey('st'))
            for c in range(nT):
                tk = slice(tok0 + c * 128, tok0 + (c + 1) * 128)
                pd, pdB = ps()
                for kc in range(8):
                    P.op('pe', lambda e, pd=pd, kc=kc, tk=tk: e.matmul(pd[:, 0:16], hT[:, kc, tk], wdt3[:, kc, :], start=(kc == 0), stop=(kc == 7)),
                         reads=[wdtB] + hb(tok0 + c * 128, 128), writes=[pdB])
                sv, (sB,) = smt[c % 3]
                dtc = dt_all[:, c * 16:(c + 1) * 16]
                ac = a_all[:, c * 16:(c + 1) * 16]
                Ec = E_all[:, c * 32:(c + 1) * 32]
                etc_ = et_all[:, c * 16:(c + 1) * 16]
                P.op('dve', lambda e, sv=sv, pd=pd: e.tensor_tensor(sv, pd[:, 0:16], dtb, ALU.add), reads=[pdB, dtbB], writes=[sB])
                P.op('act', lambda e, sv=sv: e.activation(sv, sv, AF.Exp), reads=[sB], writes=[sB])
                P.op('act', lambda e, sv=sv, dtc=dtc: e.activation(dtc, sv, AF.Ln, bias=one_c[:, 0:1]), reads=[sB, one_cB], writes=[dtB_[c]])
                P.op('dve', lambda e, dtc=dtc, ac=ac: e.tensor_tensor(ac, dtc, acoef, ALU.mult), reads=[dtB_[c], acoefB], writes=[aB[c]])
                pc, pcB = ps()
                for i, (mk, col) in enumerate(((LE, 0), (GT, 0), (GE, 8), (LT, 8))):
                    P.op('pe', lambda e, pc=pc, i=i, mk=mk, col=col, ac=ac: e.matmul(pc[:, i * 8:(i + 1) * 8], mk, ac[:, col:col + 8], start=True, stop=True),
                         reads=[masksB, aB[c]], writes=[pcB])
                P.op('pe', lambda e, pc=pc, ac=ac: e.matmul(pc[:, 32:48], ones32, ac, start=True, stop=True), reads=[ones32B, aB[c]], writes=[pcB])
                P.op('act', lambda e, pc=pc, Ec=Ec: e.activation(Ec, pc[:, 0:32], AF.Exp), reads=[pcB], writes=[EB[c]])
                P.op('act', lambda e, pc=pc, c=c: e.activation(nC_all[:, c * 16:c * 16 + 8], pc[:, 0:8], AF.Identity, scale=-1.0), reads=[pcB], writes=[nCB[c]])
                P.op('act', lambda e, pc=pc, c=c: e.activation(nC_all[:, c * 16 + 8:c * 16 + 16], pc[:, 16:24], AF.Identity, scale=-1.0), reads=[pcB, nCB[c]], writes=[nCB[c]])
                P.op('act', lambda e, pc=pc, etc_=etc_: e.activation(etc_, pc[:, 32:48], AF.Exp), reads=[pcB], writes=[etB[c]])
                fv, (fB,) = scf[c % 3]
                P.op('dve', lambda e, fv=fv, dtc=dtc, Ec=Ec: e.tensor_tensor(fv[:, 0:8], dtc[:, 0:8], Ec[:, 8:16], ALU.mult), reads=[dtB_[c], EB[c]], writes=[fB])
                P.op('dve', lambda e, fv=fv, dtc=dtc, Ec=Ec: e.tensor_tensor(fv[:, 8:16], dtc[:, 8:16], Ec[:, 24:32], ALU.mult), reads=[dtB_[c], EB[c], fB], writes=[fB])
                xsc = xs3[:, c, :].rearrange("p (h q) -> p h q", h=8)
                pss = []
                for d_ in range(2):
                    xv, (xdB,) = xcd[(c * 2 + d_) % 6]
                    P.op('dve' if d_ == 0 else 'pool', lambda e, xv=xv, fv=fv, d_=d_, xsc=xsc: e.tensor_tensor(
                        xv.rearrange("p (h q) -> p h q", h=8), xsc, fv[:, d_ * 8:(d_ + 1) * 8].unsqueeze(2).broadcast_to([128, 8, 64]), ALU.mult),
                        reads=[xsB[c], fB], writes=[xdB])
                    pS, pSB = ps()
                    P.op('pe', lambda e, pS=pS, xv=xv, c=c: e.matmul(pS, Bt3[:, c, :], xv, start=True, stop=True), reads=[BtB[c], xdB], writes=[pSB])
                    pss.append((pS, pSB))
                P.op('dve', lambda e, c=c: e.tensor_copy(pf_all[:, c * 256:(c + 1) * 256], stf), reads=[stfB], writes=[pfB[c]])
                tv, (tB,) = stmp[c % 2]
                for g in range(2):
                    rs_ = slice(g * 64, (g + 1) * 64)
                    P.op('dve', lambda e, tv=tv, rs_=rs_, g=g, etc_=etc_: e.tensor_tensor(
                        tv[rs_, :].rearrange("p (h q) -> p h q", h=4), stf[rs_, :].rearrange("p (h q) -> p h q", h=4),
                        etc_[rs_, g * 4:(g + 1) * 4].unsqueeze(2).broadcast_to([64, 4, 64]), ALU.mult),
                        reads=[stfB, etB[c]], writes=[tB])
                    P.op('dve', lambda e, tv=tv, rs_=rs_, g=g, pS=pss[0][0]: e.tensor_tensor(stf[rs_, :], tv[rs_, :], pS[rs_, g * 256:(g + 1) * 256], ALU.add),
                         reads=[tB, pss[0][1]], writes=[stfB])
                    P.op('act', lambda e, rs_=rs_, g=g, c=c, pS=pss[1][0]: e.copy(Sb_all[rs_, c * 256:(c + 1) * 256], pS[rs_, g * 256:(g + 1) * 256]),
                         reads=[pss[1][1]], writes=[SbB[c]])
            for c in range(nT - 1, -1, -1):
                etc_ = et_all[:, c * 16:(c + 1) * 16]
                P.op('dve', lambda e, c=c: e.tensor_copy(pb_all[:, c * 256:(c + 1) * 256], stb), reads=[stbB], writes=[pbB[c]])
                tv, (tB,) = stmp[c % 2]
                for g in range(2):
                    rs_ = slice(g * 64, (g + 1) * 64)
                    P.op('dve', lambda e, tv=tv, rs_=rs_, g=g, etc_=etc_: e.tensor_tensor(
                        tv[rs_, :].rearrange("p (h q) -> p h q", h=4), stb[rs_, :].rearrange("p (h q) -> p h q", h=4),
                        etc_[rs_, 8 + g * 4:8 + (g + 1) * 4].unsqueeze(2).broadcast_to([64, 4, 64]), ALU.mult),
                        reads=[stbB, etB[c]], writes=[tB])
                    P.op('dve', lambda e, tv=tv, rs_=rs_, c=c: e.tensor_tensor(stb[rs_, :], tv[rs_, :], Sb_all[rs_, c * 256:(c + 1) * 256], ALU.add),
                         reads=[tB, SbB[c]], writes=[stbB])
            if si < 2:
                P.dma('sp', nsT_d[si, 0], stf, reads=[stfB], dkey='nso')
                P.dma('sp', nsT_d[si, 1], stb, reads=[stbB], dkey='nso')
            A.release(m2)
            if sub <= 2:
                A.release(m1)
                return
            if wz3 is None:
                wz, (wzB,) = A.alloc('wz', 8 * 512, BF16)
                wz3 = wz.rearrange("p (k n) -> p k n", k=8)
                load_w(wz3, 0, 512, wzB, 'wz')
            zs = [A.alloc('zs%d' % i, 512, F32) for i in range(2)]
            Xb = [A.alloc('Xb%d' % i, 1024, F32) for i in range(2)]
            eD = [A.alloc('eD%d' % i, 512, F32) for i in range(2)]
            nGM = 4 if nT > 2 else 2
            GMb = [A.alloc('GM%d' % i, 512, BF16) for i in range(nGM)]
            nxc = 4 if nT > 2 else 2
            xcb = [A.alloc('xc%d' % i, 512, BF16) for i in range(nxc)]
            yt = [A.alloc('yt%d' % i, 512, F32) for i in range(4)]
            ssq = [A.alloc('ssq%d' % i, 2, F32) for i in range(3)]
            nl = 0
            ne = 0
            ng = 0
            small = len(ps_pool[0]) == 4
            if small:
                Gs, (GsB,) = A.alloc('Gs', 256, F32)

            def bk(k):
                if not small:
                    return ps()
                i = ps_pool[0][k]
                return PS[i][:], PB[i]
            for c in range(nT):
                tk = slice(tok0 + c * 128, tok0 + (c + 1) * 128)
                ac = a_all[:, c * 16:(c + 1) * 16]
                dtc = dt_all[:, c * 16:(c + 1) * 16]
                Ec = E_all[:, c * 32:(c + 1) * 32]
                pz, pzB = bk(0)
                for kc in range(8):
                    P.op('pe', lambda e, pz=pz, kc=kc, tk=tk: e.matmul(pz, hT[:, kc, tk], wz3[:, kc, :], start=(kc == 0), stop=(kc == 7)),
                         reads=[wzB] + hb(tok0 + c * 128, 128), writes=[pzB])
                zv, (zB,) = zs[c % 2]
                P.op('act', lambda e, zv=zv, pz=pz: e.activation(zv, pz, AF.Silu), reads=[pzB], writes=[zB])
                pG, pGB = bk(1)
                for g in range(2):
                    P.op('pe', lambda e, pG=pG, g=g, c=c: e.matmul(pG[:, g * 128:(g + 1) * 128], BCT3[:, g, c * 128:(c + 1) * 128],
                                                                  BCT3[:, 4, c * 128:(c + 1) * 128], start=True, stop=True),
                         reads=[BCTB], writes=[pGB])
                if small:
                    P.op('act', lambda e, pG=pG: e.copy(Gs, pG[:, 0:256]), reads=[pGB], writes=[GsB])
                    Gsrc, GsrcB = Gs, GsB
                else:
                    Gsrc, GsrcB = pG, pGB
                if sub <= 3:
                    continue
                xsc = xs3[:, c, :].rearrange("p (h q) -> p h q", h=8)
                pY, pYB = bk(0)
                xcs = []
                for d_ in range(2):
                    xv, (xcB,) = xcb[(c * 2 + d_) % nxc]
                    P.op('pool', lambda e, xv=xv, d_=d_, xsc=xsc, dtc=dtc: e.tensor_tensor(
                        xv.rearrange("p (h q) -> p h q", h=8), xsc, dtc[:, d_ * 8:(d_ + 1) * 8].unsqueeze(2).broadcast_to([128, 8, 64]), ALU.mult),
                        reads=[xsB[c], dtB_[c]], writes=[xcB])
                    xcs.append((xv, xcB))
                Xs = []
                for d_ in range(2):
                    Xv, (XB,) = Xb[d_]
                    mk = LE if d_ == 0 else GE
                    P.op('dve', lambda e, Xv=Xv, mk=mk, ac=ac, d_=d_: e.tensor_tensor(
                        Xv.rearrange("p (h l) -> p h l", h=8), mk.unsqueeze(1).broadcast_to([128, 8, 128]),
                        ac[:, d_ * 8:(d_ + 1) * 8].unsqueeze(2).broadcast_to([128, 8, 128]), ALU.mult),
                        reads=[masksB, aB[c]], writes=[XB], cost=2.2)
                    Xs.append((Xv, XB))
                for g in range(2):
                    gms = []
                    for d_ in range(2):
                        Xv, XB = Xs[d_]
                        ng_ = NEGF if d_ == 0 else NEGB
                        pD, pDB = bk((2, 3, 1, 2)[g * 2 + d_])
                        P.op('pe', lambda e, pD=pD, Xv=Xv, g=g: e.matmul(pD, ones32, Xv[:, g * 512:(g + 1) * 512], start=True, stop=False),
                             reads=[XB, ones32B], writes=[pDB], cost=0.9)
                        for hh in range(4):
                            P.op('pe', lambda e, pD=pD, hh=hh, ng_=ng_: e.matmul(pD[:, hh * 128:(hh + 1) * 128], ident, ng_, start=False, stop=(hh == 3)),
                                 reads=[identB, masksB], writes=[pDB], cost=0.3)
                        ev, (eB,) = eD[ne % 2]
                        ne += 1
                        for hh in range(4):
                            h = g * 4 + hh
                            P.op('act', lambda e, ev=ev, pD=pD, hh=hh, h=h, d_=d_, c=c: e.activation(
                                ev[:, hh * 128:(hh + 1) * 128], pD[:, hh * 128:(hh + 1) * 128], AF.Exp,
                                bias=nC_all[:, c * 16 + d_ * 8 + h:c * 16 + d_ * 8 + h + 1]), reads=[pDB, nCB[c]], writes=[eB], cost=0.3)
                        gm, (gmB,) = GMb[ng % nGM]
                        ng += 1
                        P.op('dve', lambda e, gm=gm, ev=ev, Gsrc=Gsrc, g=g: e.tensor_tensor(
                            gm.rearrange("p (h l) -> p h l", h=4), ev.rearrange("p (h l) -> p h l", h=4),
                            Gsrc[:, g * 128:(g + 1) * 128].unsqueeze(1).broadcast_to([128, 4, 128]), ALU.mult), reads=[eB, GsrcB], writes=[gmB])
                        gms.append((gm, gmB))
                    for hh in range(4):
                        h = g * 4 + hh
                        for d_ in range(2):
                            gm, gmB = gms[d_]
                            xv, xcB = xcs[d_]
                            P.op('pe', lambda e, pY=pY, gm=gm, hh=hh, h=h, xv=xv, d_=d_: e.matmul(
                                pY[:, h * 64:(h + 1) * 64], gm[:, hh * 128:(hh + 1) * 128], xv[:, h * 64:(h + 1) * 64], start=(d_ == 0), stop=(d_ == 1)),
                                reads=[gmB, xcB], writes=[pYB])
                if sub <= 4:
                    continue
                pZ = []
                for d_ in range(2):
                    pz_, pzB_ = bk((3, 1)[d_])
                    pall = pf_all if d_ == 0 else pb_all
                    pBl = pfB if d_ == 0 else pbB
                    for g in range(2):
                        P.op('pe', lambda e, pz_=pz_, g=g, c=c, pall=pall: e.matmul(
                            pz_[:, g * 256:(g + 1) * 256], BCT3[:, 2 + g, c * 128:(c + 1) * 128], pall[:, c * 256:(c + 1) * 256], start=True, stop=True),
                            reads=[BCTB, pBl[c]], writes=[pzB_])
                    pZ.append((pz_, pzB_))
                y0, (y0B,) = yt[(c * 2) % 4]
                y1, (y1B,) = yt[(c * 2 + 1) % 4]
                y03 = y0.rearrange("p (h q) -> p h q", h=8)
                y13 = y1.rearrange("p (h q) -> p h q", h=8)
                P.op('dve', lambda e, y03=y03, pz_=pZ[0][0], Ec=Ec: e.tensor_tensor(y03, pz_.rearrange("p (h q) -> p h q", h=8),
                                                                                   Ec[:, 0:8].unsqueeze(2).broadcast_to([128, 8, 64]), ALU.mult),
                     reads=[pZ[0][1], EB[c]], writes=[y0B])
                P.op('dve', lambda e, y13=y13, pz_=pZ[1][0], Ec=Ec: e.tensor_tensor(y13, pz_.rearrange("p (h q) -> p h q", h=8),
                                                                                   Ec[:, 16:24].unsqueeze(2).broadcast_to([128, 8, 64]), ALU.mult),
                     reads=[pZ[1][1], EB[c]], writes=[y1B])
                P.op('pool', lambda e, y0=y0, y1=y1: e.tensor_tensor(y0, y0, y1, ALU.add), reads=[y0B, y1B], writes=[y0B])
                P.op('pool', lambda e, y13=y13, xsc=xsc: e.tensor_tensor(y13, xsc, dsk.unsqueeze(2).broadcast_to([128, 8, 64]), ALU.mult),
                     reads=[xsB[c], dskB, y0B], writes=[y1B])
                P.op('dve', lambda e, y0=y0, pY=pY: e.tensor_tensor(y0, y0, pY, ALU.add), reads=[y0B, pYB], writes=[y0B])
                P.op('pool', lambda e, y0=y0, y1=y1: e.tensor_tensor(y0, y0, y1, ALU.add), reads=[y0B, y1B], writes=[y0B])
                P.op('dve', lambda e, y0=y0, zv=zv: e.tensor_tensor(y0, y0, zv, ALU.mult), reads=[y0B, zB], writes=[y0B])
                if sub <= 5:
                    continue
                qv, (qB,) = ssq[c % 3]
                P.op('dve', lambda e, y1=y1, y0=y0: e.tensor_tensor(y1, y0, y0, ALU.mult), reads=[y0B], writes=[y1B])
                P.op('dve', lambda e, y1=y1, qv=qv: e.reduce_sum(qv[:, 0:1], y1, mybir.AxisListType.X), reads=[y1B], writes=[qB])
                P.op('act', lambda e, qv=qv: e.activation(qv[:, 1:2], qv[:, 0:1], AF.Ln, bias=epsc[:, 0:1], scale=1.0 / 512), reads=[qB, epscB], writes=[qB])
                P.op('act', lambda e, qv=qv: e.activation(qv[:, 1:2], qv[:, 1:2], AF.Exp, scale=-0.5), reads=[qB], writes=[qB])
                P.op('dve', lambda e, y0=y0, qv=qv: e.tensor_scalar(y0, y0, qv[:, 1:2], None, ALU.mult), reads=[y0B, qB], writes=[y0B])
                pT, pTB = bk(2)
                for j in range(4):
                    P.op('pe', lambda e, pT=pT, j=j, y0=y0: e.transpose(pT[:, j * 128:(j + 1) * 128], y0[:, j * 128:(j + 1) * 128], ident),
                         reads=[y0B, identB], writes=[pTB])
                for j in range(4):
                    P.op('act', lambda e, pT=pT, j=j, c=c: e.activation(yssd3[:, j, c * 128:(c + 1) * 128], pT[:, j * 128:(j + 1) * 128], AF.Identity,
                                                                      scale=pv[:, PV_SNW + j:PV_SNW + j + 1]), reads=[pTB, pvB], writes=[yssdB])
            A.release(m1)

        def hyena_h2(L, h2v, h2B):
            m1 = A.mark()
            ft, (ftB,) = A.alloc('feats', L, F32)
            h1v, (h1B,) = A.alloc('h1', L, F32)
            tv, (tB,) = A.alloc('harg', L, F32)
            t2v, (t2B,) = A.alloc('harg2', L, F32)
            P.dma('sp', ft[0:33, :], feats_d[L], writes=[ftB], dkey=newkey('ft'))
            for (lw, K, src, srcB, dst, dstB, fbc) in ((hw1, 33, ft, ftB, h1v, h1B, 0), (hw2, 64, h1v, h1B, h2v, h2B, 1)):
                for t0 in range(0, L, 512):
                    n = min(512, L - t0)
                    pp, ppB = ps()
                    P.op('pe', lambda e, pp=pp, lw=lw, K=K, src=src, t0=t0, n=n: e.matmul(pp[0:64, 0:n], lw[0:K, 0:64], src[0:K, t0:t0 + n], start=True, stop=True),
                         reads=[hw1B, hw2B, srcB], writes=[ppB])
                    P.op('dve', lambda e, pp=pp, t0=t0, n=n, fbc=fbc: e.tensor_scalar(tv[0:64, t0:t0 + n], pp[0:64, 0:n], pv[0:64, PV_HFR:PV_HFR + 1], fb[0:64, fbc:fbc + 1], ALU.mult, ALU.add),
                         reads=[ppB, pvB, fbB], writes=[tB])
                for _ in range(2):
                    P.op('dve', lambda e: e.tensor_scalar(t2v[0:64, :], tv[0:64, :], PI, -2 * PI, ALU.is_gt, ALU.mult), reads=[tB], writes=[t2B])
                    P.op('dve', lambda e: e.tensor_tensor(tv[0:64, :], tv[0:64, :], t2v[0:64, :], ALU.add), reads=[tB, t2B], writes=[tB])
                    P.op('dve', lambda e: e.tensor_scalar(t2v[0:64, :], tv[0:64, :], -PI, 2 * PI, ALU.is_lt, ALU.mult), reads=[tB], writes=[t2B])
                    P.op('dve', lambda e: e.tensor_tensor(tv[0:64, :], tv[0:64, :], t2v[0:64, :], ALU.add), reads=[tB, t2B], writes=[tB])
                P.op('act', lambda e, dst=dst: e.activation(dst[0:64, :], tv[0:64, :], AF.Sin), reads=[tB], writes=[dstB])
            A.release(m1)

        def hyena_half(si, tok0, L, n_rows, q, h2v, h2B, yhy3, yhyB):
            nT = L // 128
            N2 = 2 * L
            m0_ = A.mark()
            u_tok = []
            for nm in ('v', 'x1', 'x2'):
                uv, uB = A.alloc('%s_tok' % nm, nT * 256, F32, nT)
                u_tok.append((uv.rearrange("p (c f) -> p c f", c=nT), uB))
            vb, vbB = A.alloc('vb', nT * 256, BF16, nT)
            vb3 = vb.rearrange("p (c f) -> p c f", c=nT)
            m1 = A.mark()
            wh = wh_ring
            raws = [A.alloc('hraw%d' % i, L, F32) for i in range(3)]
            cvs = [A.alloc('hcv%d' % i, L, F32) for i in range(3)]
            nr = 0
            for ui in range(3):
                wi = whn[0] % 2
                whn[0] += 1
                wv, (wB,) = wh[wi]
                wv3 = wv.rearrange("p (k n) -> p k n", k=8)
                col0 = 1296 + ui * 512 + q * 256
                load_w(wv3, col0, 256, wB, 'wh%d' % wi)
                u3, uB = u_tok[ui]
                for chrow in range(2):
                    rawv, (rawB,) = raws[nr % 3]
                    cvv, (cvB,) = cvs[nr % 3]
                    nr += 1
                    j = ui * 4 + q * 2 + chrow
                    inproj_row(wv3, chrow * 128, wB, tok0, L, rawv, rawB, cvv, cvB, PV_HCW + 12 + j, PV_HCB + j)
                    conv_row(rawv, rawB, cvv, cvB, L, n_rows, [PV_HCW + k * 12 + j for k in range(3)], PV_HCB + j)
                    for c0 in range(0, nT, 4):
                        nn = min(4, nT - c0)
                        pp, ppB = ps()
                        for i in range(nn):
                            P.op('pe', lambda e, pp=pp, cvv=cvv, c0=c0, i=i: e.transpose(pp[:, i * 128:(i + 1) * 128], cvv[:, (c0 + i) * 128:(c0 + i + 1) * 128], ident),
                                 reads=[cvB, identB], writes=[ppB], cost=0.3)
                        P.op('act', lambda e, pp=pp, c0=c0, nn=nn, chrow=chrow, u3=u3: e.copy(u3[:, c0:c0 + nn, chrow * 128:(chrow + 1) * 128],
                                                                                           pp[:, 0:nn * 128].rearrange("p (c f) -> p c f", c=nn)),
                             reads=[ppB], writes=[uB[c0 + i] for i in range(nn)], cost=0.7)
                        if ui == 0:
                            P.op('act', lambda e, pp=pp, c0=c0, nn=nn, chrow=chrow: e.copy(vb3[:, c0:c0 + nn, chrow * 128:(chrow + 1) * 128],
                                                                                        pp[:, 0:nn * 128].rearrange("p (c f) -> p c f", c=nn)),
                                 reads=[ppB], writes=[vbB[c0 + i] for i in range(nn)], cost=0.7)
            A.release(m1)
            import os
            hsub = int(os.environ.get('MK_HSUB', '9'))
            if hsub <= 1:
                A.release(m0_)
                return
            Ksp = []
            for o in range(2):
                kc_, kcB = A.alloc('Kc%d' % o, nT * 256, BF16, 1)
                ks_, ksB = A.alloc('Ks%d' % o, nT * 256, BF16, 1)
                Ksp.append((kc_.rearrange("p (c f) -> p c f", c=nT), kcB[0], ks_.rearrange("p (c f) -> p c f", c=nT), ksB[0]))
            m2 = A.mark()
            ksd = []
            for o in range(2):
                a_, aB_ = A.alloc('ksum%d' % o, nT * 256, BF16, 1)
                b_, bB_ = A.alloc('kdif%d' % o, nT * 256, BF16, 1)
                ksd.append((a_.rearrange("p (c f) -> p c f", c=nT), aB_[0], b_.rearrange("p (c f) -> p c f", c=nT), bB_[0]))
            wins = [A.alloc('win%d' % i, 1024, F32) for i in range(3)]
            ktmp = [A.alloc('ktmp%d' % i, 512, F32) for i in range(3)]
            for sc in range(nT):
                wv, (wB,) = wins[sc % 3]
                w4 = wv.rearrange("p (d o f) -> p d o f", d=2, o=2)
                P.dma('sp', wv.rearrange("p (a f) -> p a f", a=4),
                      win_d[L][sc * 128:(sc + 1) * 128, :].rearrange("p (a f) -> p a f", a=4)[:, :, q * 256:(q + 1) * 256],
                      writes=[wB], dkey='win%d' % (sc % 3))
                for o in range(2):
                    pk, pkB = ps()
                    for d_ in range(2):
                        col = d_ * 1024 + o * 512 + q * 256
                        P.op('pe', lambda e, pk=pk, d_=d_, col=col, sc=sc: e.matmul(pk[:, d_ * 256:(d_ + 1) * 256], h2v[0:64, sc * 128:(sc + 1) * 128],
                                                                                   hw3[0:64, col:col + 256], start=True, stop=True),
                             reads=[h2B, hw3B], writes=[pkB])
                    kt, (ktB,) = ktmp[(sc * 2 + o) % 3]
                    P.op('dve', lambda e, kt=kt, pk=pk, w4=w4, o=o: e.tensor_tensor(kt.rearrange("p (d f) -> p d f", d=2), pk.rearrange("p (d f) -> p d f", d=2),
                                                                                     w4[:, :, o, :], ALU.mult), reads=[pkB, wB], writes=[ktB])
                    P.op('pool', lambda e, kt=kt, o=o, sc=sc: e.tensor_tensor(ksd[o][0][:, sc, :], kt[:, 0:256], kt[:, 256:512], ALU.add), reads=[ktB], writes=[ksd[o][1]])
                    P.op('pool', lambda e, kt=kt, o=o, sc=sc: e.tensor_tensor(ksd[o][2][:, sc, :], kt[:, 256:512], kt[:, 0:256], ALU.subtract), reads=[ktB], writes=[ksd[o][3]])
            if hsub <= 2:
                A.release(m0_)
                return
            tbs = [[A.alloc('tb%d_%d' % (k, i), nT * 128, BF16) for i in range(2)] for k in range(2)]
            ntb = [0, 0, 0]

            def load_tab(k, j):
                nb_ = len(tbs[k])
                tv_, (tB_,) = tbs[k][ntb[k] % nb_]
                key = 'tb%d_%d' % (k, ntb[k] % nb_)
                ntb[k] += 1
                P.dma('sp', tv_, tab_d[L][k][j], writes=[tB_], dkey=key)
                return tv_.rearrange("p (c f) -> p c f", c=nT), tB_

            for fc in range(nT):
                tC, tCB = load_tab(0, fc)
                tS, tSB = load_tab(1, fc)
                for o in range(2):
                    K3c, KcB, K3s, KsB = Ksp[o]
                    pc_, pcB_ = ps()
                    for sc in range(nT):
                        P.op('pe', lambda e, pc_=pc_, tC=tC, sc=sc, o=o: e.matmul(pc_[:, 0:256], tC[:, sc, :], ksd[o][0][:, sc, :], start=(sc == 0), stop=(sc == nT - 1)),
                             reads=[tCB, ksd[o][1]], writes=[pcB_])
                    for sc in range(nT):
                        P.op('pe', lambda e, pc_=pc_, tS=tS, sc=sc, o=o: e.matmul(pc_[:, 256:512], tS[:, sc, :], ksd[o][2][:, sc, :], start=(sc == 0), stop=(sc == nT - 1)),
                             reads=[tSB, ksd[o][3]], writes=[pcB_])
                    P.op('act', lambda e, pc_=pc_, K3c=K3c, fc=fc: e.activation(K3c[:, fc, :], pc_[:, 0:256], AF.Identity, scale=2.0 / N2), reads=[pcB_], writes=[KcB])
                    P.op('act', lambda e, pc_=pc_, K3s=K3s, fc=fc: e.activation(K3s[:, fc, :], pc_[:, 256:512], AF.Identity, scale=2.0 / N2), reads=[pcB_], writes=[KsB])
                    if fc == 0:
                        pn, pnB = ps()
                        for sc in range(nT):
                            P.op('pe', lambda e, pn=pn, tS=tS, sc=sc, o=o: e.matmul(pn[0:1, 0:256], tS[:, sc, 0:1], ksd[o][0][:, sc, :], start=(sc == 0), stop=(sc == nT - 1)),
                                 reads=[tSB, ksd[o][1]], writes=[pnB])
                        P.op('act', lambda e, pc_=pc_, K3c=K3c: e.activation(K3c[0:1, 0, :], pc_[0:1, 0:256], AF.Identity, scale=1.0 / N2), reads=[pcB_, KcB], writes=[KcB])
                        P.op('act', lambda e, pn=pn, K3s=K3s: e.activation(K3s[0:1, 0, :], pn[0:1, 0:256], AF.Identity, scale=1.0 / N2), reads=[pnB, KsB], writes=[KsB])
            A.release(m2)
            if hsub <= 3:
                A.release(m0_)
                return
            Pc, (PcB,) = A.alloc('Pc', nT * 256, BF16)
            Pq, (PqB,) = A.alloc('Pq', nT * 256, BF16)
            Pc3 = Pc.rearrange("p (c f) -> p c f", c=nT)
            Pq3 = Pq.rearrange("p (c f) -> p c f", c=nT)
            z1, z1B = A.alloc('zz1', nT * 256, F32, nT)
            z13 = z1.rearrange("p (c f) -> p c f", c=nT)
            z1b, z1bB = A.alloc('zz1b', nT * 256, BF16, nT)
            z1b3 = z1b.rearrange("p (c f) -> p c f", c=nT)
            pt_ = [A.alloc('ptm%d' % i, 256, F32) for i in range(6)]
            tbs = [[A.alloc('tc%d_%d' % (k, i), nT * 128, BF16) for i in range(3 if k < 2 else 2)] for k in range(3)]
            ntb = [0, 0, 0]
            npt = 0
            for o in range(2):
                K3c, KcB, K3s, KsB = Ksp[o]
                zin3, zinB = (vb3, vbB) if o == 0 else (z1b3, z1bB)
                zf3, zfB = u_tok[0] if o == 0 else (z13, z1B)
                g3, gB_ = u_tok[1 + o]
                for fc in range(nT):
                    tC, tCB = load_tab(0, fc)
                    tS, tSB = load_tab(1, fc)
                    pz_, pzB_ = ps()
                    for sc in range(nT):
                        P.op('pe', lambda e, pz_=pz_, tC=tC, sc=sc, zin3=zin3: e.matmul(pz_[:, 0:256], tC[:, sc, :], zin3[:, sc, :], start=(sc == 0), stop=(sc == nT - 1)),
                             reads=[tCB, zinB[sc]], writes=[pzB_])
                    for sc in range(nT):
                        P.op('pe', lambda e, pz_=pz_, tS=tS, sc=sc, zin3=zin3: e.matmul(pz_[:, 256:512], tS[:, sc, :], zin3[:, sc, :], start=(sc == 0), stop=(sc == nT - 1)),
                             reads=[tSB, zinB[sc]], writes=[pzB_])
                    tm = [pt_[(npt + i) % 6] for i in range(4)]
                    npt += 4
                    Zc = pz_[:, 0:256]
                    Zs = pz_[:, 256:512]
                    P.op('dve', lambda e, t=tm[0][0], Zc=Zc, K3c=K3c, fc=fc: e.tensor_tensor(t, Zc, K3c[:, fc, :], ALU.mult), reads=[pzB_, KcB], writes=[tm[0][1][0]])
                    P.op('dve', lambda e, t=tm[1][0], Zs=Zs, K3s=K3s, fc=fc: e.tensor_tensor(t, Zs, K3s[:, fc, :], ALU.mult), reads=[pzB_, KsB], writes=[tm[1][1][0]])
                    P.op('pool', lambda e, t0=tm[0][0], t1=tm[1][0], fc=fc: e.tensor_tensor(Pc3[:, fc, :], t0, t1, ALU.add), reads=[tm[0][1][0], tm[1][1][0]], writes=[PcB])
                    P.op('dve', lambda e, t=tm[2][0], Zs=Zs, K3c=K3c, fc=fc: e.tensor_tensor(t, Zs, K3c[:, fc, :], ALU.mult), reads=[pzB_, KcB], writes=[tm[2][1][0]])
                    P.op('dve', lambda e, t=tm[3][0], Zc=Zc, K3s=K3s, fc=fc: e.tensor_tensor(t, Zc, K3s[:, fc, :], ALU.mult), reads=[pzB_, KsB], writes=[tm[3][1][0]])
                    P.op('pool', lambda e, t2=tm[2][0], t3=tm[3][0], fc=fc: e.tensor_tensor(Pq3[:, fc, :], t2, t3, ALU.subtract), reads=[tm[2][1][0], tm[3][1][0]], writes=[PqB])
                    if fc == 0:
                        P.op('dve', lambda e, Zc=Zc, K3c=K3c: e.tensor_tensor(Pc3[0:1, 0, :], Zc[0:1, :], K3c[0:1, 0, :], ALU.mult), reads=[pzB_, KcB, PcB], writes=[PcB])
                        P.op('dve', lambda e, Zs=Zs, K3s=K3s: e.tensor_tensor(Pq3[0:1, 0, :], Zs[0:1, :], K3s[0:1, 0, :], ALU.mult), reads=[pzB_, KsB, PqB], writes=[PqB])
                for tc in range(nT):
                    tC, tCB = load_tab(0, tc)
                    tT, tTB = load_tab(2, tc)
                    pv_, pvB_ = ps()
                    for fc in range(nT):
                        P.op('pe', lambda e, pv_=pv_, tC=tC, fc=fc: e.matmul(pv_[:, 0:256], tC[:, fc, :], Pc3[:, fc, :], start=(fc == 0), stop=False),
                             reads=[tCB, PcB], writes=[pvB_])
                    for fc in range(nT):
                        P.op('pe', lambda e, pv_=pv_, tT=tT, fc=fc: e.matmul(pv_[:, 0:256], tT[:, fc, :], Pq3[:, fc, :], start=False, stop=(fc == nT - 1)),
                             reads=[tTB, PqB], writes=[pvB_])
                    t0v, (t0B,) = pt_[npt % 6]
                    npt += 1
                    so = o * 512 + q * 256
                    P.op('pool', lambda e, t0v=t0v, zf3=zf3, tc=tc, so=so: e.tensor_tensor(t0v, zf3[:, tc, :], skip[:, so:so + 256], ALU.mult),
                         reads=[zfB[tc], skipB], writes=[t0B])
                    P.op('dve', lambda e, t0v=t0v, pv_=pv_: e.tensor_tensor(t0v, t0v, pv_[:, 0:256], ALU.add), reads=[t0B, pvB_], writes=[t0B])
                    if o == 0:
                        P.op('dve', lambda e, t0v=t0v, g3=g3, tc=tc: e.tensor_tensor(z13[:, tc, :], g3[:, tc, :], t0v, ALU.mult), reads=[t0B, gB_[tc]], writes=[z1B[tc]])
                        P.op('act', lambda e, tc=tc: e.copy(z1b3[:, tc, :], z13[:, tc, :]), reads=[z1B[tc]], writes=[z1bB[tc]])
                    else:
                        P.op('dve', lambda e, t0v=t0v, g3=g3, tc=tc: e.tensor_tensor(t0v, g3[:, tc, :], t0v, ALU.mult), reads=[t0B, gB_[tc]], writes=[t0B])
                        pT, pTB = ps()
                        for j in range(2):
                            P.op('pe', lambda e, pT=pT, j=j, t0v=t0v: e.transpose(pT[:, j * 128:(j + 1) * 128], t0v[:, j * 128:(j + 1) * 128], ident),
                                 reads=[t0B, identB], writes=[pTB])
                        P.op('act', lambda e, pT=pT, tc=tc: e.copy(yhy3[:, q * 2:q * 2 + 2, tc * 128:(tc + 1) * 128], pT[:, 0:256].rearrange("p (j t) -> p j t", j=2)),
                             reads=[pTB], writes=[yhyB])
            A.release(m0_)

        wh_ring = []
        whn = [0]

        def mixer(stage=9):
            norm_stage(1)
            m1 = A.mark()
            wz3 = wzB = None
            wh_ring.extend(A.alloc('wh%d' % i, 8 * 256, BF16) for i in range(2))
            h2s = {}
            for L in (256, 1024):
                h2v, (h2B,) = A.alloc('h2_%d' % L, L, F32)
                hyena_h2(L, h2v, h2B)
                h2s[L] = (h2v, h2B)
            base0 = A.mark()
            wzp, (wzpB,) = A.alloc('wzp', 8 * 512, BF16)
            wzp3 = wzp.rearrange("p (k n) -> p k n", k=8)
            load_w(wzp3, 0, 512, wzpB, 'wz')
            for si, (tok0, L, n_rows, c) in enumerate(SEQS):
                if si == 0:
                    A.hw = A.top
                elif si == 1:
                    A.release(A.hw)
                else:
                    A.release(base0)
                m2 = A.mark()
                ps_pool[0] = [0, 1, 2, 3] if si == 0 else ([4, 5, 6, 7] if si == 1 else list(range(8)))
                yssd, (yssdB,) = A.alloc('yssd', 4 * L, BF16)
                yssd3 = yssd.rearrange("p (j t) -> p j t", j=4)
                yhy, (yhyB,) = A.alloc('yhy', 4 * L, BF16)
                yhy3 = yhy.rearrange("p (j t) -> p j t", j=4)
                if stage >= 3:
                    ssd_seq(si, tok0, L, n_rows, wzp3 if si < 2 else None, wzpB if si < 2 else None, yssd3, yssdB)
                if stage >= 4:
                    for q in range(2):
                        hyena_half(si, tok0, L, n_rows, q, h2s[L][0], h2s[L][1], yhy3, yhyB)
                if stage < 5:
                    A.release(m2)
                    continue
                wo, (woB,) = A.alloc('wo', 8 * 1024, BF16)
                wo3 = wo.rearrange("p (k n) -> p k n", k=8)
                P.dma('pool', wo3, w_out_d.rearrange("(k p) n -> p k n", p=128), writes=[woB], dkey=newkey('wo'))
                for dc in range(8):
                    for t0 in range(0, L, 512):
                        n = min(512, L - t0)
                        po, poB = ps()
                        for mc in range(8):
                            src3, srcB = (yssd3, yssdB) if mc < 4 else (yhy3, yhyB)
                            P.op('pe', lambda e, po=po, mc=mc, dc=dc, t0=t0, n=n, src3=src3, wo3=wo3: e.matmul(
                                po[:, 0:n], wo3[:, mc, dc * 128:(dc + 1) * 128], src3[:, mc % 4, t0:t0 + n], start=(mc == 0), stop=(mc == 7)),
                                reads=[woB, srcB], writes=[poB])
                        xsl = xT[:, dc, tok0 + t0:tok0 + t0 + n]
                        P.op('dve', lambda e, po=po, xsl=xsl, n=n, dc=dc, c=c: e.scalar_tensor_tensor(xsl, po[:, 0:n], gscap(1, dc, c), xsl, ALU.mult, ALU.add),
                             reads=[poB, gscBs[1]] + xb(dc, tok0 + t0, n), writes=xb(dc, tok0 + t0, n))
                A.release(m2)
            A.release(m1)

        import os
        stage = int(os.environ.get('MK_STAGE', '9'))
        if stage >= 1:
            ffn(0, 0)
        if stage >= 2:
            mixer(stage)
        ps_pool[0] = list(range(8))
        if stage >= 9:
            ffn(1, 2)
        m1 = A.mark()
        yo, yoB = A.alloc('yo', 8 * NTOK, F32, 24)
        yo3 = yo.rearrange("p (k t) -> p k t", k=8)
        norm_stage(0, final=True, outv=yo3, outB=yoB)
        yd3 = yT_d.rearrange("p (k t) -> p k t", k=8)
        for kc in range(8):
            P.dma('sp', yd3[:, kc, :], yo3[:, kc, :], reads=[yoB[kc * 3 + t] for t in range(3)], dkey='out', group=True)
        A.release(m1)
        P.emit(['out', 'nso'])
        build_program.peak = A.peak
        build_program.makespan = getattr(P, "makespan", None)
    return nc


_CACHE = {}


def _get_program():
    if 'nc' not in _CACHE:
        _CACHE['nc'] = build_program()
    return _CACHE['nc']


def _consts():
    if 'c' in _CACHE:
        return _CACHE['c']
    c = {"ident": np.eye(128, dtype=np.float32), "masks": _masks()}
    for L in (256, 1024):
        nT = L // 128
        tC, tS, tST = _dft_tables(L)
        c["tabC_%d" % L] = tC.reshape(nT, 128, nT * 128)
        c["tabS_%d" % L] = tS.reshape(nT, 128, nT * 128)
        c["tabST_%d" % L] = tST.reshape(nT, 128, nT * 128)
        ft, wf, wb = _filter_consts(L)
        c["featsT_%d" % L] = ft
        c["win_%d" % L] = np.ascontiguousarray(np.concatenate([wf.reshape(L, 1024), wb.reshape(L, 1024)], axis=1))
    _CACHE['c'] = c
    return c


def _fm(v):
    v = np.asarray(v, np.float32).reshape(-1, 128)
    return np.ascontiguousarray(v.T)


def kernel(x_prompt, x_sample, state_ssd, c, c_ctx, w_ada, b_ada, norm_ffn1, ffn1_w_gate, ffn1_w_up,
           ffn1_w_down, norm_mix, w_in, w_out, ssd_conv_w, ssd_conv_b, ssd_dt_bias, ssd_a_log, ssd_d,
           ssd_norm_w, hy_conv_w, hy_conv_b, hy_w1, hy_b1, hy_freq, hy_w2, hy_b2, hy_w3, hy_skip,
           norm_ffn2, ffn2_w_gate, ffn2_w_up, ffn2_w_down, norm_final):
    f = lambda a: np.ascontiguousarray(np.asarray(a, dtype=np.float32))
    x_prompt, x_sample, state_ssd, c, c_ctx = f(x_prompt), f(x_sample), f(state_ssd), f(c), f(c_ctx)
    nc = _get_program()
    pvm = np.zeros((128, NPV), np.float32)
    pvm[:, PV_BADA:PV_BADA + 72] = _fm(f(b_ada)[0])
    pvm[:, PV_N1:PV_N1 + 8] = _fm(f(norm_ffn1)[0])
    pvm[:, PV_NM:PV_NM + 8] = _fm(f(norm_mix)[0])
    pvm[:, PV_N2:PV_N2 + 8] = _fm(f(norm_ffn2)[0])
    pvm[:, PV_NF:PV_NF + 8] = _fm(f(norm_final))
    for k in range(3):
        pvm[:, PV_SCW + k * 6:PV_SCW + (k + 1) * 6] = _fm(f(ssd_conv_w)[0, k])
        pvm[:, PV_HCW + k * 12:PV_HCW + (k + 1) * 12] = _fm(f(hy_conv_w)[0, k])
    pvm[:, PV_SCB:PV_SCB + 6] = _fm(f(ssd_conv_b)[0])
    pvm[:, PV_HCB:PV_HCB + 12] = _fm(f(hy_conv_b)[0])
    pvm[:, PV_SNW:PV_SNW + 4] = _fm(f(ssd_norm_w)[0])
    pvm[0:64, PV_HB1] = f(hy_b1)[0]
    pvm[0:64, PV_HFR] = f(hy_freq)[0]
    pvm[0:64, PV_HB2] = f(hy_b2)[0]
    pvm[0:64, PV_GM0] = 1.0
    pvm[64:128, PV_GM1] = 1.0
    shared = dict(_consts())
    shared.update({
        "w_ada": f(w_ada)[0], "ffn1_w_gate": f(ffn1_w_gate)[0], "ffn1_w_up": f(ffn1_w_up)[0], "ffn1_w_down": f(ffn1_w_down)[0],
        "ffn2_w_gate": f(ffn2_w_gate)[0], "ffn2_w_up": f(ffn2_w_up)[0], "ffn2_w_down": f(ffn2_w_down)[0],
        "w_in": f(w_in)[0], "w_out": f(w_out)[0], "pv": pvm,
        "dt_bias": f(ssd_dt_bias)[0].reshape(16), "a_log": f(ssd_a_log)[0].reshape(16), "ssd_d": f(ssd_d)[0].reshape(8),
        "hy_skip": f(hy_skip)[0].reshape(1024), "hy_w1": f(hy_w1)[0], "hy_w2": f(hy_w2)[0], "hy_w3": f(hy_w3)[0],
    })
    in_maps = []
    for ci in range(8):
        xt = np.concatenate([x_prompt[2 * ci], x_prompt[2 * ci + 1], x_sample[ci]], axis=0)
        xTm = np.ascontiguousarray(xt.reshape(NTOK, 8, 128).transpose(2, 1, 0)).reshape(128, 8 * NTOK)
        cond = np.stack([c_ctx, c[ci]], axis=0)
        condT = np.ascontiguousarray(cond.reshape(2, 8, 128).transpose(2, 1, 0)).reshape(128, 16)
        s = state_ssd[ci, 0]
        stT = np.ascontiguousarray(s.reshape(2, 2, 4, 64, 64).transpose(0, 1, 4, 2, 3)).reshape(2, 128, 256)
        m = dict(shared)
        m.update({"xT": xTm, "condT": condT, "stT": stT})
        in_maps.append(m)
    res = run_bass_kernel_spmd(nc, in_maps, core_ids=list(range(8)))
    y_prompt = np.empty((16, 256, D), np.float32)
    y_sample = np.empty((8, 1024, D), np.float32)
    new_state = np.empty((16, 1, 2, 8, 64, 64), np.float32)
    for ci in range(8):
        r = res.results[ci]
        yt = np.asarray(r["yT"]).reshape(128, 8, NTOK).transpose(2, 1, 0).reshape(NTOK, D)
        y_prompt[2 * ci] = yt[0:256]
        y_prompt[2 * ci + 1] = yt[256:512]
        y_sample[ci] = yt[512:]
        ns = np.asarray(r["nsT"]).reshape(2, 2, 2, 64, 4, 64)
        new_state[2 * ci:2 * ci + 2, 0] = ns.transpose(0, 1, 2, 4, 5, 3).reshape(2, 2, 8, 64, 64)
    return (y_prompt, y_sample, new_state)
```

```python
import math
import contextlib
import numpy as np
import ml_dtypes
import concourse.bass as bass
import concourse.mybir as mybir
from concourse.bass_utils import run_bass_kernel_spmd

F32 = mybir.dt.float32
BF16 = mybir.dt.bfloat16
AF = mybir.ActivationFunctionType
ALU = mybir.AluOpType

ENGS = ('pe', 'act', 'dve', 'pool', 'sp')
D = 1024
FF = 2752
NTOK = 1536
INC = 2832
RMS_EPS = 1e-6
NFC = 22
HALVES = (list(range(0, 12)), list(range(12, 22)))
SEQS = ((0, 256, 1, 0), (256, 256, 1, 0), (512, 1024, 16, 1))
PI = math.pi


class Buf:
    __slots__ = ('name', 'w', 'rs')

    def __init__(self, name):
        self.name = name
        self.w = None
        self.rs = []


import os as _os
XLAT = float(_os.environ.get('MK_LAT', '1.0'))
_CS = float(_os.environ.get('MK_CS', '1.0'))
DEF_COST = {'pe': 0.15, 'act': 0.45 * _CS, 'dve': 0.55 * _CS, 'pool': 1.0 * _CS, 'sp': 0.06}


class Op:
    __slots__ = ('eng', 'fn', 'deps', 'inc', 'cnt', 'dma', 'dkey', 'dcnt', 'cost', 'idx', 'sdeps')

    def __init__(self, eng, fn, dma=False, dkey=None, cost=None):
        self.eng = eng
        self.fn = fn
        self.deps = []
        self.sdeps = []
        self.cost = cost if cost is not None else (2.5 if dma else DEF_COST[eng])
        self.idx = 0
        self.inc = False
        self.cnt = 0
        self.dma = dma
        self.dkey = dkey
        self.dcnt = 0


class Prog:
    def __init__(self, nc):
        self.nc = nc
        self.q = {e: [] for e in ENGS}
        self.all = []
        self.dkeys = {}
        self.groups = set()
        self.lastdma = {}

    def _add(self, op, reads, writes):
        deps = []
        for b in reads:
            if b.w is not None:
                deps.append(b.w)
        for b in writes:
            if b.w is not None:
                deps.append(b.w)
            deps.extend(b.rs)
        seen = set(id(d) for d in op.deps)
        for d in deps:
            if d is op or id(d) in seen:
                continue
            seen.add(id(d))
            if d.eng == op.eng and not d.dma and not op.dma and op.eng == 'pe':
                op.sdeps.append(d)
                continue
            op.deps.append(d)
        for b in reads:
            b.rs.append(op)
        for b in writes:
            b.w = op
            b.rs = []
        op.idx = len(self.all)
        self.all.append(op)
        return op

    def op(self, eng, fn, reads=(), writes=(), cost=None):
        return self._add(Op(eng, fn, cost=cost), list(reads), list(writes))

    def schedule(self):
        import heapq
        ops = self.all
        n = len(ops)
        succ = [[] for _ in range(n)]
        indeg = [0] * n
        for o in ops:
            ds = set(id(d) for d in o.deps) | set(id(d) for d in o.sdeps)
            o_all = {d.idx for d in o.deps} | {d.idx for d in o.sdeps}
            for di in o_all:
                succ[di].append(o.idx)
            indeg[o.idx] = len(o_all)
        ready = [0.0] * n
        fin = [0.0] * n
        efree = {e: 0.0 for e in ENGS}
        bl = [0.0] * n
        for o in reversed(ops):
            m = 0.0
            for j in succ[o.idx]:
                v = bl[j] + (0.05 if (ops[j].eng == o.eng and not o.dma) else XLAT)
                if v > m:
                    m = v
            bl[o.idx] = o.cost + m
        rs = {e: [] for e in ENGS}
        for o in ops:
            if indeg[o.idx] == 0:
                rs[o.eng].append(o.idx)
        self.q = {e: [] for e in ENGS}
        done = 0
        while done < n:
            best = None
            for e in ENGS:
                if not rs[e]:
                    continue
                t = max(efree[e], min(ready[i] for i in rs[e]))
                if best is None or t < best[0]:
                    best = (t, e)
            t, e = best
            i = max((i for i in rs[e] if ready[i] <= t + 1e-9), key=lambda i: (bl[i], -i))
            rs[e].remove(i)
            o = ops[i]
            if o.dma:
                efree[e] = t + DEF_COST['sp']
            else:
                efree[e] = t + o.cost
            fin[i] = t + o.cost
            self.q[e].append(o)
            done += 1
            for j in succ[i]:
                v = fin[i] + (0.05 if (ops[j].eng == o.eng and not o.dma) else XLAT)
                if v > ready[j]:
                    ready[j] = v
                indeg[j] -= 1
                if indeg[j] == 0:
                    rs[ops[j].eng].append(j)
        assert done == n, (done, n)
        self.makespan = max(fin) if fin else 0.0

    def dma(self, eng, out, in_, reads=(), writes=(), dkey=None, group=False, cost=None):
        o = Op(eng, lambda e: e.dma_start(out=out, in_=in_), dma=True, dkey=dkey, cost=cost)
        self.dkeys[dkey] = self.dkeys.get(dkey, 0) + 1
        o.dcnt = self.dkeys[dkey] * 16
        if group:
            self.groups.add(dkey)
        else:
            prev = self.lastdma.get(dkey)
            if prev is not None:
                o.deps.append(prev)
            self.lastdma[dkey] = o
        return self._add(o, list(reads), list(writes))

    def emit(self, final_dkeys, sched=True):
        nc = self.nc
        if sched:
            self.schedule()
        else:
            self.q = {e: [] for e in ENGS}
            for o in self.all:
                self.q[o.eng].append(o)
        for e in ENGS:
            for o in self.q[e]:
                for d in o.deps:
                    if not d.dma:
                        d.inc = True
        for e in ENGS:
            c = 0
            for o in self.q[e]:
                if o.inc and not o.dma:
                    c += 1
                o.cnt = c
        with contextlib.ExitStack() as st:
            esem = {e: st.enter_context(nc.semaphore('s_' + e)) for e in ENGS if e != 'sp'}
            dsem = {k: st.enter_context(nc.semaphore('d_%d' % i)) for i, k in enumerate(self.dkeys)}
            block = st.enter_context(nc.Block())

            def run(engname, eng):
                known = {}
                for o in self.q[engname]:
                    need = {}
                    for d in o.deps:
                        if d.dma:
                            key, val = ('d', d.dkey), (self.dkeys[d.dkey] * 16 if d.dkey in self.groups else d.dcnt)
                        else:
                            key, val = ('e', d.eng), d.cnt
                        if val > need.get(key, 0):
                            need[key] = val
                    for key, val in need.items():
                        if known.get(key, 0) >= val:
                            continue
                        known[key] = val
                        eng.wait_ge(dsem[key[1]] if key[0] == 'd' else esem[key[1]], val)
                    ins = o.fn(eng)
                    if o.dma:
                        ins.then_inc(dsem[o.dkey], 16)
                    elif o.inc:
                        ins.then_inc(esem[engname], 1)
                if engname == 'sp':
                    for k in final_dkeys:
                        if k not in self.dkeys:
                            continue
                        eng.wait_ge(dsem[k], self.dkeys[k] * 16)

            @block.sync
            def _(e):
                run('sp', e)

            @block.tensor
            def _(e):
                run('pe', e)

            @block.scalar
            def _(e):
                run('act', e)

            @block.vector
            def _(e):
                run('dve', e)

            @block.gpsimd
            def _(e):
                run('pool', e)


class Arena:
    def __init__(self, nc, st, nbytes):
        self.t = st.enter_context(nc.sbuf_tensor("arena", [128, nbytes // 2], BF16))
        self.t32 = self.t.bitcast(F32)
        self.cap = nbytes
        self.top = 0
        self.hist = []
        self.peak = 0
        self.hw = 0

    def alloc(self, name, cols, dt, nbufs=1):
        esz = 4 if dt is F32 else 2
        start = (self.top + 31) // 32 * 32
        end = start + cols * esz
        assert end <= self.cap, (name, end, self.cap)
        self.top = end
        self.peak = max(self.peak, end)
        self.hw = max(self.hw, end)
        bufs = [Buf('%s%d' % (name, i)) for i in range(nbufs)]
        inh = []
        keep = []
        for (s, e, obs) in self.hist:
            if s < end and start < e:
                for ob in obs:
                    if ob.w is not None:
                        inh.append(ob.w)
                    inh.extend(ob.rs)
                if s >= start and e <= end:
                    continue
            keep.append((s, e, obs))
        self.hist = keep
        for b in bufs:
            b.rs = list(inh)
        self.hist.append((start, end, bufs))
        if dt is F32:
            v = self.t32[:, start // 4:start // 4 + cols]
        else:
            v = self.t[:, start // 2:start // 2 + cols]
        return v, bufs

    def mark(self):
        return self.top

    def release(self, m):
        self.top = m


def _dft_tables(L):
    N = 2 * L
    nT = L // 128
    s = np.arange(L, dtype=np.float64)[:, None]
    f = np.arange(L, dtype=np.float64)[None, :]
    C = np.cos(2 * np.pi * s * f / N)
    S = np.sin(2 * np.pi * s * f / N)
    S[:, 0] = (-1.0) ** np.arange(L)

    def blk(M):
        return np.ascontiguousarray(M.reshape(nT, 128, nT, 128).transpose(2, 1, 0, 3)).astype(ml_dtypes.bfloat16)
    return blk(C), blk(S), blk(S.T.copy())


def _filter_consts(L):
    f32 = np.float32
    t = np.linspace(0.0, 1.0, L, dtype=f32)[:, None]
    w = (f32(2.0 * math.pi / L)) * np.arange(L, dtype=f32)[:, None]
    f = np.linspace(1e-4, 16 - 1, 16, dtype=f32)[None, :]
    feats = np.concatenate([t, np.cos(f * w), -np.sin(f * w)], axis=-1).astype(f32)
    mn = math.log(1e-2) / 1.5
    mx = math.log(1e-2) / 0.3
    deltas = np.abs(np.linspace(mn, mx, 1024, dtype=f32))
    window = np.exp(-t[:, :, None] * deltas.reshape(2, 512)).astype(f32)
    wb = window.copy()
    wb[0] = 0.0
    return np.ascontiguousarray(feats.T), window, wb


def _masks():
    t = np.arange(128)
    le = (t[:, None] <= t[None, :])
    gt = (t[:, None] > t[None, :])
    ge = (t[:, None] >= t[None, :])
    lt = (t[:, None] < t[None, :])
    negf = -30000.0 * (t[None, :] < t[:, None])
    negb = -30000.0 * (t[None, :] > t[:, None])
    return np.stack([le, gt, ge, lt, negf, negb]).astype(np.float32).transpose(1, 0, 2).reshape(128, 768).copy()


PV_BADA = 0
PV_N1 = 72
PV_NM = 80
PV_N2 = 88
PV_NF = 96
PV_SCW = 104
PV_SCB = 122
PV_HCW = 128
PV_HCB = 164
PV_SNW = 176
PV_HB1 = 180
PV_HFR = 181
PV_HB2 = 182
PV_GM0 = 184
PV_GM1 = 185
NPV = 186


def build_program():
    nc = bass.Bass("TRN2", target_bir_lowering=False)

    def din(name, shape, dt=F32):
        return nc.dram_tensor(name, list(shape), dt, kind="ExternalInput").ap()

    def dout(name, shape):
        return nc.dram_tensor(name, list(shape), F32, kind="ExternalOutput").ap()

    xT_d = din("xT", [128, 8 * NTOK])
    condT_d = din("condT", [128, 16])
    stT_d = din("stT", [2, 128, 256])
    w_ada_d = din("w_ada", [D, 9 * D])
    wg_d = [din("ffn1_w_gate", [D, FF]), din("ffn2_w_gate", [D, FF])]
    wu_d = [din("ffn1_w_up", [D, FF]), din("ffn2_w_up", [D, FF])]
    wd_d = [din("ffn1_w_down", [FF, D]), din("ffn2_w_down", [FF, D])]
    w_in_d = din("w_in", [D, INC])
    w_out_d = din("w_out", [D, D])
    pv_d = din("pv", [128, NPV])
    dtb_d = din("dt_bias", [16])
    alog_d = din("a_log", [16])
    dsk_d = din("ssd_d", [8])
    skip_d = din("hy_skip", [1024])
    hw1_d = din("hy_w1", [33, 64])
    hw2_d = din("hy_w2", [64, 64])
    hw3_d = din("hy_w3", [64, 2048])
    ident_d = din("ident", [128, 128])
    masks_d = din("masks", [128, 768])
    tab_d = {}
    feats_d = {}
    win_d = {}
    for L in (256, 1024):
        nT = L // 128
        tab_d[L] = [din("tab%s_%d" % (n, L), [nT, 128, nT * 128], BF16) for n in ("C", "S", "ST")]
        feats_d[L] = din("featsT_%d" % L, [33, L])
        win_d[L] = din("win_%d" % L, [L, 2048])
    yT_d = dout("yT", [128, 8 * NTOK])
    nsT_d = dout("nsT", [2, 2, 128, 256])

    st = contextlib.ExitStack()
    with st:
        PS = [st.enter_context(nc.psum_tensor("ps%d" % i, [128, 512], F32)) for i in range(8)]
        PB = [Buf('ps%d' % i) for i in range(8)]
        A = Arena(nc, st, int(_os.environ.get("MK_CAP", "210944")))
        P = Prog(nc)
        psn = [0]

        ps_pool = [list(range(8))]

        def ps():
            pool = ps_pool[0]
            i = pool[psn[0] % len(pool)]
            psn[0] += 1
            return PS[i][:], PB[i]

        dk = [0]

        def newkey(p='k'):
            dk[0] += 1
            return '%s%d' % (p, dk[0])

        xT2, xB = A.alloc('xT', 8 * NTOK, F32, 24)
        xT = xT2.rearrange("p (k t) -> p k t", k=8)
        hT2, hB = A.alloc('hT', 8 * NTOK, BF16, 6)
        hT = hT2.rearrange("p (k t) -> p k t", k=8)
        ident, (identB,) = A.alloc('ident', 128, F32)
        masks, (masksB,) = A.alloc('masks', 768, F32)
        LE, GT, GE, LT, NEGF, NEGB = (masks[:, i * 128:(i + 1) * 128] for i in range(6))
        onesM, (onesMB,) = A.alloc('onesM', 128, BF16)
        ones32, (ones32B,) = A.alloc('ones32', 128, F32)
        pv, (pvB,) = A.alloc('pv', NPV, F32)
        mod, (modB,) = A.alloc('mod', 144, F32)
        Asc, (AscB,) = A.alloc('Asc', 48, F32)
        gsc, (gscB,) = A.alloc('gsc', 48, F32)
        dtb, (dtbB,) = A.alloc('dtb', 16, F32)
        acoef, (acoefB,) = A.alloc('acoef', 16, F32)
        dsk, (dskB,) = A.alloc('dsk', 8, F32)
        skip, (skipB,) = A.alloc('skip', 1024, F32)
        condT, (condTB,) = A.alloc('condT', 16, BF16)
        condf, (condfB,) = A.alloc('condf', 16, F32)
        hw1, (hw1B,) = A.alloc('hw1', 64, F32)
        hw2, (hw2B,) = A.alloc('hw2', 64, F32)
        hw3, (hw3B,) = A.alloc('hw3', 2048, F32)
        fb, (fbB,) = A.alloc('fb', 4, F32)
        negpi, (negpiB,) = A.alloc('negpi', 1, F32)
        epsc, (epscB,) = A.alloc('epsc', 1, F32)
        one_c, (one_cB,) = A.alloc('one_c', 1, F32)

        def xb(dc, tok0, n):
            t0 = tok0 // 512
            t1 = (tok0 + n - 1) // 512
            return [xB[dc * 3 + t] for t in range(t0, t1 + 1)]

        def hb(tok0, n):
            return [hB[i] for i in range(tok0 // 256, (tok0 + n - 1) // 256 + 1)]

        xd3 = xT_d.rearrange("p (k t) -> p k t", k=8)
        for kc in range(8):
            P.dma('sp', xT[:, kc, :], xd3[:, kc, :], writes=[xB[kc * 3 + t] for t in range(3)], dkey='xin', group=True)
        P.dma('sp', ident, ident_d, writes=[identB], dkey='c0', group=True)
        P.dma('sp', masks, masks_d, writes=[masksB], dkey='c0', group=True)
        P.dma('sp', pv, pv_d, writes=[pvB], dkey='c0', group=True)
        P.dma('sp', condf, condT_d, writes=[condfB], dkey='c0', group=True)
        P.dma('sp', dtb, dtb_d.partition_broadcast(128), writes=[dtbB], dkey='c0', group=True)
        P.dma('sp', acoef, alog_d.partition_broadcast(128), writes=[acoefB], dkey='c0', group=True)
        P.dma('sp', dsk, dsk_d.partition_broadcast(128), writes=[dskB], dkey='c0', group=True)
        P.dma('sp', skip, skip_d.partition_broadcast(128), writes=[skipB], dkey='c0', group=True)
        P.dma('sp', hw1[0:33, :], hw1_d, writes=[hw1B], dkey='c0', group=True)
        P.dma('sp', hw2[0:64, :], hw2_d, writes=[hw2B], dkey='c0', group=True)
        P.dma('sp', hw3[0:64, :], hw3_d, writes=[hw3B], dkey='c0', group=True)
        P.op('dve', lambda e: e.memset(onesM, 1.0 / D), writes=[onesMB])
        P.op('dve', lambda e: e.memset(ones32, 1.0), writes=[ones32B])
        P.op('dve', lambda e: e.memset(negpi, -PI), writes=[negpiB])
        P.op('dve', lambda e: e.memset(epsc, RMS_EPS), writes=[epscB])
        P.op('dve', lambda e: e.memset(one_c, 1.0), writes=[one_cB])
        P.op('act', lambda e: e.activation(acoef, acoef, AF.Exp), reads=[acoefB], writes=[acoefB])
        P.op('dve', lambda e: e.tensor_scalar(acoef, acoef, -1.0, None, ALU.mult), reads=[acoefB], writes=[acoefB])
        P.op('dve', lambda e: e.tensor_tensor(fb[0:64, 0:1], pv[0:64, PV_HFR:PV_HFR + 1], pv[0:64, PV_HB1:PV_HB1 + 1], ALU.mult),
             reads=[pvB], writes=[fbB])
        P.op('dve', lambda e: e.tensor_tensor(fb[0:64, 1:2], pv[0:64, PV_HFR:PV_HFR + 1], pv[0:64, PV_HB2:PV_HB2 + 1], ALU.mult),
             reads=[pvB], writes=[fbB])
        P.op('act', lambda e: e.activation(condT, condf, AF.Silu), reads=[condfB], writes=[condTB])

        m0 = A.mark()
        wab = [A.alloc('wab%d' % i, 8 * 512, BF16) for i in range(3)]
        modBs = [Buf('mod%d' % m) for m in range(9)]
        for m in range(9):
            pm, pmB = ps()
            for hb_ in range(2):
                blk = m * 2 + hb_
                wv, (wB,) = wab[blk % 3]
                wv3 = wv.rearrange("p (k n) -> p k n", k=8)
                P.dma('pool', wv3, w_ada_d[:, blk * 512:(blk + 1) * 512].rearrange("(k p) n -> p k n", p=128),
                      writes=[wB], dkey='wab%d' % (blk % 3), cost=5.0)
                for j in range(4):
                    cj = hb_ * 4 + j
                    for kc in range(8):
                        P.op('pe', lambda e, pm=pm, cj=cj, j=j, kc=kc, wv3=wv3: e.matmul(
                            pm[:, cj * 2:cj * 2 + 2], wv3[:, kc, j * 128:(j + 1) * 128], condT[:, kc * 2:kc * 2 + 2],
                            start=(kc == 0), stop=(kc == 7)), reads=[wB, condTB], writes=[pmB])
            P.op('dve', lambda e, pm=pm, m=m: e.tensor_tensor(mod[:, m * 16:(m + 1) * 16].rearrange("p (c o) -> p c o", o=2),
                                                            pm[:, 0:16].rearrange("p (c o) -> p c o", o=2),
                                                            pv[:, PV_BADA + m * 8:PV_BADA + (m + 1) * 8].unsqueeze(2).broadcast_to([128, 8, 2]), ALU.add),
                 reads=[pmB, pvB], writes=[modBs[m]])
        A.release(m0)

        def modap(m, dc, c):
            col = (m * 8 + dc) * 2 + c
            return mod[:, col:col + 1]

        AscBs = [Buf('Asc%d' % i) for i in range(3)]
        gscBs = [Buf('gsc%d' % i) for i in range(3)]
        for n, (pvo, msc, mg, gmul) in enumerate(((PV_N1, 1, 2, 0.5), (PV_NM, 4, 5, 1.0), (PV_N2, 7, 8, 0.5))):
            a3 = Asc[:, n * 16:(n + 1) * 16].rearrange("p (d c) -> p d c", c=2)
            m3 = mod[:, msc * 16:(msc + 1) * 16].rearrange("p (d c) -> p d c", c=2)
            P.op('dve', lambda e, a3=a3, m3=m3: e.tensor_scalar(a3, m3, 1.0, None, ALU.add), reads=[modBs[msc]], writes=[AscBs[n]])
            P.op('dve', lambda e, a3=a3, pvo=pvo: e.tensor_tensor(a3, a3, pv[:, pvo:pvo + 8].unsqueeze(2).broadcast_to([128, 8, 2]), ALU.mult),
                 reads=[AscBs[n], pvB], writes=[AscBs[n]])
            P.op('dve', lambda e, n=n, mg=mg, gmul=gmul: e.tensor_scalar(gsc[:, n * 16:(n + 1) * 16], mod[:, mg * 16:(mg + 1) * 16], gmul, None, ALU.mult),
                 reads=[modBs[mg]], writes=[gscBs[n]])

        def asc(n, dc, c):
            col = n * 16 + dc * 2 + c
            return Asc[:, col:col + 1]

        def gscap(n, dc, c):
            col = n * 16 + dc * 2 + c
            return gsc[:, col:col + 1]

        def norm_stage(n, final=False, outv=None, outB=None):
            m1 = A.mark()
            saved_pool = ps_pool[0]
            ps_pool[0] = [6, 7]
            sq = [A.alloc('sq%d' % i, 512, BF16) for i in range(3)]
            tmp = [A.alloc('nt%d' % i, 512, F32) for i in range(3)]
            rstd = [A.alloc('rstd%d' % i, 512, F32) for i in range(2)]
            cnt = 0
            for t in range(3):
                c = 0 if t == 0 else 1
                tk = slice(t * 512, (t + 1) * 512)
                pt, ptB = ps()
                for kc in range(8):
                    sv, (sB,) = sq[cnt % 3]
                    cnt += 1
                    P.op('act', lambda e, sv=sv, kc=kc, tk=tk: e.activation(sv, xT[:, kc, tk], AF.Square),
                         reads=[xB[kc * 3 + t]], writes=[sB])
                    P.op('pe', lambda e, sv=sv, kc=kc, pt=pt: e.matmul(pt, onesM, sv, start=(kc == 0), stop=(kc == 7)),
                         reads=[sB, onesMB], writes=[ptB], cost=0.215)
                rv, (rB,) = rstd[t % 2]
                P.op('act', lambda e, rv=rv, pt=pt: e.activation(rv, pt, AF.Sqrt, bias=epsc[:, 0:1]), reads=[ptB, epscB], writes=[rB])
                P.op('dve', lambda e, rv=rv: e.reciprocal(rv, rv), reads=[rB], writes=[rB])
                for kc in range(8):
                    tv, (tB,) = tmp[cnt % 3]
                    cnt += 1
                    P.op('dve', lambda e, tv=tv, kc=kc, tk=tk, rv=rv: e.tensor_tensor(tv, xT[:, kc, tk], rv, ALU.mult),
                         reads=[xB[kc * 3 + t], rB], writes=[tB])
                    if final:
                        P.op('act', lambda e, tv=tv, kc=kc, tk=tk: e.activation(outv[:, kc, tk], tv, AF.Identity, scale=pv[:, PV_NF + kc:PV_NF + kc + 1]),
                             reads=[tB, pvB], writes=[outB[kc * 3 + t]])
                    else:
                        P.op('act', lambda e, tv=tv, kc=kc, tk=tk, c=c: e.activation(hT[:, kc, tk], tv, AF.Identity,
                                                                                      bias=modap(3 * n, kc, c), scale=asc(n, kc, c)),
                             reads=[tB, modBs[3 * n], AscBs[n]], writes=hb(t * 512, 512))
            ps_pool[0] = saved_pool
            A.release(m1)

        def ffn(li, n):
            norm_stage(n)
            m1 = A.mark()
            ps_pool[0] = [0, 1, 2, 3, 4, 5]
            actv, actB = A.alloc('act', 12 * NTOK, BF16, 36)
            act3 = actv.rearrange("p (f t) -> p f t", f=12)
            wgb = [A.alloc('wg%d' % i, 8 * 256, BF16) for i in range(3)]
            wub = [A.alloc('wu%d' % i, 8 * 256, BF16) for i in range(3)]
            wdb = [A.alloc('wd%d' % i, 12 * 256, BF16) for i in range(4)]
            sgb = [A.alloc('sg%d' % i, 512, F32) for i in range(3)]
            kg = 'wg%d_' % li
            nblk = 0
            nsg = 0
            nwd = 0
            for hf, chunks in enumerate(HALVES):
                for i, fc in enumerate(chunks):
                    w = 128 if fc < 21 else 64
                    if fc % 2 == 0:
                        bw = 256 if fc < 20 else 192
                        gv, (gB,) = wgb[nblk % 3]
                        uv, (uB,) = wub[nblk % 3]
                        g3 = gv.rearrange("p (k n) -> p k n", k=8)
                        u3 = uv.rearrange("p (k n) -> p k n", k=8)
                        P.dma('pool', g3[:, :, 0:bw], wg_d[li][:, fc * 128:fc * 128 + bw].rearrange("(k p) n -> p k n", p=128),
                              writes=[gB], dkey=kg + 'g%d' % (nblk % 3), cost=4.0)
                        P.dma('pool', u3[:, :, 0:bw], wu_d[li][:, fc * 128:fc * 128 + bw].rearrange("(k p) n -> p k n", p=128),
                              writes=[uB], dkey=kg + 'u%d' % (nblk % 3), cost=4.0)
                        nblk += 1
                    off = (fc % 2) * 128
                    for t in range(3):
                        tk = slice(t * 512, (t + 1) * 512)
                        pg, pgB = ps()
                        pu, puB = ps()
                        for kc in range(8):
                            P.op('pe', lambda e, pg=pg, g3=g3, kc=kc, off=off, w=w, tk=tk: e.matmul(
                                pg[0:w, :], g3[:, kc, off:off + w], hT[:, kc, tk], start=(kc == 0), stop=(kc == 7)),
                                reads=[gB] + hb(t * 512, 512), writes=[pgB], cost=0.215)
                        for kc in range(8):
                            P.op('pe', lambda e, pu=pu, u3=u3, kc=kc, off=off, w=w, tk=tk: e.matmul(
                                pu[0:w, :], u3[:, kc, off:off + w], hT[:, kc, tk], start=(kc == 0), stop=(kc == 7)),
                                reads=[uB] + hb(t * 512, 512), writes=[puB], cost=0.215)
                        sv, (sB,) = sgb[nsg % 3]
                        nsg += 1
                        P.op('act', lambda e, sv=sv, pg=pg, w=w: e.activation(sv[0:w, :], pg[0:w, :], AF.Silu), reads=[pgB], writes=[sB])
                        P.op('dve', lambda e, sv=sv, pu=pu, w=w, i=i, tk=tk: e.tensor_tensor(act3[0:w, i, tk], sv[0:w, :], pu[0:w, :], ALU.mult),
                             reads=[sB, puB], writes=[actB[i * 3 + t]])
                nf = len(chunks)
                f0 = chunks[0]
                nfull = nf if chunks[-1] < 21 else nf - 1
                dblk = []
                for db in range(4):
                    dv, (dB,) = wdb[nwd % 4]
                    d3 = dv.rearrange("p (f n) -> p f n", f=12)
                    key = 'wd%d_%d' % (li, nwd % 4)
                    nwd += 1
                    P.dma('pool', d3[:, 0:nfull, :], wd_d[li][f0 * 128:(f0 + nfull) * 128, db * 256:(db + 1) * 256].rearrange("(f p) n -> p f n", p=128),
                          writes=[dB], dkey=key, cost=4.0)
                    if nfull < nf:
                        P.dma('pool', d3[0:64, nfull, :], wd_d[li][21 * 128:21 * 128 + 64, db * 256:(db + 1) * 256],
                              writes=[dB], dkey=key)
                    dblk.append((d3, dB))
                order = [(dc, t) for dc in range(8) for t in range(3)] if hf == 0 else [(dc, t) for t in range(3) for dc in range(8)]
                for dc, t in order:
                    d3, dB = dblk[dc // 2]
                    dcl = dc % 2
                    c = 0 if t == 0 else 1
                    tk = slice(t * 512, (t + 1) * 512)
                    po, poB = ps()
                    for i, fc in enumerate(chunks):
                        w = 128 if fc < 21 else 64
                        P.op('pe', lambda e, po=po, d3=d3, i=i, dcl=dcl, w=w, tk=tk, nf=nf: e.matmul(
                            po, d3[0:w, i, dcl * 128:(dcl + 1) * 128], act3[0:w, i, tk], start=(i == 0), stop=(i == nf - 1)),
                            reads=[dB, actB[i * 3 + t]], writes=[poB], cost=0.215)
                    P.op('dve', lambda e, po=po, dc=dc, tk=tk, c=c: e.scalar_tensor_tensor(
                        xT[:, dc, tk], po, gscap(n, dc, c), xT[:, dc, tk], ALU.mult, ALU.add),
                        reads=[poB, gscBs[n], xB[dc * 3 + t]], writes=[xB[dc * 3 + t]], cost=0.6)
            ps_pool[0] = list(range(8))
            A.release(m1)

        def load_w(view3, cols0, ncols, wB, key):
            P.dma('pool', view3[:, :, 0:ncols], w_in_d[:, cols0:cols0 + ncols].rearrange("(k p) n -> p k n", p=128),
                  writes=[wB], dkey=key)

        def inproj_row(w3, woff, wB, tok0, L, rawv, rawB, cvv=None, cvB=None, w1col=None, bcol=None):
            for t0 in range(0, L, 512):
                n = min(512, L - t0)
                pp, ppB = ps()
                for kc in range(8):
                    P.op('pe', lambda e, pp=pp, kc=kc, t0=t0, n=n: e.matmul(
                        pp[:, 0:n], w3[:, kc, woff:woff + 128], hT[:, kc, tok0 + t0:tok0 + t0 + n], start=(kc == 0), stop=(kc == 7)),
                        reads=[wB] + hb(tok0 + t0, n), writes=[ppB], cost=0.215 * n / 512 + 0.02)
                P.op('act', lambda e, pp=pp, t0=t0, n=n: e.copy(rawv[:, t0:t0 + n], pp[:, 0:n]), reads=[ppB], writes=[rawB])
                if cvv is not None:
                    P.op('act', lambda e, pp=pp, t0=t0, n=n: e.activation(cvv[:, t0:t0 + n], pp[:, 0:n], AF.Identity,
                                                                        bias=pv[:, bcol:bcol + 1], scale=pv[:, w1col:w1col + 1]),
                         reads=[ppB, pvB], writes=[cvB])

        def conv_row(rawv, rawB, cvv, cvB, L, n_rows, wcols, bcol):
            seg = L // n_rows
            r3 = rawv.rearrange("p (r s) -> p r s", r=n_rows)
            c3 = cvv.rearrange("p (r s) -> p r s", r=n_rows)
            P.op('dve', lambda e: e.scalar_tensor_tensor(c3[:, :, 1:seg], r3[:, :, 0:seg - 1], pv[:, wcols[0]:wcols[0] + 1], c3[:, :, 1:seg], ALU.mult, ALU.add),
                 reads=[rawB, pvB, cvB], writes=[cvB])
            P.op('dve', lambda e: e.scalar_tensor_tensor(c3[:, :, 0:seg - 1], r3[:, :, 1:seg], pv[:, wcols[2]:wcols[2] + 1], c3[:, :, 0:seg - 1], ALU.mult, ALU.add),
                 reads=[rawB, pvB, cvB], writes=[cvB])

        def ssd_seq(si, tok0, L, n_rows, wz3, wzB, yssd3, yssdB):
            nT = L // 128
            m1 = A.mark()
            BCT, (BCTB,) = A.alloc('BCT', 5 * L, BF16)
            BCT3 = BCT.rearrange("p (b l) -> p b l", b=5)
            xs_tok, xsB = A.alloc('xs_tok', nT * 512, F32, nT)
            xs3 = xs_tok.rearrange("p (c f) -> p c f", c=nT)
            Btok, BtB = A.alloc('Btok', nT * 128, BF16, nT)
            Bt3 = Btok.rearrange("p (c f) -> p c f", c=nT)
            a_all, aB = A.alloc('a_all', nT * 16, F32, nT)
            dt_all, dtB_ = A.alloc('dt_all', nT * 16, F32, nT)
            E_all, EB = A.alloc('E_all', nT * 32, F32, nT)
            et_all, etB = A.alloc('et_all', nT * 16, F32, nT)
            nC_all, nCB = A.alloc('nC_all', nT * 16, F32, nT)
            pf_all, pfB = A.alloc('pf_all', nT * 256, BF16, nT)
            pb_all, pbB = A.alloc('pb_all', nT * 256, BF16, nT)
            stf, (stfB,) = A.alloc('stf', 256, F32)
            stb, (stbB,) = A.alloc('stb', 256, F32)
            m2 = A.mark()
            wx, (wxB,) = A.alloc('wx', 8 * 768, BF16)
            wx3 = wx.rearrange("p (k n) -> p k n", k=8)
            load_w(wx3, 512, 768, wxB, newkey('wx'))
            raws = [A.alloc('raw%d' % i, L, F32) for i in range(3)]
            cvs = [A.alloc('cv%d' % i, L, F32) for i in range(3)]
            for j in range(6):
                rawv, (rawB,) = raws[j % 3]
                cvv, (cvB,) = cvs[j % 3]
                inproj_row(wx3, j * 128, wxB, tok0, L, rawv, rawB, cvv, cvB, PV_SCW + 6 + j, PV_SCB + j)
                conv_row(rawv, rawB, cvv, cvB, L, n_rows, [PV_SCW + k * 6 + j for k in range(3)], PV_SCB + j)
                if j < 4:
                    P.op('act', lambda e, cvv=cvv: e.activation(cvv, cvv, AF.Silu), reads=[cvB], writes=[cvB])
                    for c0 in range(0, nT, 4):
                        nn = min(4, nT - c0)
                        pp, ppB = ps()
                        for i in range(nn):
                            P.op('pe', lambda e, pp=pp, cvv=cvv, c0=c0, i=i: e.transpose(pp[:, i * 128:(i + 1) * 128], cvv[:, (c0 + i) * 128:(c0 + i + 1) * 128], ident),
                                 reads=[cvB, identB], writes=[ppB])
                        P.op('act', lambda e, pp=pp, c0=c0, nn=nn, j=j: e.copy(xs3[:, c0:c0 + nn, j * 128:(j + 1) * 128],
                                                                             pp[:, 0:nn * 128].rearrange("p (c f) -> p c f", c=nn)),
                             reads=[ppB], writes=[xsB[c0 + i] for i in range(nn)])
                else:
                    P.op('act', lambda e, cvv=cvv: e.activation(cvv, cvv, AF.Silu), reads=[cvB], writes=[cvB])
                    for g in range(2):
                        P.op('dve', lambda e, cvv=cvv, j=j, g=g: e.tensor_scalar(BCT3[:, (j - 4) * 2 + g, :], cvv, pv[:, PV_GM0 + g:PV_GM0 + g + 1], None, ALU.mult),
                             reads=[cvB, pvB], writes=[BCTB])
                    if j == 5:
                        P.op('act', lambda e, cvv=cvv: e.copy(BCT3[:, 4, :], cvv), reads=[cvB], writes=[BCTB])
                    if j == 4:
                        for c in range(nT):
                            pp, ppB = ps()
                            P.op('pe', lambda e, pp=pp, cvv=cvv, c=c: e.transpose(pp[:, 0:128], cvv[:, c * 128:(c + 1) * 128], ident),
                                 reads=[cvB, identB], writes=[ppB])
                            P.op('dve', lambda e, pp=pp, c=c: e.tensor_copy(Bt3[:, c, :], pp[:, 0:128]), reads=[ppB], writes=[BtB[c]])
            A.release(m2)
            import os
            sub = int(os.environ.get('MK_SUB', '9'))
            if sub <= 1:
                A.release(m1)
                return
            Sb_all, SbB = A.alloc('Sb_all', nT * 256, F32, nT)
            wdt, (wdtB,) = A.alloc('wdt', 8 * 16, BF16)
            wdt3 = wdt.rearrange("p (k n) -> p k n", k=8)
            load_w(wdt3, 1280, 16, wdtB, newkey('wdt'))
            smt = [A.alloc('smt%d' % i, 16, F32) for i in range(3)]
            xcd = [A.alloc('xcd%d' % i, 512, BF16) for i in range(6)]
            scf = [A.alloc('scf%d' % i, 16, F32) for i in range(3)]
            stmp = [A.alloc('stmp%d' % i, 256, F32) for i in range(2)]
            if si < 2:
                P.op('dve', lambda e: e.memset(stf, 0.0), writes=[stfB])
                P.op('dve', lambda e: e.memset(stb, 0.0), writes=[stbB])
            else:
                P.dma('sp', stf, stT_d[0], writes=[stfB], dkey=newkey('st'))
                P.dma('sp', stb, stT_d[1], writes=[stbB], dkey=newkey('st'))
            for c in range(nT):
                tk = slice(tok0 + c * 128, tok0 + (c + 1) * 128)
                pd, pdB = ps()
                for kc in range(8):
                    P.op('pe', lambda e, pd=pd, kc=kc, tk=tk: e.matmul(pd[:, 0:16], hT[:, kc, tk], wdt3[:, kc, :], start=(kc == 0), stop=(kc == 7)),
                         reads=[wdtB] + hb(tok0 + c * 128, 128), writes=[pdB])
                sv, (sB,) = smt[c % 3]
                dtc = dt_all[:, c * 16:(c + 1) * 16]
                ac = a_all[:, c * 16:(c + 1) * 16]
                Ec = E_all[:, c * 32:(c + 1) * 32]
                etc_ = et_all[:, c * 16:(c + 1) * 16]
                P.op('dve', lambda e, sv=sv, pd=pd: e.tensor_tensor(sv, pd[:, 0:16], dtb, ALU.add), reads=[pdB, dtbB], writes=[sB])
                P.op('act', lambda e, sv=sv: e.activation(sv, sv, AF.Exp), reads=[sB], writes=[sB])
                P.op('act', lambda e, sv=sv, dtc=dtc: e.activation(dtc, sv, AF.Ln, bias=one_c[:, 0:1]), reads=[sB, one_cB], writes=[dtB_[c]])
                P.op('dve', lambda e, dtc=dtc, ac=ac: e.tensor_tensor(ac, dtc, acoef, ALU.mult), reads=[dtB_[c], acoefB], writes=[aB[c]])
                pc, pcB = ps()
                for i, (mk, col) in enumerate(((LE, 0), (GT, 0), (GE, 8), (LT, 8))):
                    P.op('pe', lambda e, pc=pc, i=i, mk=mk, col=col, ac=ac: e.matmul(pc[:, i * 8:(i + 1) * 8], mk, ac[:, col:col + 8], start=True, stop=True),
                         reads=[masksB, aB[c]], writes=[pcB])
                P.op('pe', lambda e, pc=pc, ac=ac: e.matmul(pc[:, 32:48], ones32, ac, start=True, stop=True), reads=[ones32B, aB[c]], writes=[pcB])
                P.op('act', lambda e, pc=pc, Ec=Ec: e.activation(Ec, pc[:, 0:32], AF.Exp), reads=[pcB], writes=[EB[c]])
                P.op('act', lambda e, pc=pc, c=c: e.activation(nC_all[:, c * 16:c * 16 + 8], pc[:, 0:8], AF.Identity, scale=-1.0), reads=[pcB], writes=[nCB[c]])
                P.op('act', lambda e, pc=pc, c=c: e.activation(nC_all[:, c * 16 + 8:c * 16 + 16], pc[:, 16:24], AF.Identity, scale=-1.0), reads=[pcB, nCB[c]], writes=[nCB[c]])
                P.op('act', lambda e, pc=pc, etc_=etc_: e.activation(etc_, pc[:, 32:48], AF.Exp), reads=[pcB], writes=[etB[c]])
                fv, (fB,) = scf[c % 3]
                P.op('dve', lambda e, fv=fv, dtc=dtc, Ec=Ec: e.tensor_tensor(fv[:, 0:8], dtc[:, 0:8], Ec[:, 8:16], ALU.mult), reads=[dtB_[c], EB[c]], writes=[fB])
                P.op('dve', lambda e, fv=fv, dtc=dtc, Ec=Ec: e.tensor_tensor(fv[:, 8:16], dtc[:, 8:16], Ec[:, 24:32], ALU.mult), reads=[dtB_[c], EB[c], fB], writes=[fB])
                xsc = xs3[:, c, :].rearrange("p (h q) -> p h q", h=8)
                pss = []
                for d_ in range(2):
                    xv, (xdB,) = xcd[(c * 2 + d_) % 6]
                    P.op('dve' if d_ == 0 else 'pool', lambda e, xv=xv, fv=fv, d_=d_, xsc=xsc: e.tensor_tensor(
                        xv.rearrange("p (h q) -> p h q", h=8), xsc, fv[:, d_ * 8:(d_ + 1) * 8].unsqueeze(2).broadcast_to([128, 8, 64]), ALU.mult),
                        reads=[xsB[c], fB], writes=[xdB])
                    pS, pSB = ps()
                    P.op('pe', lambda e, pS=pS, xv=xv, c=c: e.matmul(pS, Bt3[:, c, :], xv, start=True, stop=True), reads=[BtB[c], xdB], writes=[pSB])
                    pss.append((pS, pSB))
                P.op('dve', lambda e, c=c: e.tensor_copy(pf_all[:, c * 256:(c + 1) * 256], stf), reads=[stfB], writes=[pfB[c]])
                tv, (tB,) = stmp[c % 2]
                for g in range(2):
                    rs_ = slice(g * 64, (g + 1) * 64)
                    P.op('dve', lambda e, tv=tv, rs_=rs_, g=g, etc_=etc_: e.tensor_tensor(
                        tv[rs_, :].rearrange("p (h q) -> p h q", h=4), stf[rs_, :].rearrange("p (h q) -> p h q", h=4),
                        etc_[rs_, g * 4:(g + 1) * 4].unsqueeze(2).broadcast_to([64, 4, 64]), ALU.mult),
                        reads=[stfB, etB[c]], writes=[tB])
                    P.op('dve', lambda e, tv=tv, rs_=rs_, g=g, pS=pss[0][0]: e.tensor_tensor(stf[rs_, :], tv[rs_, :], pS[rs_, g * 256:(g + 1) * 256], ALU.add),
                         reads=[tB, pss[0][1]], writes=[stfB])
                    P.op('act', lambda e, rs_=rs_, g=g, c=c, pS=pss[1][0]: e.copy(Sb_all[rs_, c * 256:(c + 1) * 256], pS[rs_, g * 256:(g + 1) * 256]),
                         reads=[pss[1][1]], writes=[SbB[c]])
            for c in range(nT - 1, -1, -1):
                etc_ = et_all[:, c * 16:(c + 1) * 16]
                P.op('dve', lambda e, c=c: e.tensor_copy(pb_all[:, c * 256:(c + 1) * 256], stb), reads=[stbB], writes=[pbB[c]])
                tv, (tB,) = stmp[c % 2]
                for g in range(2):
                    rs_ = slice(g * 64, (g + 1) * 64)
                    P.op('dve', lambda e, tv=tv, rs_=rs_, g=g, etc_=etc_: e.tensor_tensor(
                        tv[rs_, :].rearrange("p (h q) -> p h q", h=4), stb[rs_, :].rearrange("p (h q) -> p h q", h=4),
                        etc_[rs_, 8 + g * 4:8 + (g + 1) * 4].unsqueeze(2).broadcast_to([64, 4, 64]), ALU.mult),
                        reads=[stbB, etB[c]], writes=[tB])
                    P.op('dve', lambda e, tv=tv, rs_=rs_, c=c: e.tensor_tensor(stb[rs_, :], tv[rs_, :], Sb_all[rs_, c * 256:(c + 1) * 256], ALU.add),
                         reads=[tB, SbB[c]], writes=[stbB])
            if si < 2:
                P.dma('sp', nsT_d[si, 0], stf, reads=[stfB], dkey='nso')
                P.dma('sp', nsT_d[si, 1], stb, reads=[stbB], dkey='nso')
            A.release(m2)
            if sub <= 2:
                A.release(m1)
                return
            if wz3 is None:
                wz, (wzB,) = A.alloc('wz', 8 * 512, BF16)
                wz3 = wz.rearrange("p (k n) -> p k n", k=8)
                load_w(wz3, 0, 512, wzB, 'wz')
            zs = [A.alloc('zs%d' % i, 512, F32) for i in range(2)]
            Xb = [A.alloc('Xb%d' % i, 1024, F32) for i in range(2)]
            eD = [A.alloc('eD%d' % i, 512, F32) for i in range(2)]
            nGM = 4 if nT > 2 else 2
            GMb = [A.alloc('GM%d' % i, 512, BF16) for i in range(nGM)]
            nxc = 4 if nT > 2 else 2
            xcb = [A.alloc('xc%d' % i, 512, BF16) for i in range(nxc)]
            yt = [A.alloc('yt%d' % i, 512, F32) for i in range(4)]
            ssq = [A.alloc('ssq%d' % i, 2, F32) for i in range(3)]
            nl = 0
            ne = 0
            ng = 0
            small = len(ps_pool[0]) == 4
            if small:
                Gs, (GsB,) = A.alloc('Gs', 256, F32)

            def bk(k):
                if not small:
                    return ps()
                i = ps_pool[0][k]
                return PS[i][:], PB[i]
            for c in range(nT):
                tk = slice(tok0 + c * 128, tok0 + (c + 1) * 128)
                ac = a_all[:, c * 16:(c + 1) * 16]
                dtc = dt_all[:, c * 16:(c + 1) * 16]
                Ec = E_all[:, c * 32:(c + 1) * 32]
                pz, pzB = bk(0)
                for kc in range(8):
                    P.op('pe', lambda e, pz=pz, kc=kc, tk=tk: e.matmul(pz, hT[:, kc, tk], wz3[:, kc, :], start=(kc == 0), stop=(kc == 7)),
                         reads=[wzB] + hb(tok0 + c * 128, 128), writes=[pzB])
                zv, (zB,) = zs[c % 2]
                P.op('act', lambda e, zv=zv, pz=pz: e.activation(zv, pz, AF.Silu), reads=[pzB], writes=[zB])
                pG, pGB = bk(1)
                for g in range(2):
                    P.op('pe', lambda e, pG=pG, g=g, c=c: e.matmul(pG[:, g * 128:(g + 1) * 128], BCT3[:, g, c * 128:(c + 1) * 128],
                                                                  BCT3[:, 4, c * 128:(c + 1) * 128], start=True, stop=True),
                         reads=[BCTB], writes=[pGB])
                if small:
                    P.op('act', lambda e, pG=pG: e.copy(Gs, pG[:, 0:256]), reads=[pGB], writes=[GsB])
                    Gsrc, GsrcB = Gs, GsB
                else:
                    Gsrc, GsrcB = pG, pGB
                if sub <= 3:
                    continue
                xsc = xs3[:, c, :].rearrange("p (h q) -> p h q", h=8)
                pY, pYB = bk(0)
                xcs = []
                for d_ in range(2):
                    xv, (xcB,) = xcb[(c * 2 + d_) % nxc]
                    P.op('pool', lambda e, xv=xv, d_=d_, xsc=xsc, dtc=dtc: e.tensor_tensor(
                        xv.rearrange("p (h q) -> p h q", h=8), xsc, dtc[:, d_ * 8:(d_ + 1) * 8].unsqueeze(2).broadcast_to([128, 8, 64]), ALU.mult),
                        reads=[xsB[c], dtB_[c]], writes=[xcB])
                    xcs.append((xv, xcB))
                Xs = []
                for d_ in range(2):
                    Xv, (XB,) = Xb[d_]
                    mk = LE if d_ == 0 else GE
                    P.op('dve', lambda e, Xv=Xv, mk=mk, ac=ac, d_=d_: e.tensor_tensor(
                        Xv.rearrange("p (h l) -> p h l", h=8), mk.unsqueeze(1).broadcast_to([128, 8, 128]),
                        ac[:, d_ * 8:(d_ + 1) * 8].unsqueeze(2).broadcast_to([128, 8, 128]), ALU.mult),
                        reads=[masksB, aB[c]], writes=[XB], cost=2.2)
                    Xs.append((Xv, XB))
                for g in range(2):
                    gms = []
                    for d_ in range(2):
                        Xv, XB = Xs[d_]
                        ng_ = NEGF if d_ == 0 else NEGB
                        pD, pDB = bk((2, 3, 1, 2)[g * 2 + d_])
                        P.op('pe', lambda e, pD=pD, Xv=Xv, g=g: e.matmul(pD, ones32, Xv[:, g * 512:(g + 1) * 512], start=True, stop=False),
                             reads=[XB, ones32B], writes=[pDB], cost=0.9)
                        for hh in range(4):
                            P.op('pe', lambda e, pD=pD, hh=hh, ng_=ng_: e.matmul(pD[:, hh * 128:(hh + 1) * 128], ident, ng_, start=False, stop=(hh == 3)),
                                 reads=[identB, masksB], writes=[pDB], cost=0.3)
                        ev, (eB,) = eD[ne % 2]
                        ne += 1
                        for hh in range(4):
                            h = g * 4 + hh
                            P.op('act', lambda e, ev=ev, pD=pD, hh=hh, h=h, d_=d_, c=c: e.activation(
                                ev[:, hh * 128:(hh + 1) * 128], pD[:, hh * 128:(hh + 1) * 128], AF.Exp,
                                bias=nC_all[:, c * 16 + d_ * 8 + h:c * 16 + d_ * 8 + h + 1]), reads=[pDB, nCB[c]], writes=[eB], cost=0.3)
                        gm, (gmB,) = GMb[ng % nGM]
                        ng += 1
                        P.op('dve', lambda e, gm=gm, ev=ev, Gsrc=Gsrc, g=g: e.tensor_tensor(
                            gm.rearrange("p (h l) -> p h l", h=4), ev.rearrange("p (h l) -> p h l", h=4),
                            Gsrc[:, g * 128:(g + 1) * 128].unsqueeze(1).broadcast_to([128, 4, 128]), ALU.mult), reads=[eB, GsrcB], writes=[gmB])
                        gms.append((gm, gmB))
                    for hh in range(4):
                        h = g * 4 + hh
                        for d_ in range(2):
                            gm, gmB = gms[d_]
                            xv, xcB = xcs[d_]
                            P.op('pe', lambda e, pY=pY, gm=gm, hh=hh, h=h, xv=xv, d_=d_: e.matmul(
                                pY[:, h * 64:(h + 1) * 64], gm[:, hh * 128:(hh + 1) * 128], xv[:, h * 64:(h + 1) * 64], start=(d_ == 0), stop=(d_ == 1)),
                                reads=[gmB, xcB], writes=[pYB])
                if sub <= 4:
                    continue
                pZ = []
                for d_ in range(2):
                    pz_, pzB_ = bk((3, 1)[d_])
                    pall = pf_all if d_ == 0 else pb_all
                    pBl = pfB if d_ == 0 else pbB
                    for g in range(2):
                        P.op('pe', lambda e, pz_=pz_, g=g, c=c, pall=pall: e.matmul(
                            pz_[:, g * 256:(g + 1) * 256], BCT3[:, 2 + g, c * 128:(c + 1) * 128], pall[:, c * 256:(c + 1) * 256], start=True, stop=True),
                            reads=[BCTB, pBl[c]], writes=[pzB_])
                    pZ.append((pz_, pzB_))
                y0, (y0B,) = yt[(c * 2) % 4]
                y1, (y1B,) = yt[(c * 2 + 1) % 4]
                y03 = y0.rearrange("p (h q) -> p h q", h=8)
                y13 = y1.rearrange("p (h q) -> p h q", h=8)
                P.op('dve', lambda e, y03=y03, pz_=pZ[0][0], Ec=Ec: e.tensor_tensor(y03, pz_.rearrange("p (h q) -> p h q", h=8),
                                                                                   Ec[:, 0:8].unsqueeze(2).broadcast_to([128, 8, 64]), ALU.mult),
                     reads=[pZ[0][1], EB[c]], writes=[y0B])
                P.op('dve', lambda e, y13=y13, pz_=pZ[1][0], Ec=Ec: e.tensor_tensor(y13, pz_.rearrange("p (h q) -> p h q", h=8),
                                                                                   Ec[:, 16:24].unsqueeze(2).broadcast_to([128, 8, 64]), ALU.mult),
                     reads=[pZ[1][1], EB[c]], writes=[y1B])
                P.op('pool', lambda e, y0=y0, y1=y1: e.tensor_tensor(y0, y0, y1, ALU.add), reads=[y0B, y1B], writes=[y0B])
                P.op('pool', lambda e, y13=y13, xsc=xsc: e.tensor_tensor(y13, xsc, dsk.unsqueeze(2).broadcast_to([128, 8, 64]), ALU.mult),
                     reads=[xsB[c], dskB, y0B], writes=[y1B])
                P.op('dve', lambda e, y0=y0, pY=pY: e.tensor_tensor(y0, y0, pY, ALU.add), reads=[y0B, pYB], writes=[y0B])
                P.op('pool', lambda e, y0=y0, y1=y1: e.tensor_tensor(y0, y0, y1, ALU.add), reads=[y0B, y1B], writes=[y0B])
                P.op('dve', lambda e, y0=y0, zv=zv: e.tensor_tensor(y0, y0, zv, ALU.mult), reads=[y0B, zB], writes=[y0B])
                if sub <= 5:
                    continue
                qv, (qB,) = ssq[c % 3]
                P.op('dve', lambda e, y1=y1, y0=y0: e.tensor_tensor(y1, y0, y0, ALU.mult), reads=[y0B], writes=[y1B])
                P.op('dve', lambda e, y1=y1, qv=qv: e.reduce_sum(qv[:, 0:1], y1, mybir.AxisListType.X), reads=[y1B], writes=[qB])
                P.op('act', lambda e, qv=qv: e.activation(qv[:, 1:2], qv[:, 0:1], AF.Ln, bias=epsc[:, 0:1], scale=1.0 / 512), reads=[qB, epscB], writes=[qB])
                P.op('act', lambda e, qv=qv: e.activation(qv[:, 1:2], qv[:, 1:2], AF.Exp, scale=-0.5), reads=[qB], writes=[qB])
                P.op('dve', lambda e, y0=y0, qv=qv: e.tensor_scalar(y0, y0, qv[:, 1:2], None, ALU.mult), reads=[y0B, qB], writes=[y0B])
                pT, pTB = bk(2)
                for j in range(4):
                    P.op('pe', lambda e, pT=pT, j=j, y0=y0: e.transpose(pT[:, j * 128:(j + 1) * 128], y0[:, j * 128:(j + 1) * 128], ident),
                         reads=[y0B, identB], writes=[pTB])
                for j in range(4):
                    P.op('act', lambda e, pT=pT, j=j, c=c: e.activation(yssd3[:, j, c * 128:(c + 1) * 128], pT[:, j * 128:(j + 1) * 128], AF.Identity,
                                                                      scale=pv[:, PV_SNW + j:PV_SNW + j + 1]), reads=[pTB, pvB], writes=[yssdB])
            A.release(m1)

        def hyena_h2(L, h2v, h2B):
            m1 = A.mark()
            ft, (ftB,) = A.alloc('feats', L, F32)
            h1v, (h1B,) = A.alloc('h1', L, F32)
            tv, (tB,) = A.alloc('harg', L, F32)
            t2v, (t2B,) = A.alloc('harg2', L, F32)
            P.dma('sp', ft[0:33, :], feats_d[L], writes=[ftB], dkey=newkey('ft'))
            for (lw, K, src, srcB, dst, dstB, fbc) in ((hw1, 33, ft, ftB, h1v, h1B, 0), (hw2, 64, h1v, h1B, h2v, h2B, 1)):
                for t0 in range(0, L, 512):
                    n = min(512, L - t0)
                    pp, ppB = ps()
                    P.op('pe', lambda e, pp=pp, lw=lw, K=K, src=src, t0=t0, n=n: e.matmul(pp[0:64, 0:n], lw[0:K, 0:64], src[0:K, t0:t0 + n], start=True, stop=True),
                         reads=[hw1B, hw2B, srcB], writes=[ppB])
                    P.op('dve', lambda e, pp=pp, t0=t0, n=n, fbc=fbc: e.tensor_scalar(tv[0:64, t0:t0 + n], pp[0:64, 0:n], pv[0:64, PV_HFR:PV_HFR + 1], fb[0:64, fbc:fbc + 1], ALU.mult, ALU.add),
                         reads=[ppB, pvB, fbB], writes=[tB])
                for _ in range(2):
                    P.op('dve', lambda e: e.tensor_scalar(t2v[0:64, :], tv[0:64, :], PI, -2 * PI, ALU.is_gt, ALU.mult), reads=[tB], writes=[t2B])
                    P.op('dve', lambda e: e.tensor_tensor(tv[0:64, :], tv[0:64, :], t2v[0:64, :], ALU.add), reads=[tB, t2B], writes=[tB])
                    P.op('dve', lambda e: e.tensor_scalar(t2v[0:64, :], tv[0:64, :], -PI, 2 * PI, ALU.is_lt, ALU.mult), reads=[tB], writes=[t2B])
                    P.op('dve', lambda e: e.tensor_tensor(tv[0:64, :], tv[0:64, :], t2v[0:64, :], ALU.add), reads=[tB, t2B], writes=[tB])
                P.op('act', lambda e, dst=dst: e.activation(dst[0:64, :], tv[0:64, :], AF.Sin), reads=[tB], writes=[dstB])
            A.release(m1)

        def hyena_half(si, tok0, L, n_rows, q, h2v, h2B, yhy3, yhyB):
            nT = L // 128
            N2 = 2 * L
            m0_ = A.mark()
            u_tok = []
            for nm in ('v', 'x1', 'x2'):
                uv, uB = A.alloc('%s_tok' % nm, nT * 256, F32, nT)
                u_tok.append((uv.rearrange("p (c f) -> p c f", c=nT), uB))
            vb, vbB = A.alloc('vb', nT * 256, BF16, nT)
            vb3 = vb.rearrange("p (c f) -> p c f", c=nT)
            m1 = A.mark()
            wh = wh_ring
            raws = [A.alloc('hraw%d' % i, L, F32) for i in range(3)]
            cvs = [A.alloc('hcv%d' % i, L, F32) for i in range(3)]
            nr = 0
            for ui in range(3):
                wi = whn[0] % 2
                whn[0] += 1
                wv, (wB,) = wh[wi]
                wv3 = wv.rearrange("p (k n) -> p k n", k=8)
                col0 = 1296 + ui * 512 + q * 256
                load_w(wv3, col0, 256, wB, 'wh%d' % wi)
                u3, uB = u_tok[ui]
                for chrow in range(2):
                    rawv, (rawB,) = raws[nr % 3]
                    cvv, (cvB,) = cvs[nr % 3]
                    nr += 1
                    j = ui * 4 + q * 2 + chrow
                    inproj_row(wv3, chrow * 128, wB, tok0, L, rawv, rawB, cvv, cvB, PV_HCW + 12 + j, PV_HCB + j)
                    conv_row(rawv, rawB, cvv, cvB, L, n_rows, [PV_HCW + k * 12 + j for k in range(3)], PV_HCB + j)
                    for c0 in range(0, nT, 4):
                        nn = min(4, nT - c0)
                        pp, ppB = ps()
                        for i in range(nn):
                            P.op('pe', lambda e, pp=pp, cvv=cvv, c0=c0, i=i: e.transpose(pp[:, i * 128:(i + 1) * 128], cvv[:, (c0 + i) * 128:(c0 + i + 1) * 128], ident),
                                 reads=[cvB, identB], writes=[ppB], cost=0.3)
                        P.op('act', lambda e, pp=pp, c0=c0, nn=nn, chrow=chrow, u3=u3: e.copy(u3[:, c0:c0 + nn, chrow * 128:(chrow + 1) * 128],
                                                                                           pp[:, 0:nn * 128].rearrange("p (c f) -> p c f", c=nn)),
                             reads=[ppB], writes=[uB[c0 + i] for i in range(nn)], cost=0.7)
                        if ui == 0:
                            P.op('act', lambda e, pp=pp, c0=c0, nn=nn, chrow=chrow: e.copy(vb3[:, c0:c0 + nn, chrow * 128:(chrow + 1) * 128],
                                                                                        pp[:, 0:nn * 128].rearrange("p (c f) -> p c f", c=nn)),
                                 reads=[ppB], writes=[vbB[c0 + i] for i in range(nn)], cost=0.7)
            A.release(m1)
            import os
            hsub = int(os.environ.get('MK_HSUB', '9'))
            if hsub <= 1:
                A.release(m0_)
                return
            Ksp = []
            for o in range(2):
                kc_, kcB = A.alloc('Kc%d' % o, nT * 256, BF16, 1)
                ks_, ksB = A.alloc('Ks%d' % o, nT * 256, BF16, 1)
                Ksp.append((kc_.rearrange("p (c f) -> p c f", c=nT), kcB[0], ks_.rearrange("p (c f) -> p c f", c=nT), ksB[0]))
            m2 = A.mark()
            ksd = []
            for o in range(2):
                a_, aB_ = A.alloc('ksum%d' % o, nT * 256, BF16, 1)
                b_, bB_ = A.alloc('kdif%d' % o, nT * 256, BF16, 1)
                ksd.append((a_.rearrange("p (c f) -> p c f", c=nT), aB_[0], b_.rearrange("p (c f) -> p c f", c=nT), bB_[0]))
            wins = [A.alloc('win%d' % i, 1024, F32) for i in range(3)]
            ktmp = [A.alloc('ktmp%d' % i, 512, F32) for i in range(3)]
            for sc in range(nT):
                wv, (wB,) = wins[sc % 3]
                w4 = wv.rearrange("p (d o f) -> p d o f", d=2, o=2)
                P.dma('sp', wv.rearrange("p (a f) -> p a f", a=4),
                      win_d[L][sc * 128:(sc + 1) * 128, :].rearrange("p (a f) -> p a f", a=4)[:, :, q * 256:(q + 1) * 256],
                      writes=[wB], dkey='win%d' % (sc % 3))
                for o in range(2):
                    pk, pkB = ps()
                    for d_ in range(2):
                        col = d_ * 1024 + o * 512 + q * 256
                        P.op('pe', lambda e, pk=pk, d_=d_, col=col, sc=sc: e.matmul(pk[:, d_ * 256:(d_ + 1) * 256], h2v[0:64, sc * 128:(sc + 1) * 128],
                                                                                   hw3[0:64, col:col + 256], start=True, stop=True),
                             reads=[h2B, hw3B], writes=[pkB])
                    kt, (ktB,) = ktmp[(sc * 2 + o) % 3]
                    P.op('dve', lambda e, kt=kt, pk=pk, w4=w4, o=o: e.tensor_tensor(kt.rearrange("p (d f) -> p d f", d=2), pk.rearrange("p (d f) -> p d f", d=2),
                                                                                     w4[:, :, o, :], ALU.mult), reads=[pkB, wB], writes=[ktB])
                    P.op('pool', lambda e, kt=kt, o=o, sc=sc: e.tensor_tensor(ksd[o][0][:, sc, :], kt[:, 0:256], kt[:, 256:512], ALU.add), reads=[ktB], writes=[ksd[o][1]])
                    P.op('pool', lambda e, kt=kt, o=o, sc=sc: e.tensor_tensor(ksd[o][2][:, sc, :], kt[:, 256:512], kt[:, 0:256], ALU.subtract), reads=[ktB], writes=[ksd[o][3]])
            if hsub <= 2:
                A.release(m0_)
                return
            tbs = [[A.alloc('tb%d_%d' % (k, i), nT * 128, BF16) for i in range(2)] for k in range(2)]
            ntb = [0, 0, 0]

            def load_tab(k, j):
                nb_ = len(tbs[k])
                tv_, (tB_,) = tbs[k][ntb[k] % nb_]
                key = 'tb%d_%d' % (k, ntb[k] % nb_)
                ntb[k] += 1
                P.dma('sp', tv_, tab_d[L][k][j], writes=[tB_], dkey=key)
                return tv_.rearrange("p (c f) -> p c f", c=nT), tB_

            for fc in range(nT):
                tC, tCB = load_tab(0, fc)
                tS, tSB = load_tab(1, fc)
                for o in range(2):
                    K3c, KcB, K3s, KsB = Ksp[o]
                    pc_, pcB_ = ps()
                    for sc in range(nT):
                        P.op('pe', lambda e, pc_=pc_, tC=tC, sc=sc, o=o: e.matmul(pc_[:, 0:256], tC[:, sc, :], ksd[o][0][:, sc, :], start=(sc == 0), stop=(sc == nT - 1)),
                             reads=[tCB, ksd[o][1]], writes=[pcB_])
                    for sc in range(nT):
                        P.op('pe', lambda e, pc_=pc_, tS=tS, sc=sc, o=o: e.matmul(pc_[:, 256:512], tS[:, sc, :], ksd[o][2][:, sc, :], start=(sc == 0), stop=(sc == nT - 1)),
                             reads=[tSB, ksd[o][3]], writes=[pcB_])
                    P.op('act', lambda e, pc_=pc_, K3c=K3c, fc=fc: e.activation(K3c[:, fc, :], pc_[:, 0:256], AF.Identity, scale=2.0 / N2), reads=[pcB_], writes=[KcB])
                    P.op('act', lambda e, pc_=pc_, K3s=K3s, fc=fc: e.activation(K3s[:, fc, :], pc_[:, 256:512], AF.Identity, scale=2.0 / N2), reads=[pcB_], writes=[KsB])
                    if fc == 0:
                        pn, pnB = ps()
                        for sc in range(nT):
                            P.op('pe', lambda e, pn=pn, tS=tS, sc=sc, o=o: e.matmul(pn[0:1, 0:256], tS[:, sc, 0:1], ksd[o][0][:, sc, :], start=(sc == 0), stop=(sc == nT - 1)),
                                 reads=[tSB, ksd[o][1]], writes=[pnB])
                        P.op('act', lambda e, pc_=pc_, K3c=K3c: e.activation(K3c[0:1, 0, :], pc_[0:1, 0:256], AF.Identity, scale=1.0 / N2), reads=[pcB_, KcB], writes=[KcB])
                        P.op('act', lambda e, pn=pn, K3s=K3s: e.activation(K3s[0:1, 0, :], pn[0:1, 0:256], AF.Identity, scale=1.0 / N2), reads=[pnB, KsB], writes=[KsB])
            A.release(m2)
            if hsub <= 3:
                A.release(m0_)
                return
            Pc, (PcB,) = A.alloc('Pc', nT * 256, BF16)
            Pq, (PqB,) = A.alloc('Pq', nT * 256, BF16)
            Pc3 = Pc.rearrange("p (c f) -> p c f", c=nT)
            Pq3 = Pq.rearrange("p (c f) -> p c f", c=nT)
            z1, z1B = A.alloc('zz1', nT * 256, F32, nT)
            z13 = z1.rearrange("p (c f) -> p c f", c=nT)
            z1b, z1bB = A.alloc('zz1b', nT * 256, BF16, nT)
            z1b3 = z1b.rearrange("p (c f) -> p c f", c=nT)
            pt_ = [A.alloc('ptm%d' % i, 256, F32) for i in range(6)]
            tbs = [[A.alloc('tc%d_%d' % (k, i), nT * 128, BF16) for i in range(3 if k < 2 else 2)] for k in range(3)]
            ntb = [0, 0, 0]
            npt = 0
            for o in range(2):
                K3c, KcB, K3s, KsB = Ksp[o]
                zin3, zinB = (vb3, vbB) if o == 0 else (z1b3, z1bB)
                zf3, zfB = u_tok[0] if o == 0 else (z13, z1B)
                g3, gB_ = u_tok[1 + o]
                for fc in range(nT):
                    tC, tCB = load_tab(0, fc)
                    tS, tSB = load_tab(1, fc)
                    pz_, pzB_ = ps()
                    for sc in range(nT):
                        P.op('pe', lambda e, pz_=pz_, tC=tC, sc=sc, zin3=zin3: e.matmul(pz_[:, 0:256], tC[:, sc, :], zin3[:, sc, :], start=(sc == 0), stop=(sc == nT - 1)),
                             reads=[tCB, zinB[sc]], writes=[pzB_])
                    for sc in range(nT):
                        P.op('pe', lambda e, pz_=pz_, tS=tS, sc=sc, zin3=zin3: e.matmul(pz_[:, 256:512], tS[:, sc, :], zin3[:, sc, :], start=(sc == 0), stop=(sc == nT - 1)),
                             reads=[tSB, zinB[sc]], writes=[pzB_])
                    tm = [pt_[(npt + i) % 6] for i in range(4)]
                    npt += 4
                    Zc = pz_[:, 0:256]
                    Zs = pz_[:, 256:512]
                    P.op('dve', lambda e, t=tm[0][0], Zc=Zc, K3c=K3c, fc=fc: e.tensor_tensor(t, Zc, K3c[:, fc, :], ALU.mult), reads=[pzB_, KcB], writes=[tm[0][1][0]])
                    P.op('dve', lambda e, t=tm[1][0], Zs=Zs, K3s=K3s, fc=fc: e.tensor_tensor(t, Zs, K3s[:, fc, :], ALU.mult), reads=[pzB_, KsB], writes=[tm[1][1][0]])
                    P.op('pool', lambda e, t0=tm[0][0], t1=tm[1][0], fc=fc: e.tensor_tensor(Pc3[:, fc, :], t0, t1, ALU.add), reads=[tm[0][1][0], tm[1][1][0]], writes=[PcB])
                    P.op('dve', lambda e, t=tm[2][0], Zs=Zs, K3c=K3c, fc=fc: e.tensor_tensor(t, Zs, K3c[:, fc, :], ALU.mult), reads=[pzB_, KcB], writes=[tm[2][1][0]])
                    P.op('dve', lambda e, t=tm[3][0], Zc=Zc, K3s=K3s, fc=fc: e.tensor_tensor(t, Zc, K3s[:, fc, :], ALU.mult), reads=[pzB_, KsB], writes=[tm[3][1][0]])
                    P.op('pool', lambda e, t2=tm[2][0], t3=tm[3][0], fc=fc: e.tensor_tensor(Pq3[:, fc, :], t2, t3, ALU.subtract), reads=[tm[2][1][0], tm[3][1][0]], writes=[PqB])
                    if fc == 0:
                        P.op('dve', lambda e, Zc=Zc, K3c=K3c: e.tensor_tensor(Pc3[0:1, 0, :], Zc[0:1, :], K3c[0:1, 0, :], ALU.mult), reads=[pzB_, KcB, PcB], writes=[PcB])
                        P.op('dve', lambda e, Zs=Zs, K3s=K3s: e.tensor_tensor(Pq3[0:1, 0, :], Zs[0:1, :], K3s[0:1, 0, :], ALU.mult), reads=[pzB_, KsB, PqB], writes=[PqB])
                for tc in range(nT):
                    tC, tCB = load_tab(0, tc)
                    tT, tTB = load_tab(2, tc)
                    pv_, pvB_ = ps()
                    for fc in range(nT):
                        P.op('pe', lambda e, pv_=pv_, tC=tC, fc=fc: e.matmul(pv_[:, 0:256], tC[:, fc, :], Pc3[:, fc, :], start=(fc == 0), stop=False),
                             reads=[tCB, PcB], writes=[pvB_])
                    for fc in range(nT):
                        P.op('pe', lambda e, pv_=pv_, tT=tT, fc=fc: e.matmul(pv_[:, 0:256], tT[:, fc, :], Pq3[:, fc, :], start=False, stop=(fc == nT - 1)),
                             reads=[tTB, PqB], writes=[pvB_])
                    t0v, (t0B,) = pt_[npt % 6]
                    npt += 1
                    so = o * 512 + q * 256
                    P.op('pool', lambda e, t0v=t0v, zf3=zf3, tc=tc, so=so: e.tensor_tensor(t0v, zf3[:, tc, :], skip[:, so:so + 256], ALU.mult),
                         reads=[zfB[tc], skipB], writes=[t0B])
                    P.op('dve', lambda e, t0v=t0v, pv_=pv_: e.tensor_tensor(t0v, t0v, pv_[:, 0:256], ALU.add), reads=[t0B, pvB_], writes=[t0B])
                    if o == 0:
                        P.op('dve', lambda e, t0v=t0v, g3=g3, tc=tc: e.tensor_tensor(z13[:, tc, :], g3[:, tc, :], t0v, ALU.mult), reads=[t0B, gB_[tc]], writes=[z1B[tc]])
                        P.op('act', lambda e, tc=tc: e.copy(z1b3[:, tc, :], z13[:, tc, :]), reads=[z1B[tc]], writes=[z1bB[tc]])
                    else:
                        P.op('dve', lambda e, t0v=t0v, g3=g3, tc=tc: e.tensor_tensor(t0v, g3[:, tc, :], t0v, ALU.mult), reads=[t0B, gB_[tc]], writes=[t0B])
                        pT, pTB = ps()
                        for j in range(2):
                            P.op('pe', lambda e, pT=pT, j=j, t0v=t0v: e.transpose(pT[:, j * 128:(j + 1) * 128], t0v[:, j * 128:(j + 1) * 128], ident),
                                 reads=[t0B, identB], writes=[pTB])
                        P.op('act', lambda e, pT=pT, tc=tc: e.copy(yhy3[:, q * 2:q * 2 + 2, tc * 128:(tc + 1) * 128], pT[:, 0:256].rearrange("p (j t) -> p j t", j=2)),
                             reads=[pTB], writes=[yhyB])
            A.release(m0_)

        wh_ring = []
        whn = [0]

        def mixer(stage=9):
            norm_stage(1)
            m1 = A.mark()
            wz3 = wzB = None
            wh_ring.extend(A.alloc('wh%d' % i, 8 * 256, BF16) for i in range(2))
            h2s = {}
            for L in (256, 1024):
                h2v, (h2B,) = A.alloc('h2_%d' % L, L, F32)
                hyena_h2(L, h2v, h2B)
                h2s[L] = (h2v, h2B)
            base0 = A.mark()
            wzp, (wzpB,) = A.alloc('wzp', 8 * 512, BF16)
            wzp3 = wzp.rearrange("p (k n) -> p k n", k=8)
            load_w(wzp3, 0, 512, wzpB, 'wz')
            for si, (tok0, L, n_rows, c) in enumerate(SEQS):
                if si == 0:
                    A.hw = A.top
                elif si == 1:
                    A.release(A.hw)
                else:
                    A.release(base0)
                m2 = A.mark()
                ps_pool[0] = [0, 1, 2, 3] if si == 0 else ([4, 5, 6, 7] if si == 1 else list(range(8)))
                yssd, (yssdB,) = A.alloc('yssd', 4 * L, BF16)
                yssd3 = yssd.rearrange("p (j t) -> p j t", j=4)
                yhy, (yhyB,) = A.alloc('yhy', 4 * L, BF16)
                yhy3 = yhy.rearrange("p (j t) -> p j t", j=4)
                if stage >= 3:
                    ssd_seq(si, tok0, L, n_rows, wzp3 if si < 2 else None, wzpB if si < 2 else None, yssd3, yssdB)
                if stage >= 4:
                    for q in range(2):
                        hyena_half(si, tok0, L, n_rows, q, h2s[L][0], h2s[L][1], yhy3, yhyB)
                if stage < 5:
                    A.release(m2)
                    continue
                wo, (woB,) = A.alloc('wo', 8 * 1024, BF16)
                wo3 = wo.rearrange("p (k n) -> p k n", k=8)
                P.dma('pool', wo3, w_out_d.rearrange("(k p) n -> p k n", p=128), writes=[woB], dkey=newkey('wo'))
                for dc in range(8):
                    for t0 in range(0, L, 512):
                        n = min(512, L - t0)
                        po, poB = ps()
                        for mc in range(8):
                            src3, srcB = (yssd3, yssdB) if mc < 4 else (yhy3, yhyB)
                            P.op('pe', lambda e, po=po, mc=mc, dc=dc, t0=t0, n=n, src3=src3, wo3=wo3: e.matmul(
                                po[:, 0:n], wo3[:, mc, dc * 128:(dc + 1) * 128], src3[:, mc % 4, t0:t0 + n], start=(mc == 0), stop=(mc == 7)),
                                reads=[woB, srcB], writes=[poB])
                        xsl = xT[:, dc, tok0 + t0:tok0 + t0 + n]
                        P.op('dve', lambda e, po=po, xsl=xsl, n=n, dc=dc, c=c: e.scalar_tensor_tensor(xsl, po[:, 0:n], gscap(1, dc, c), xsl, ALU.mult, ALU.add),
                             reads=[poB, gscBs[1]] + xb(dc, tok0 + t0, n), writes=xb(dc, tok0 + t0, n))
                A.release(m2)
            A.release(m1)

        import os
        stage = int(os.environ.get('MK_STAGE', '9'))
        if stage >= 1:
            ffn(0, 0)
        if stage >= 2:
            mixer(stage)
        ps_pool[0] = list(range(8))
        if stage >= 9:
            ffn(1, 2)
        m1 = A.mark()
        yo, yoB = A.alloc('yo', 8 * NTOK, F32, 24)
        yo3 = yo.rearrange("p (k t) -> p k t", k=8)
        norm_stage(0, final=True, outv=yo3, outB=yoB)
        yd3 = yT_d.rearrange("p (k t) -> p k t", k=8)
        for kc in range(8):
            P.dma('sp', yd3[:, kc, :], yo3[:, kc, :], reads=[yoB[kc * 3 + t] for t in range(3)], dkey='out', group=True)
        A.release(m1)
        P.emit(['out', 'nso'])
        build_program.peak = A.peak
        build_program.makespan = getattr(P, "makespan", None)
    return nc


_CACHE = {}


def _get_program():
    if 'nc' not in _CACHE:
        _CACHE['nc'] = build_program()
    return _CACHE['nc']


def _consts():
    if 'c' in _CACHE:
        return _CACHE['c']
    c = {"ident": np.eye(128, dtype=np.float32), "masks": _masks()}
    for L in (256, 1024):
        nT = L // 128
        tC, tS, tST = _dft_tables(L)
        c["tabC_%d" % L] = tC.reshape(nT, 128, nT * 128)
        c["tabS_%d" % L] = tS.reshape(nT, 128, nT * 128)
        c["tabST_%d" % L] = tST.reshape(nT, 128, nT * 128)
        ft, wf, wb = _filter_consts(L)
        c["featsT_%d" % L] = ft
        c["win_%d" % L] = np.ascontiguousarray(np.concatenate([wf.reshape(L, 1024), wb.reshape(L, 1024)], axis=1))
    _CACHE['c'] = c
    return c


def _fm(v):
    v = np.asarray(v, np.float32).reshape(-1, 128)
    return np.ascontiguousarray(v.T)


def kernel(x_prompt, x_sample, state_ssd, c, c_ctx, w_ada, b_ada, norm_ffn1, ffn1_w_gate, ffn1_w_up,
           ffn1_w_down, norm_mix, w_in, w_out, ssd_conv_w, ssd_conv_b, ssd_dt_bias, ssd_a_log, ssd_d,
           ssd_norm_w, hy_conv_w, hy_conv_b, hy_w1, hy_b1, hy_freq, hy_w2, hy_b2, hy_w3, hy_skip,
           norm_ffn2, ffn2_w_gate, ffn2_w_up, ffn2_w_down, norm_final):
    f = lambda a: np.ascontiguousarray(np.asarray(a, dtype=np.float32))
    x_prompt, x_sample, state_ssd, c, c_ctx = f(x_prompt), f(x_sample), f(state_ssd), f(c), f(c_ctx)
    nc = _get_program()
    pvm = np.zeros((128, NPV), np.float32)
    pvm[:, PV_BADA:PV_BADA + 72] = _fm(f(b_ada)[0])
    pvm[:, PV_N1:PV_N1 + 8] = _fm(f(norm_ffn1)[0])
    pvm[:, PV_NM:PV_NM + 8] = _fm(f(norm_mix)[0])
    pvm[:, PV_N2:PV_N2 + 8] = _fm(f(norm_ffn2)[0])
    pvm[:, PV_NF:PV_NF + 8] = _fm(f(norm_final))
    for k in range(3):
        pvm[:, PV_SCW + k * 6:PV_SCW + (k + 1) * 6] = _fm(f(ssd_conv_w)[0, k])
        pvm[:, PV_HCW + k * 12:PV_HCW + (k + 1) * 12] = _fm(f(hy_conv_w)[0, k])
    pvm[:, PV_SCB:PV_SCB + 6] = _fm(f(ssd_conv_b)[0])
    pvm[:, PV_HCB:PV_HCB + 12] = _fm(f(hy_conv_b)[0])
    pvm[:, PV_SNW:PV_SNW + 4] = _fm(f(ssd_norm_w)[0])
    pvm[0:64, PV_HB1] = f(hy_b1)[0]
    pvm[0:64, PV_HFR] = f(hy_freq)[0]
    pvm[0:64, PV_HB2] = f(hy_b2)[0]
    pvm[0:64, PV_GM0] = 1.0
    pvm[64:128, PV_GM1] = 1.0
    shared = dict(_consts())
    shared.update({
        "w_ada": f(w_ada)[0], "ffn1_w_gate": f(ffn1_w_gate)[0], "ffn1_w_up": f(ffn1_w_up)[0], "ffn1_w_down": f(ffn1_w_down)[0],
        "ffn2_w_gate": f(ffn2_w_gate)[0], "ffn2_w_up": f(ffn2_w_up)[0], "ffn2_w_down": f(ffn2_w_down)[0],
        "w_in": f(w_in)[0], "w_out": f(w_out)[0], "pv": pvm,
        "dt_bias": f(ssd_dt_bias)[0].reshape(16), "a_log": f(ssd_a_log)[0].reshape(16), "ssd_d": f(ssd_d)[0].reshape(8),
        "hy_skip": f(hy_skip)[0].reshape(1024), "hy_w1": f(hy_w1)[0], "hy_w2": f(hy_w2)[0], "hy_w3": f(hy_w3)[0],
    })
    in_maps = []
    for ci in range(8):
        xt = np.concatenate([x_prompt[2 * ci], x_prompt[2 * ci + 1], x_sample[ci]], axis=0)
        xTm = np.ascontiguousarray(xt.reshape(NTOK, 8, 128).transpose(2, 1, 0)).reshape(128, 8 * NTOK)
        cond = np.stack([c_ctx, c[ci]], axis=0)
        condT = np.ascontiguousarray(cond.reshape(2, 8, 128).transpose(2, 1, 0)).reshape(128, 16)
        s = state_ssd[ci, 0]
        stT = np.ascontiguousarray(s.reshape(2, 2, 4, 64, 64).transpose(0, 1, 4, 2, 3)).reshape(2, 128, 256)
        m = dict(shared)
        m.update({"xT": xTm, "condT": condT, "stT": stT})
        in_maps.append(m)
    res = run_bass_kernel_spmd(nc, in_maps, core_ids=list(range(8)))
    y_prompt = np.empty((16, 256, D), np.float32)
    y_sample = np.empty((8, 1024, D), np.float32)
    new_state = np.empty((16, 1, 2, 8, 64, 64), np.float32)
    for ci in range(8):
        r = res.results[ci]
        yt = np.asarray(r["yT"]).reshape(128, 8, NTOK).transpose(2, 1, 0).reshape(NTOK, D)
        y_prompt[2 * ci] = yt[0:256]
        y_prompt[2 * ci + 1] = yt[256:512]
        y_sample[ci] = yt[512:]
        ns = np.asarray(r["nsT"]).reshape(2, 2, 2, 64, 4, 64)
        new_state[2 * ci:2 * ci + 2, 0] = ns.transpose(0, 1, 2, 4, 5, 3).reshape(2, 2, 8, 64, 64)
    return (y_prompt, y_sample, new_state)
```

```python
import math
import contextlib
import numpy as np
import ml_dtypes
import concourse.bass as bass
import concourse.mybir as mybir
from concourse.bass_utils import run_bass_kernel_spmd

F32 = mybir.dt.float32
BF16 = mybir.dt.bfloat16
AF = mybir.ActivationFunctionType
ALU = mybir.AluOpType

ENGS = ('pe', 'act', 'dve', 'pool', 'sp')
D = 1024
FF = 2752
NTOK = 1536
INC = 2832
RMS_EPS = 1e-6
NFC = 22
HALVES = (list(range(0, 12)), list(range(12, 22)))
SEQS = ((0, 256, 1, 0), (256, 256, 1, 0), (512, 1024, 16, 1))
PI = math.pi


class Buf:
    __slots__ = ('name', 'w', 'rs')

    def __init__(self, name):
        self.name = name
        self.w = None
        self.rs = []


import os as _os
XLAT = float(_os.environ.get('MK_LAT', '1.0'))
_CS = float(_os.environ.get('MK_CS', '1.0'))
DEF_COST = {'pe': 0.15, 'act': 0.45 * _CS, 'dve': 0.55 * _CS, 'pool': 1.0 * _CS, 'sp': 0.06}


class Op:
    __slots__ = ('eng', 'fn', 'deps', 'inc', 'cnt', 'dma', 'dkey', 'dcnt', 'cost', 'idx', 'sdeps')

    def __init__(self, eng, fn, dma=False, dkey=None, cost=None):
        self.eng = eng
        self.fn = fn
        self.deps = []
        self.sdeps = []
        self.cost = cost if cost is not None else (2.5 if dma else DEF_COST[eng])
        self.idx = 0
        self.inc = False
        self.cnt = 0
        self.dma = dma
        self.dkey = dkey
        self.dcnt = 0


class Prog:
    def __init__(self, nc):
        self.nc = nc
        self.q = {e: [] for e in ENGS}
        self.all = []
        self.dkeys = {}
        self.groups = set()
        self.lastdma = {}

    def _add(self, op, reads, writes):
        deps = []
        for b in reads:
            if b.w is not None:
                deps.append(b.w)
        for b in writes:
            if b.w is not None:
                deps.append(b.w)
            deps.extend(b.rs)
        seen = set(id(d) for d in op.deps)
        for d in deps:
            if d is op or id(d) in seen:
                continue
            seen.add(id(d))
            if d.eng == op.eng and not d.dma and not op.dma and op.eng == 'pe':
                op.sdeps.append(d)
                continue
            op.deps.append(d)
        for b in reads:
            b.rs.append(op)
        for b in writes:
            b.w = op
            b.rs = []
        op.idx = len(self.all)
        self.all.append(op)
        return op

    def op(self, eng, fn, reads=(), writes=(), cost=None):
        return self._add(Op(eng, fn, cost=cost), list(reads), list(writes))

    def schedule(self):
        import heapq
        ops = self.all
        n = len(ops)
        succ = [[] for _ in range(n)]
        indeg = [0] * n
        for o in ops:
            ds = set(id(d) for d in o.deps) | set(id(d) for d in o.sdeps)
            o_all = {d.idx for d in o.deps} | {d.idx for d in o.sdeps}
            for di in o_all:
                succ[di].append(o.idx)
            indeg[o.idx] = len(o_all)
        ready = [0.0] * n
        fin = [0.0] * n
        efree = {e: 0.0 for e in ENGS}
        bl = [0.0] * n
        for o in reversed(ops):
            m = 0.0
            for j in succ[o.idx]:
                v = bl[j] + (0.05 if (ops[j].eng == o.eng and not o.dma) else XLAT)
                if v > m:
                    m = v
            bl[o.idx] = o.cost + m
        rs = {e: [] for e in ENGS}
        for o in ops:
            if indeg[o.idx] == 0:
                rs[o.eng].append(o.idx)
        self.q = {e: [] for e in ENGS}
        done = 0
        while done < n:
            best = None
            for e in ENGS:
                if not rs[e]:
                    continue
                t = max(efree[e], min(ready[i] for i in rs[e]))
                if best is None or t < best[0]:
                    best = (t, e)
            t, e = best
            i = max((i for i in rs[e] if ready[i] <= t + 1e-9), key=lambda i: (bl[i], -i))
            rs[e].remove(i)
            o = ops[i]
            if o.dma:
                efree[e] = t + DEF_COST['sp']
            else:
                efree[e] = t + o.cost
            fin[i] = t + o.cost
            self.q[e].append(o)
            done += 1
            for j in succ[i]:
                v = fin[i] + (0.05 if (ops[j].eng == o.eng and not o.dma) else XLAT)
                if v > ready[j]:
                    ready[j] = v
                indeg[j] -= 1
                if indeg[j] == 0:
                    rs[ops[j].eng].append(j)
        assert done == n, (done, n)
        self.makespan = max(fin) if fin else 0.0

    def dma(self, eng, out, in_, reads=(), writes=(), dkey=None, group=False, cost=None):
        o = Op(eng, lambda e: e.dma_start(out=out, in_=in_), dma=True, dkey=dkey, cost=cost)
        self.dkeys[dkey] = self.dkeys.get(dkey, 0) + 1
        o.dcnt = self.dkeys[dkey] * 16
        if group:
            self.groups.add(dkey)
        else:
            prev = self.lastdma.get(dkey)
            if prev is not None:
                o.deps.append(prev)
            self.lastdma[dkey] = o
        return self._add(o, list(reads), list(writes))

    def emit(self, final_dkeys, sched=True):
        nc = self.nc
        if sched:
            self.schedule()
        else:
            self.q = {e: [] for e in ENGS}
            for o in self.all:
                self.q[o.eng].append(o)
        for e in ENGS:
            for o in self.q[e]:
                for d in o.deps:
                    if not d.dma:
                        d.inc = True
        for e in ENGS:
            c = 0
            for o in self.q[e]:
                if o.inc and not o.dma:
                    c += 1
                o.cnt = c
        with contextlib.ExitStack() as st:
            esem = {e: st.enter_context(nc.semaphore('s_' + e)) for e in ENGS if e != 'sp'}
            dsem = {k: st.enter_context(nc.semaphore('d_%d' % i)) for i, k in enumerate(self.dkeys)}
            block = st.enter_context(nc.Block())

            def run(engname, eng):
                known = {}
                for o in self.q[engname]:
                    need = {}
                    for d in o.deps:
                        if d.dma:
                            key, val = ('d', d.dkey), (self.dkeys[d.dkey] * 16 if d.dkey in self.groups else d.dcnt)
                        else:
                            key, val = ('e', d.eng), d.cnt
                        if val > need.get(key, 0):
                            need[key] = val
                    for key, val in need.items():
                        if known.get(key, 0) >= val:
                            continue
                        known[key] = val
                        eng.wait_ge(dsem[key[1]] if key[0] == 'd' else esem[key[1]], val)
                    ins = o.fn(eng)
                    if o.dma:
                        ins.then_inc(dsem[o.dkey], 16)
                    elif o.inc:
                        ins.then_inc(esem[engname], 1)
                if engname == 'sp':
                    for k in final_dkeys:
                        if k not in self.dkeys:
                            continue
                        eng.wait_ge(dsem[k], self.dkeys[k] * 16)

            @block.sync
            def _(e):
                run('sp', e)

            @block.tensor
            def _(e):
                run('pe', e)

            @block.scalar
            def _(e):
                run('act', e)

            @block.vector
            def _(e):
                run('dve', e)

            @block.gpsimd
            def _(e):
                run('pool', e)


class Arena:
    def __init__(self, nc, st, nbytes):
        self.t = st.enter_context(nc.sbuf_tensor("arena", [128, nbytes // 2], BF16))
        self.t32 = self.t.bitcast(F32)
        self.cap = nbytes
        self.top = 0
        self.hist = []
        self.peak = 0
        self.hw = 0

    def alloc(self, name, cols, dt, nbufs=1):
        esz = 4 if dt is F32 else 2
        start = (self.top + 31) // 32 * 32
        end = start + cols * esz
        assert end <= self.cap, (name, end, self.cap)
        self.top = end
        self.peak = max(self.peak, end)
        self.hw = max(self.hw, end)
        bufs = [Buf('%s%d' % (name, i)) for i in range(nbufs)]
        inh = []
        keep = []
        for (s, e, obs) in self.hist:
            if s < end and start < e:
                for ob in obs:
                    if ob.w is not None:
                        inh.append(ob.w)
                    inh.extend(ob.rs)
                if s >= start and e <= end:
                    continue
            keep.append((s, e, obs))
        self.hist = keep
        for b in bufs:
            b.rs = list(inh)
        self.hist.append((start, end, bufs))
        if dt is F32:
            v = self.t32[:, start // 4:start // 4 + cols]
        else:
            v = self.t[:, start // 2:start // 2 + cols]
        return v, bufs

    def mark(self):
        return self.top

    def release(self, m):
        self.top = m


def _dft_tables(L):
    N = 2 * L
    nT = L // 128
    s = np.arange(L, dtype=np.float64)[:, None]
    f = np.arange(L, dtype=np.float64)[None, :]
    C = np.cos(2 * np.pi * s * f / N)
    S = np.sin(2 * np.pi * s * f / N)
    S[:, 0] = (-1.0) ** np.arange(L)

    def blk(M):
        return np.ascontiguousarray(M.reshape(nT, 128, nT, 128).transpose(2, 1, 0, 3)).astype(ml_dtypes.bfloat16)
    return blk(C), blk(S), blk(S.T.copy())


def _filter_consts(L):
    f32 = np.float32
    t = np.linspace(0.0, 1.0, L, dtype=f32)[:, None]
    w = (f32(2.0 * math.pi / L)) * np.arange(L, dtype=f32)[:, None]
    f = np.linspace(1e-4, 16 - 1, 16, dtype=f32)[None, :]
    feats = np.concatenate([t, np.cos(f * w), -np.sin(f * w)], axis=-1).astype(f32)
    mn = math.log(1e-2) / 1.5
    mx = math.log(1e-2) / 0.3
    deltas = np.abs(np.linspace(mn, mx, 1024, dtype=f32))
    window = np.exp(-t[:, :, None] * deltas.reshape(2, 512)).astype(f32)
    wb = window.copy()
    wb[0] = 0.0
    return np.ascontiguousarray(feats.T), window, wb


def _masks():
    t = np.arange(128)
    le = (t[:, None] <= t[None, :])
    gt = (t[:, None] > t[None, :])
    ge = (t[:, None] >= t[None, :])
    lt = (t[:, None] < t[None, :])
    negf = -30000.0 * (t[None, :] < t[:, None])
    negb = -30000.0 * (t[None, :] > t[:, None])
    return np.stack([le, gt, ge, lt, negf, negb]).astype(np.float32).transpose(1, 0, 2).reshape(128, 768).copy()


PV_BADA = 0
PV_N1 = 72
PV_NM = 80
PV_N2 = 88
PV_NF = 96
PV_SCW = 104
PV_SCB = 122
PV_HCW = 128
PV_HCB = 164
PV_SNW = 176
PV_HB1 = 180
PV_HFR = 181
PV_HB2 = 182
PV_GM0 = 184
PV_GM1 = 185
NPV = 186


def build_program():
    nc = bass.Bass("TRN2", target_bir_lowering=False)

    def din(name, shape, dt=F32):
        return nc.dram_tensor(name, list(shape), dt, kind="ExternalInput").ap()

    def dout(name, shape):
        return nc.dram_tensor(name, list(shape), F32, kind="ExternalOutput").ap()

    xT_d = din("xT", [128, 8 * NTOK])
    condT_d = din("condT", [128, 16])
    stT_d = din("stT", [2, 128, 256])
    w_ada_d = din("w_ada", [D, 9 * D])
    wg_d = [din("ffn1_w_gate", [D, FF]), din("ffn2_w_gate", [D, FF])]
    wu_d = [din("ffn1_w_up", [D, FF]), din("ffn2_w_up", [D, FF])]
    wd_d = [din("ffn1_w_down", [FF, D]), din("ffn2_w_down", [FF, D])]
    w_in_d = din("w_in", [D, INC])
    w_out_d = din("w_out", [D, D])
    pv_d = din("pv", [128, NPV])
    dtb_d = din("dt_bias", [16])
    alog_d = din("a_log", [16])
    dsk_d = din("ssd_d", [8])
    skip_d = din("hy_skip", [1024])
    hw1_d = din("hy_w1", [33, 64])
    hw2_d = din("hy_w2", [64, 64])
    hw3_d = din("hy_w3", [64, 2048])
    ident_d = din("ident", [128, 128])
    masks_d = din("masks", [128, 768])
    tab_d = {}
    feats_d = {}
    win_d = {}
    for L in (256, 1024):
        nT = L // 128
        tab_d[L] = [din("tab%s_%d" % (n, L), [nT, 128, nT * 128], BF16) for n in ("C", "S", "ST")]
        feats_d[L] = din("featsT_%d" % L, [33, L])
        win_d[L] = din("win_%d" % L, [L, 2048])
    yT_d = dout("yT", [128, 8 * NTOK])
    nsT_d = dout("nsT", [2, 2, 128, 256])

    st = contextlib.ExitStack()
    with st:
        PS = [st.enter_context(nc.psum_tensor("ps%d" % i, [128, 512], F32)) for i in range(8)]
        PB = [Buf('ps%d' % i) for i in range(8)]
        A = Arena(nc, st, int(_os.environ.get("MK_CAP", "210944")))
        P = Prog(nc)
        psn = [0]

        ps_pool = [list(range(8))]

        def ps():
            pool = ps_pool[0]
            i = pool[psn[0] % len(pool)]
            psn[0] += 1
            return PS[i][:], PB[i]

        dk = [0]

        def newkey(p='k'):
            dk[0] += 1
            return '%s%d' % (p, dk[0])

        xT2, xB = A.alloc('xT', 8 * NTOK, F32, 24)
        xT = xT2.rearrange("p (k t) -> p k t", k=8)
        hT2, hB = A.alloc('hT', 8 * NTOK, BF16, 6)
        hT = hT2.rearrange("p (k t) -> p k t", k=8)
        ident, (identB,) = A.alloc('ident', 128, F32)
        masks, (masksB,) = A.alloc('masks', 768, F32)
        LE, GT, GE, LT, NEGF, NEGB = (masks[:, i * 128:(i + 1) * 128] for i in range(6))
        onesM, (onesMB,) = A.alloc('onesM', 128, BF16)
        ones32, (ones32B,) = A.alloc('ones32', 128, F32)
        pv, (pvB,) = A.alloc('pv', NPV, F32)
        mod, (modB,) = A.alloc('mod', 144, F32)
        Asc, (AscB,) = A.alloc('Asc', 48, F32)
        gsc, (gscB,) = A.alloc('gsc', 48, F32)
        dtb, (dtbB,) = A.alloc('dtb', 16, F32)
        acoef, (acoefB,) = A.alloc('acoef', 16, F32)
        dsk, (dskB,) = A.alloc('dsk', 8, F32)
        skip, (skipB,) = A.alloc('skip', 1024, F32)
        condT, (condTB,) = A.alloc('condT', 16, BF16)
        condf, (condfB,) = A.alloc('condf', 16, F32)
        hw1, (hw1B,) = A.alloc('hw1', 64, F32)
        hw2, (hw2B,) = A.alloc('hw2', 64, F32)
        hw3, (hw3B,) = A.alloc('hw3', 2048, F32)
        fb, (fbB,) = A.alloc('fb', 4, F32)
        negpi, (negpiB,) = A.alloc('negpi', 1, F32)
        epsc, (epscB,) = A.alloc('epsc', 1, F32)
        one_c, (one_cB,) = A.alloc('one_c', 1, F32)

        def xb(dc, tok0, n):
            t0 = tok0 // 512
            t1 = (tok0 + n - 1) // 512
            return [xB[dc * 3 + t] for t in range(t0, t1 + 1)]

        def hb(tok0, n):
            return [hB[i] for i in range(tok0 // 256, (tok0 + n - 1) // 256 + 1)]

        xd3 = xT_d.rearrange("p (k t) -> p k t", k=8)
        for kc in range(8):
            P.dma('sp', xT[:, kc, :], xd3[:, kc, :], writes=[xB[kc * 3 + t] for t in range(3)], dkey='xin', group=True)
        P.dma('sp', ident, ident_d, writes=[identB], dkey='c0', group=True)
        P.dma('sp', masks, masks_d, writes=[masksB], dkey='c0', group=True)
        P.dma('sp', pv, pv_d, writes=[pvB], dkey='c0', group=True)
        P.dma('sp', condf, condT_d, writes=[condfB], dkey='c0', group=True)
        P.dma('sp', dtb, dtb_d.partition_broadcast(128), writes=[dtbB], dkey='c0', group=True)
        P.dma('sp', acoef, alog_d.partition_broadcast(128), writes=[acoefB], dkey='c0', group=True)
        P.dma('sp', dsk, dsk_d.partition_broadcast(128), writes=[dskB], dkey='c0', group=True)
        P.dma('sp', skip, skip_d.partition_broadcast(128), writes=[skipB], dkey='c0', group=True)
        P.dma('sp', hw1[0:33, :], hw1_d, writes=[hw1B], dkey='c0', group=True)
        P.dma('sp', hw2[0:64, :], hw2_d, writes=[hw2B], dkey='c0', group=True)
        P.dma('sp', hw3[0:64, :], hw3_d, writes=[hw3B], dkey='c0', group=True)
        P.op('dve', lambda e: e.memset(onesM, 1.0 / D), writes=[onesMB])
        P.op('dve', lambda e: e.memset(ones32, 1.0), writes=[ones32B])
        P.op('dve', lambda e: e.memset(negpi, -PI), writes=[negpiB])
        P.op('dve', lambda e: e.memset(epsc, RMS_EPS), writes=[epscB])
        P.op('dve', lambda e: e.memset(one_c, 1.0), writes=[one_cB])
        P.op('act', lambda e: e.activation(acoef, acoef, AF.Exp), reads=[acoefB], writes=[acoefB])
        P.op('dve', lambda e: e.tensor_scalar(acoef, acoef, -1.0, None, ALU.mult), reads=[acoefB], writes=[acoefB])
        P.op('dve', lambda e: e.tensor_tensor(fb[0:64, 0:1], pv[0:64, PV_HFR:PV_HFR + 1], pv[0:64, PV_HB1:PV_HB1 + 1], ALU.mult),
             reads=[pvB], writes=[fbB])
        P.op('dve', lambda e: e.tensor_tensor(fb[0:64, 1:2], pv[0:64, PV_HFR:PV_HFR + 1], pv[0:64, PV_HB2:PV_HB2 + 1], ALU.mult),
             reads=[pvB], writes=[fbB])
        P.op('act', lambda e: e.activation(condT, condf, AF.Silu), reads=[condfB], writes=[condTB])

        m0 = A.mark()
        wab = [A.alloc('wab%d' % i, 8 * 512, BF16) for i in range(3)]
        modBs = [Buf('mod%d' % m) for m in range(9)]
        for m in range(9):
            pm, pmB = ps()
            for hb_ in range(2):
                blk = m * 2 + hb_
                wv, (wB,) = wab[blk % 3]
                wv3 = wv.rearrange("p (k n) -> p k n", k=8)
                P.dma('pool', wv3, w_ada_d[:, blk * 512:(blk + 1) * 512].rearrange("(k p) n -> p k n", p=128),
                      writes=[wB], dkey='wab%d' % (blk % 3), cost=5.0)
                for j in range(4):
                    cj = hb_ * 4 + j
                    for kc in range(8):
                        P.op('pe', lambda e, pm=pm, cj=cj, j=j, kc=kc, wv3=wv3: e.matmul(
                            pm[:, cj * 2:cj * 2 + 2], wv3[:, kc, j * 128:(j + 1) * 128], condT[:, kc * 2:kc * 2 + 2],
                            start=(kc == 0), stop=(kc == 7)), reads=[wB, condTB], writes=[pmB])
            P.op('dve', lambda e, pm=pm, m=m: e.tensor_tensor(mod[:, m * 16:(m + 1) * 16].rearrange("p (c o) -> p c o", o=2),
                                                            pm[:, 0:16].rearrange("p (c o) -> p c o", o=2),
                                                            pv[:, PV_BADA + m * 8:PV_BADA + (m + 1) * 8].unsqueeze(2).broadcast_to([128, 8, 2]), ALU.add),
                 reads=[pmB, pvB], writes=[modBs[m]])
        A.release(m0)

        def modap(m, dc, c):
            col = (m * 8 + dc) * 2 + c
            return mod[:, col:col + 1]

        AscBs = [Buf('Asc%d' % i) for i in range(3)]
        gscBs = [Buf('gsc%d' % i) for i in range(3)]
        for n, (pvo, msc, mg, gmul) in enumerate(((PV_N1, 1, 2, 0.5), (PV_NM, 4, 5, 1.0), (PV_N2, 7, 8, 0.5))):
            a3 = Asc[:, n * 16:(n + 1) * 16].rearrange("p (d c) -> p d c", c=2)
            m3 = mod[:, msc * 16:(msc + 1) * 16].rearrange("p (d c) -> p d c", c=2)
            P.op('dve', lambda e, a3=a3, m3=m3: e.tensor_scalar(a3, m3, 1.0, None, ALU.add), reads=[modBs[msc]], writes=[AscBs[n]])
            P.op('dve', lambda e, a3=a3, pvo=pvo: e.tensor_tensor(a3, a3, pv[:, pvo:pvo + 8].unsqueeze(2).broadcast_to([128, 8, 2]), ALU.mult),
                 reads=[AscBs[n], pvB], writes=[AscBs[n]])
            P.op('dve', lambda e, n=n, mg=mg, gmul=gmul: e.tensor_scalar(gsc[:, n * 16:(n + 1) * 16], mod[:, mg * 16:(mg + 1) * 16], gmul, None, ALU.mult),
                 reads=[modBs[mg]], writes=[gscBs[n]])

        def asc(n, dc, c):
            col = n * 16 + dc * 2 + c
            return Asc[:, col:col + 1]

        def gscap(n, dc, c):
            col = n * 16 + dc * 2 + c
            return gsc[:, col:col + 1]

        def norm_stage(n, final=False, outv=None, outB=None):
            m1 = A.mark()
            saved_pool = ps_pool[0]
            ps_pool[0] = [6, 7]
            sq = [A.alloc('sq%d' % i, 512, BF16) for i in range(3)]
            tmp = [A.alloc('nt%d' % i, 512, F32) for i in range(3)]
            rstd = [A.alloc('rstd%d' % i, 512, F32) for i in range(2)]
            cnt = 0
            for t in range(3):
                c = 0 if t == 0 else 1
                tk = slice(t * 512, (t + 1) * 512)
                pt, ptB = ps()
                for kc in range(8):
                    sv, (sB,) = sq[cnt % 3]
                    cnt += 1
                    P.op('act', lambda e, sv=sv, kc=kc, tk=tk: e.activation(sv, xT[:, kc, tk], AF.Square),
                         reads=[xB[kc * 3 + t]], writes=[sB])
                    P.op('pe', lambda e, sv=sv, kc=kc, pt=pt: e.matmul(pt, onesM, sv, start=(kc == 0), stop=(kc == 7)),
                         reads=[sB, onesMB], writes=[ptB], cost=0.215)
                rv, (rB,) = rstd[t % 2]
                P.op('act', lambda e, rv=rv, pt=pt: e.activation(rv, pt, AF.Sqrt, bias=epsc[:, 0:1]), reads=[ptB, epscB], writes=[rB])
                P.op('dve', lambda e, rv=rv: e.reciprocal(rv, rv), reads=[rB], writes=[rB])
                for kc in range(8):
                    tv, (tB,) = tmp[cnt % 3]
                    cnt += 1
                    P.op('dve', lambda e, tv=tv, kc=kc, tk=tk, rv=rv: e.tensor_tensor(tv, xT[:, kc, tk], rv, ALU.mult),
                         reads=[xB[kc * 3 + t], rB], writes=[tB])
                    if final:
                        P.op('act', lambda e, tv=tv, kc=kc, tk=tk: e.activation(outv[:, kc, tk], tv, AF.Identity, scale=pv[:, PV_NF + kc:PV_NF + kc + 1]),
                             reads=[tB, pvB], writes=[outB[kc * 3 + t]])
                    else:
                        P.op('act', lambda e, tv=tv, kc=kc, tk=tk, c=c: e.activation(hT[:, kc, tk], tv, AF.Identity,
                                                                                      bias=modap(3 * n, kc, c), scale=asc(n, kc, c)),
                             reads=[tB, modBs[3 * n], AscBs[n]], writes=hb(t * 512, 512))
            ps_pool[0] = saved_pool
            A.release(m1)

        def ffn(li, n):
            norm_stage(n)
            m1 = A.mark()
            ps_pool[0] = [0, 1, 2, 3, 4, 5]
            actv, actB = A.alloc('act', 12 * NTOK, BF16, 36)
            act3 = actv.rearrange("p (f t) -> p f t", f=12)
            wgb = [A.alloc('wg%d' % i, 8 * 256, BF16) for i in range(3)]
            wub = [A.alloc('wu%d' % i, 8 * 256, BF16) for i in range(3)]
            wdb = [A.alloc('wd%d' % i, 12 * 256, BF16) for i in range(4)]
            sgb = [A.alloc('sg%d' % i, 512, F32) for i in range(3)]
            kg = 'wg%d_' % li
            nblk = 0
            nsg = 0
            nwd = 0
            for hf, chunks in enumerate(HALVES):
                for i, fc in enumerate(chunks):
                    w = 128 if fc < 21 else 64
                    if fc % 2 == 0:
                        bw = 256 if fc < 20 else 192
                        gv, (gB,) = wgb[nblk % 3]
                        uv, (uB,) = wub[nblk % 3]
                        g3 = gv.rearrange("p (k n) -> p k n", k=8)
                        u3 = uv.rearrange("p (k n) -> p k n", k=8)
                        P.dma('pool', g3[:, :, 0:bw], wg_d[li][:, fc * 128:fc * 128 + bw].rearrange("(k p) n -> p k n", p=128),
                              writes=[gB], dkey=kg + 'g%d' % (nblk % 3), cost=4.0)
                        P.dma('pool', u3[:, :, 0:bw], wu_d[li][:, fc * 128:fc * 128 + bw].rearrange("(k p) n -> p k n", p=128),
                              writes=[uB], dkey=kg + 'u%d' % (nblk % 3), cost=4.0)
                        nblk += 1
                    off = (fc % 2) * 128
                    for t in range(3):
                        tk = slice(t * 512, (t + 1) * 512)
                        pg, pgB = ps()
                        pu, puB = ps()
                        for kc in range(8):
                            P.op('pe', lambda e, pg=pg, g3=g3, kc=kc, off=off, w=w, tk=tk: e.matmul(
                                pg[0:w, :], g3[:, kc, off:off + w], hT[:, kc, tk], start=(kc == 0), stop=(kc == 7)),
                                reads=[gB] + hb(t * 512, 512), writes=[pgB], cost=0.215)
                        for kc in range(8):
                            P.op('pe', lambda e, pu=pu, u3=u3, kc=kc, off=off, w=w, tk=tk: e.matmul(
                                pu[0:w, :], u3[:, kc, off:off + w], hT[:, kc, tk], start=(kc == 0), stop=(kc == 7)),
                                reads=[uB] + hb(t * 512, 512), writes=[puB], cost=0.215)
                        sv, (sB,) = sgb[nsg % 3]
                        nsg += 1
                        P.op('act', lambda e, sv=sv, pg=pg, w=w: e.activation(sv[0:w, :], pg[0:w, :], AF.Silu), reads=[pgB], writes=[sB])
                        P.op('dve', lambda e, sv=sv, pu=pu, w=w, i=i, tk=tk: e.tensor_tensor(act3[0:w, i, tk], sv[0:w, :], pu[0:w, :], ALU.mult),
                             reads=[sB, puB], writes=[actB[i * 3 + t]])
                nf = len(chunks)
                f0 = chunks[0]
                nfull = nf if chunks[-1] < 21 else nf - 1
                dblk = []
                for db in range(4):
                    dv, (dB,) = wdb[nwd % 4]
                    d3 = dv.rearrange("p (f n) -> p f n", f=12)
                    key = 'wd%d_%d' % (li, nwd % 4)
                    nwd += 1
                    P.dma('pool', d3[:, 0:nfull, :], wd_d[li][f0 * 128:(f0 + nfull) * 128, db * 256:(db + 1) * 256].rearrange("(f p) n -> p f n", p=128),
                          writes=[dB], dkey=key, cost=4.0)
                    if nfull < nf:
                        P.dma('pool', d3[0:64, nfull, :], wd_d[li][21 * 128:21 * 128 + 64, db * 256:(db + 1) * 256],
                              writes=[dB], dkey=key)
                    dblk.append((d3, dB))
                order = [(dc, t) for dc in range(8) for t in range(3)] if hf == 0 else [(dc, t) for t in range(3) for dc in range(8)]
                for dc, t in order:
                    d3, dB = dblk[dc // 2]
                    dcl = dc % 2
                    c = 0 if t == 0 else 1
                    tk = slice(t * 512, (t + 1) * 512)
                    po, poB = ps()
                    for i, fc in enumerate(chunks):
                        w = 128 if fc < 21 else 64
                        P.op('pe', lambda e, po=po, d3=d3, i=i, dcl=dcl, w=w, tk=tk, nf=nf: e.matmul(
                            po, d3[0:w, i, dcl * 128:(dcl + 1) * 128], act3[0:w, i, tk], start=(i == 0), stop=(i == nf - 1)),
                            reads=[dB, actB[i * 3 + t]], writes=[poB], cost=0.215)
                    P.op('dve', lambda e, po=po, dc=dc, tk=tk, c=c: e.scalar_tensor_tensor(
                        xT[:, dc, tk], po, gscap(n, dc, c), xT[:, dc, tk], ALU.mult, ALU.add),
                        reads=[poB, gscBs[n], xB[dc * 3 + t]], writes=[xB[dc * 3 + t]], cost=0.6)
            ps_pool[0] = list(range(8))
            A.release(m1)

        def load_w(view3, cols0, ncols, wB, key):
            P.dma('pool', view3[:, :, 0:ncols], w_in_d[:, cols0:cols0 + ncols].rearrange("(k p) n -> p k n", p=128),
                  writes=[wB], dkey=key)

        def inproj_row(w3, woff, wB, tok0, L, rawv, rawB, cvv=None, cvB=None, w1col=None, bcol=None):
            for t0 in range(0, L, 512):
                n = min(512, L - t0)
                pp, ppB = ps()
                for kc in range(8):
                    P.op('pe', lambda e, pp=pp, kc=kc, t0=t0, n=n: e.matmul(
                        pp[:, 0:n], w3[:, kc, woff:woff + 128], hT[:, kc, tok0 + t0:tok0 + t0 + n], start=(kc == 0), stop=(kc == 7)),
                        reads=[wB] + hb(tok0 + t0, n), writes=[ppB], cost=0.215 * n / 512 + 0.02)
                P.op('act', lambda e, pp=pp, t0=t0, n=n: e.copy(rawv[:, t0:t0 + n], pp[:, 0:n]), reads=[ppB], writes=[rawB])
                if cvv is not None:
                    P.op('act', lambda e, pp=pp, t0=t0, n=n: e.activation(cvv[:, t0:t0 + n], pp[:, 0:n], AF.Identity,
                                                                        bias=pv[:, bcol:bcol + 1], scale=pv[:, w1col:w1col + 1]),
                         reads=[ppB, pvB], writes=[cvB])

        def conv_row(rawv, rawB, cvv, cvB, L, n_rows, wcols, bcol):
            seg = L // n_rows
            r3 = rawv.rearrange("p (r s) -> p r s", r=n_rows)
            c3 = cvv.rearrange("p (r s) -> p r s", r=n_rows)
            P.op('dve', lambda e: e.scalar_tensor_tensor(c3[:, :, 1:seg], r3[:, :, 0:seg - 1], pv[:, wcols[0]:wcols[0] + 1], c3[:, :, 1:seg], ALU.mult, ALU.add),
                 reads=[rawB, pvB, cvB], writes=[cvB], cost=0.1 + 1.1 * L / 1024)
            P.op('dve', lambda e: e.scalar_tensor_tensor(c3[:, :, 0:seg - 1], r3[:, :, 1:seg], pv[:, wcols[2]:wcols[2] + 1], c3[:, :, 0:seg - 1], ALU.mult, ALU.add),
                 reads=[rawB, pvB, cvB], writes=[cvB], cost=0.1 + 1.1 * L / 1024)

        def ssd_seq(si, tok0, L, n_rows, wz3, wzB, yssd3, yssdB):
            nT = L // 128
            m1 = A.mark()
            BCT, (BCTB,) = A.alloc('BCT', 5 * L, BF16)
            BCT3 = BCT.rearrange("p (b l) -> p b l", b=5)
            xs_tok, xsB = A.alloc('xs_tok', nT * 512, F32, nT)
            xs3 = xs_tok.rearrange("p (c f) -> p c f", c=nT)
            Btok, BtB = A.alloc('Btok', nT * 128, BF16, nT)
            Bt3 = Btok.rearrange("p (c f) -> p c f", c=nT)
            a_all, aB = A.alloc('a_all', nT * 16, F32, nT)
            dt_all, dtB_ = A.alloc('dt_all', nT * 16, F32, nT)
            E_all, EB = A.alloc('E_all', nT * 32, F32, nT)
            et_all, etB = A.alloc('et_all', nT * 16, F32, nT)
            nC_all, nCB = A.alloc('nC_all', nT * 16, F32, nT)
            pf_all, pfB = A.alloc('pf_all', nT * 256, BF16, nT)
            pb_all, pbB = A.alloc('pb_all', nT * 256, BF16, nT)
            stf, (stfB,) = A.alloc('stf', 256, F32)
            stb, (stbB,) = A.alloc('stb', 256, F32)
            m2 = A.mark()
            wx, (wxB,) = A.alloc('wx', 8 * 768, BF16)
            wx3 = wx.rearrange("p (k n) -> p k n", k=8)
            load_w(wx3, 512, 768, wxB, newkey('wx'))
            raws = [A.alloc('raw%d' % i, L, F32) for i in range(3)]
            cvs = [A.alloc('cv%d' % i, L, F32) for i in range(3)]
            for j in range(6):
                rawv, (rawB,) = raws[j % 3]
                cvv, (cvB,) = cvs[j % 3]
                inproj_row(wx3, j * 128, wxB, tok0, L, rawv, rawB, cvv, cvB, PV_SCW + 6 + j, PV_SCB + j)
                conv_row(rawv, rawB, cvv, cvB, L, n_rows, [PV_SCW + k * 6 + j for k in range(3)], PV_SCB + j)
                if j < 4:
                    P.op('act', lambda e, cvv=cvv: e.activation(cvv, cvv, AF.Silu), reads=[cvB], writes=[cvB])
                    for c0 in range(0, nT, 4):
                        nn = min(4, nT - c0)
                        pp, ppB = ps()
                        for i in range(nn):
                            P.op('pe', lambda e, pp=pp, cvv=cvv, c0=c0, i=i: e.transpose(pp[:, i * 128:(i + 1) * 128], cvv[:, (c0 + i) * 128:(c0 + i + 1) * 128], ident),
                                 reads=[cvB, identB], writes=[ppB])
                        P.op('act', lambda e, pp=pp, c0=c0, nn=nn, j=j: e.copy(xs3[:, c0:c0 + nn, j * 128:(j + 1) * 128],
                                                                             pp[:, 0:nn * 128].rearrange("p (c f) -> p c f", c=nn)),
                             reads=[ppB], writes=[xsB[c0 + i] for i in range(nn)])
                else:
                    P.op('act', lambda e, cvv=cvv: e.activation(cvv, cvv, AF.Silu), reads=[cvB], writes=[cvB])
                    for g in range(2):
                        P.op('dve', lambda e, cvv=cvv, j=j, g=g: e.tensor_scalar(BCT3[:, (j - 4) * 2 + g, :], cvv, pv[:, PV_GM0 + g:PV_GM0 + g + 1], None, ALU.mult),
                             reads=[cvB, pvB], writes=[BCTB])
                    if j == 5:
                        P.op('act', lambda e, cvv=cvv: e.copy(BCT3[:, 4, :], cvv), reads=[cvB], writes=[BCTB])
                    if j == 4:
                        for c in range(nT):
                            pp, ppB = ps()
                            P.op('pe', lambda e, pp=pp, cvv=cvv, c=c: e.transpose(pp[:, 0:128], cvv[:, c * 128:(c + 1) * 128], ident),
                                 reads=[cvB, identB], writes=[ppB])
                            P.op('dve', lambda e, pp=pp, c=c: e.tensor_copy(Bt3[:, c, :], pp[:, 0:128]), reads=[ppB], writes=[BtB[c]])
            A.release(m2)
            import os
            sub = int(os.environ.get('MK_SUB', '9'))
            if sub <= 1:
                A.release(m1)
                return
            Sb_all, SbB = A.alloc('Sb_all', nT * 256, F32, nT)
            wdt, (wdtB,) = A.alloc('wdt', 8 * 16, BF16)
            wdt3 = wdt.rearrange("p (k n) -> p k n", k=8)
            load_w(wdt3, 1280, 16, wdtB, newkey('wdt'))
            smt = [A.alloc('smt%d' % i, 16, F32) for i in range(3)]
            xcd = [A.alloc('xcd%d' % i, 512, BF16) for i in range(6)]
            scf = [A.alloc('scf%d' % i, 16, F32) for i in range(3)]
            stmp = [A.alloc('stmp%d' % i, 256, F32) for i in range(2)]
            if si < 2:
                P.op('dve', lambda e: e.memset(stf, 0.0), writes=[stfB])
                P.op('dve', lambda e: e.memset(stb, 0.0), writes=[stbB])
            else:
                P.dma('sp', stf, stT_d[0], writes=[stfB], dkey=newkey('st'))
                P.dma('sp', stb, stT_d[1], writes=[stbB], dkey=newkey('st'))
            for c in range(nT):
                tk = slice(tok0 + c * 128, tok0 + (c + 1) * 128)
                pd, pdB = ps()
                for kc in range(8):
                    P.op('pe', lambda e, pd=pd, kc=kc, tk=tk: e.matmul(pd[:, 0:16], hT[:, kc, tk], wdt3[:, kc, :], start=(kc == 0), stop=(kc == 7)),
                         reads=[wdtB] + hb(tok0 + c * 128, 128), writes=[pdB])
                sv, (sB,) = smt[c % 3]
                dtc = dt_all[:, c * 16:(c + 1) * 16]
                ac = a_all[:, c * 16:(c + 1) * 16]
                Ec = E_all[:, c * 32:(c + 1) * 32]
                etc_ = et_all[:, c * 16:(c + 1) * 16]
                P.op('dve', lambda e, sv=sv, pd=pd: e.tensor_tensor(sv, pd[:, 0:16], dtb, ALU.add), reads=[pdB, dtbB], writes=[sB])
                P.op('act', lambda e, sv=sv: e.activation(sv, sv, AF.Exp), reads=[sB], writes=[sB])
                P.op('act', lambda e, sv=sv, dtc=dtc: e.activation(dtc, sv, AF.Ln, bias=one_c[:, 0:1]), reads=[sB, one_cB], writes=[dtB_[c]])
                P.op('dve', lambda e, dtc=dtc, ac=ac: e.tensor_tensor(ac, dtc, acoef, ALU.mult), reads=[dtB_[c], acoefB], writes=[aB[c]])
                pc, pcB = ps()
                for i, (mk, col) in enumerate(((LE, 0), (GT, 0), (GE, 8), (LT, 8))):
                    P.op('pe', lambda e, pc=pc, i=i, mk=mk, col=col, ac=ac: e.matmul(pc[:, i * 8:(i + 1) * 8], mk, ac[:, col:col + 8], start=True, stop=True),
                         reads=[masksB, aB[c]], writes=[pcB])
                P.op('pe', lambda e, pc=pc, ac=ac: e.matmul(pc[:, 32:48], ones32, ac, start=True, stop=True), reads=[ones32B, aB[c]], writes=[pcB])
                P.op('act', lambda e, pc=pc, Ec=Ec: e.activation(Ec, pc[:, 0:32], AF.Exp), reads=[pcB], writes=[EB[c]])
                P.op('act', lambda e, pc=pc, c=c: e.activation(nC_all[:, c * 16:c * 16 + 8], pc[:, 0:8], AF.Identity, scale=-1.0), reads=[pcB], writes=[nCB[c]])
                P.op('act', lambda e, pc=pc, c=c: e.activation(nC_all[:, c * 16 + 8:c * 16 + 16], pc[:, 16:24], AF.Identity, scale=-1.0), reads=[pcB, nCB[c]], writes=[nCB[c]])
                P.op('act', lambda e, pc=pc, etc_=etc_: e.activation(etc_, pc[:, 32:48], AF.Exp), reads=[pcB], writes=[etB[c]])
                fv, (fB,) = scf[c % 3]
                P.op('dve', lambda e, fv=fv, dtc=dtc, Ec=Ec: e.tensor_tensor(fv[:, 0:8], dtc[:, 0:8], Ec[:, 8:16], ALU.mult), reads=[dtB_[c], EB[c]], writes=[fB])
                P.op('dve', lambda e, fv=fv, dtc=dtc, Ec=Ec: e.tensor_tensor(fv[:, 8:16], dtc[:, 8:16], Ec[:, 24:32], ALU.mult), reads=[dtB_[c], EB[c], fB], writes=[fB])
                xsc = xs3[:, c, :].rearrange("p (h q) -> p h q", h=8)
                pss = []
                for d_ in range(2):
                    xv, (xdB,) = xcd[(c * 2 + d_) % 6]
                    P.op('dve' if d_ == 0 else 'pool', lambda e, xv=xv, fv=fv, d_=d_, xsc=xsc: e.tensor_tensor(
                        xv.rearrange("p (h q) -> p h q", h=8), xsc, fv[:, d_ * 8:(d_ + 1) * 8].unsqueeze(2).broadcast_to([128, 8, 64]), ALU.mult),
                        reads=[xsB[c], fB], writes=[xdB])
                    pS, pSB = ps()
                    P.op('pe', lambda e, pS=pS, xv=xv, c=c: e.matmul(pS, Bt3[:, c, :], xv, start=True, stop=True), reads=[BtB[c], xdB], writes=[pSB])
                    pss.append((pS, pSB))
                P.op('dve', lambda e, c=c: e.tensor_copy(pf_all[:, c * 256:(c + 1) * 256], stf), reads=[stfB], writes=[pfB[c]])
                tv, (tB,) = stmp[c % 2]
                for g in range(2):
                    rs_ = slice(g * 64, (g + 1) * 64)
                    P.op('dve', lambda e, tv=tv, rs_=rs_, g=g, etc_=etc_: e.tensor_tensor(
                        tv[rs_, :].rearrange("p (h q) -> p h q", h=4), stf[rs_, :].rearrange("p (h q) -> p h q", h=4),
                        etc_[rs_, g * 4:(g + 1) * 4].unsqueeze(2).broadcast_to([64, 4, 64]), ALU.mult),
                        reads=[stfB, etB[c]], writes=[tB])
                    P.op('dve', lambda e, tv=tv, rs_=rs_, g=g, pS=pss[0][0]: e.tensor_tensor(stf[rs_, :], tv[rs_, :], pS[rs_, g * 256:(g + 1) * 256], ALU.add),
                         reads=[tB, pss[0][1]], writes=[stfB])
                    P.op('act', lambda e, rs_=rs_, g=g, c=c, pS=pss[1][0]: e.copy(Sb_all[rs_, c * 256:(c + 1) * 256], pS[rs_, g * 256:(g + 1) * 256]),
                         reads=[pss[1][1]], writes=[SbB[c]])
            for c in range(nT - 1, -1, -1):
                etc_ = et_all[:, c * 16:(c + 1) * 16]
                P.op('dve', lambda e, c=c: e.tensor_copy(pb_all[:, c * 256:(c + 1) * 256], stb), reads=[stbB], writes=[pbB[c]])
                tv, (tB,) = stmp[c % 2]
                for g in range(2):
                    rs_ = slice(g * 64, (g + 1) * 64)
                    P.op('dve', lambda e, tv=tv, rs_=rs_, g=g, etc_=etc_: e.tensor_tensor(
                        tv[rs_, :].rearrange("p (h q) -> p h q", h=4), stb[rs_, :].rearrange("p (h q) -> p h q", h=4),
                        etc_[rs_, 8 + g * 4:8 + (g + 1) * 4].unsqueeze(2).broadcast_to([64, 4, 64]), ALU.mult),
                        reads=[stbB, etB[c]], writes=[tB])
                    P.op('dve', lambda e, tv=tv, rs_=rs_, c=c: e.tensor_tensor(stb[rs_, :], tv[rs_, :], Sb_all[rs_, c * 256:(c + 1) * 256], ALU.add),
                         reads=[tB, SbB[c]], writes=[stbB])
            if si < 2:
                P.dma('sp', nsT_d[si, 0], stf, reads=[stfB], dkey='nso')
                P.dma('sp', nsT_d[si, 1], stb, reads=[stbB], dkey='nso')
            A.release(m2)
            if sub <= 2:
                A.release(m1)
                return
            if wz3 is None:
                wz, (wzB,) = A.alloc('wz', 8 * 512, BF16)
                wz3 = wz.rearrange("p (k n) -> p k n", k=8)
                load_w(wz3, 0, 512, wzB, 'wz')
            zs = [A.alloc('zs%d' % i, 512, F32) for i in range(2)]
            Xb = [A.alloc('Xb%d' % i, 1024, F32) for i in range(2)]
            eD = [A.alloc('eD%d' % i, 512, F32) for i in range(2)]
            nGM = 4 if nT > 2 else 2
            GMb = [A.alloc('GM%d' % i, 512, BF16) for i in range(nGM)]
            nxc = 4 if nT > 2 else 2
            xcb = [A.alloc('xc%d' % i, 512, BF16) for i in range(nxc)]
            yt = [A.alloc('yt%d' % i, 512, F32) for i in range(4)]
            ssq = [A.alloc('ssq%d' % i, 2, F32) for i in range(3)]
            nl = 0
            ne = 0
            ng = 0
            small = len(ps_pool[0]) == 4
            if small:
                Gs, (GsB,) = A.alloc('Gs', 256, F32)

            def bk(k):
                if not small:
                    return ps()
                i = ps_pool[0][k]
                return PS[i][:], PB[i]
            for c in range(nT):
                tk = slice(tok0 + c * 128, tok0 + (c + 1) * 128)
                ac = a_all[:, c * 16:(c + 1) * 16]
                dtc = dt_all[:, c * 16:(c + 1) * 16]
                Ec = E_all[:, c * 32:(c + 1) * 32]
                pz, pzB = bk(0)
                for kc in range(8):
                    P.op('pe', lambda e, pz=pz, kc=kc, tk=tk: e.matmul(pz, hT[:, kc, tk], wz3[:, kc, :], start=(kc == 0), stop=(kc == 7)),
                         reads=[wzB] + hb(tok0 + c * 128, 128), writes=[pzB])
                zv, (zB,) = zs[c % 2]
                P.op('act', lambda e, zv=zv, pz=pz: e.activation(zv, pz, AF.Silu), reads=[pzB], writes=[zB])
                pG, pGB = bk(1)
                for g in range(2):
                    P.op('pe', lambda e, pG=pG, g=g, c=c: e.matmul(pG[:, g * 128:(g + 1) * 128], BCT3[:, g, c * 128:(c + 1) * 128],
                                                                  BCT3[:, 4, c * 128:(c + 1) * 128], start=True, stop=True),
                         reads=[BCTB], writes=[pGB])
                if small:
                    P.op('act', lambda e, pG=pG: e.copy(Gs, pG[:, 0:256]), reads=[pGB], writes=[GsB])
                    Gsrc, GsrcB = Gs, GsB
                else:
                    Gsrc, GsrcB = pG, pGB
                if sub <= 3:
                    continue
                xsc = xs3[:, c, :].rearrange("p (h q) -> p h q", h=8)
                pY, pYB = bk(0)
                xcs = []
                for d_ in range(2):
                    xv, (xcB,) = xcb[(c * 2 + d_) % nxc]
                    P.op('pool', lambda e, xv=xv, d_=d_, xsc=xsc, dtc=dtc: e.tensor_tensor(
                        xv.rearrange("p (h q) -> p h q", h=8), xsc, dtc[:, d_ * 8:(d_ + 1) * 8].unsqueeze(2).broadcast_to([128, 8, 64]), ALU.mult),
                        reads=[xsB[c], dtB_[c]], writes=[xcB])
                    xcs.append((xv, xcB))
                Xs = []
                for d_ in range(2):
                    Xv, (XB,) = Xb[d_]
                    mk = LE if d_ == 0 else GE
                    P.op('dve', lambda e, Xv=Xv, mk=mk, ac=ac, d_=d_: e.tensor_tensor(
                        Xv.rearrange("p (h l) -> p h l", h=8), mk.unsqueeze(1).broadcast_to([128, 8, 128]),
                        ac[:, d_ * 8:(d_ + 1) * 8].unsqueeze(2).broadcast_to([128, 8, 128]), ALU.mult),
                        reads=[masksB, aB[c]], writes=[XB], cost=2.2)
                    Xs.append((Xv, XB))
                for g in range(2):
                    gms = []
                    for d_ in range(2):
                        Xv, XB = Xs[d_]
                        ng_ = NEGF if d_ == 0 else NEGB
                        pD, pDB = bk((2, 3, 1, 2)[g * 2 + d_])
                        P.op('pe', lambda e, pD=pD, Xv=Xv, g=g: e.matmul(pD, ones32, Xv[:, g * 512:(g + 1) * 512], start=True, stop=False),
                             reads=[XB, ones32B], writes=[pDB], cost=0.9)
                        for hh in range(4):
                            P.op('pe', lambda e, pD=pD, hh=hh, ng_=ng_: e.matmul(pD[:, hh * 128:(hh + 1) * 128], ident, ng_, start=False, stop=(hh == 3)),
                                 reads=[identB, masksB], writes=[pDB], cost=0.3)
                        ev, (eB,) = eD[ne % 2]
                        ne += 1
                        for hh in range(4):
                            h = g * 4 + hh
                            P.op('act', lambda e, ev=ev, pD=pD, hh=hh, h=h, d_=d_, c=c: e.activation(
                                ev[:, hh * 128:(hh + 1) * 128], pD[:, hh * 128:(hh + 1) * 128], AF.Exp,
                                bias=nC_all[:, c * 16 + d_ * 8 + h:c * 16 + d_ * 8 + h + 1]), reads=[pDB, nCB[c]], writes=[eB], cost=0.3)
                        gm, (gmB,) = GMb[ng % nGM]
                        ng += 1
                        P.op('dve', lambda e, gm=gm, ev=ev, Gsrc=Gsrc, g=g: e.tensor_tensor(
                            gm.rearrange("p (h l) -> p h l", h=4), ev.rearrange("p (h l) -> p h l", h=4),
                            Gsrc[:, g * 128:(g + 1) * 128].unsqueeze(1).broadcast_to([128, 4, 128]), ALU.mult), reads=[eB, GsrcB], writes=[gmB])
                        gms.append((gm, gmB))
                    for hh in range(4):
                        h = g * 4 + hh
                        for d_ in range(2):
                            gm, gmB = gms[d_]
                            xv, xcB = xcs[d_]
                            P.op('pe', lambda e, pY=pY, gm=gm, hh=hh, h=h, xv=xv, d_=d_: e.matmul(
                                pY[:, h * 64:(h + 1) * 64], gm[:, hh * 128:(hh + 1) * 128], xv[:, h * 64:(h + 1) * 64], start=(d_ == 0), stop=(d_ == 1)),
                                reads=[gmB, xcB], writes=[pYB])
                if sub <= 4:
                    continue
                pZ = []
                for d_ in range(2):
                    pz_, pzB_ = bk((3, 1)[d_])
                    pall = pf_all if d_ == 0 else pb_all
                    pBl = pfB if d_ == 0 else pbB
                    for g in range(2):
                        P.op('pe', lambda e, pz_=pz_, g=g, c=c, pall=pall: e.matmul(
                            pz_[:, g * 256:(g + 1) * 256], BCT3[:, 2 + g, c * 128:(c + 1) * 128], pall[:, c * 256:(c + 1) * 256], start=True, stop=True),
                            reads=[BCTB, pBl[c]], writes=[pzB_])
                    pZ.append((pz_, pzB_))
                y0, (y0B,) = yt[(c * 2) % 4]
                y1, (y1B,) = yt[(c * 2 + 1) % 4]
                y03 = y0.rearrange("p (h q) -> p h q", h=8)
                y13 = y1.rearrange("p (h q) -> p h q", h=8)
                P.op('dve', lambda e, y03=y03, pz_=pZ[0][0], Ec=Ec: e.tensor_tensor(y03, pz_.rearrange("p (h q) -> p h q", h=8),
                                                                                   Ec[:, 0:8].unsqueeze(2).broadcast_to([128, 8, 64]), ALU.mult),
                     reads=[pZ[0][1], EB[c]], writes=[y0B])
                P.op('dve', lambda e, y13=y13, pz_=pZ[1][0], Ec=Ec: e.tensor_tensor(y13, pz_.rearrange("p (h q) -> p h q", h=8),
                                                                                   Ec[:, 16:24].unsqueeze(2).broadcast_to([128, 8, 64]), ALU.mult),
                     reads=[pZ[1][1], EB[c]], writes=[y1B])
                P.op('pool', lambda e, y0=y0, y1=y1: e.tensor_tensor(y0, y0, y1, ALU.add), reads=[y0B, y1B], writes=[y0B])
                P.op('pool', lambda e, y13=y13, xsc=xsc: e.tensor_tensor(y13, xsc, dsk.unsqueeze(2).broadcast_to([128, 8, 64]), ALU.mult),
                     reads=[xsB[c], dskB, y0B], writes=[y1B])
                P.op('dve', lambda e, y0=y0, pY=pY: e.tensor_tensor(y0, y0, pY, ALU.add), reads=[y0B, pYB], writes=[y0B])
                P.op('pool', lambda e, y0=y0, y1=y1: e.tensor_tensor(y0, y0, y1, ALU.add), reads=[y0B, y1B], writes=[y0B])
                P.op('dve', lambda e, y0=y0, zv=zv: e.tensor_tensor(y0, y0, zv, ALU.mult), reads=[y0B, zB], writes=[y0B])
                if sub <= 5:
                    continue
                qv, (qB,) = ssq[c % 3]
                P.op('dve', lambda e, y1=y1, y0=y0: e.tensor_tensor(y1, y0, y0, ALU.mult), reads=[y0B], writes=[y1B])
                P.op('dve', lambda e, y1=y1, qv=qv: e.reduce_sum(qv[:, 0:1], y1, mybir.AxisListType.X), reads=[y1B], writes=[qB])
                P.op('act', lambda e, qv=qv: e.activation(qv[:, 1:2], qv[:, 0:1], AF.Ln, bias=epsc[:, 0:1], scale=1.0 / 512), reads=[qB, epscB], writes=[qB])
                P.op('act', lambda e, qv=qv: e.activation(qv[:, 1:2], qv[:, 1:2], AF.Exp, scale=-0.5), reads=[qB], writes=[qB])
                P.op('dve', lambda e, y0=y0, qv=qv: e.tensor_scalar(y0, y0, qv[:, 1:2], None, ALU.mult), reads=[y0B, qB], writes=[y0B])
                pT, pTB = bk(2)
                for j in range(4):
                    P.op('pe', lambda e, pT=pT, j=j, y0=y0: e.transpose(pT[:, j * 128:(j + 1) * 128], y0[:, j * 128:(j + 1) * 128], ident),
                         reads=[y0B, identB], writes=[pTB])
                for j in range(4):
                    P.op('act', lambda e, pT=pT, j=j, c=c: e.activation(yssd3[:, j, c * 128:(c + 1) * 128], pT[:, j * 128:(j + 1) * 128], AF.Identity,
                                                                      scale=pv[:, PV_SNW + j:PV_SNW + j + 1]), reads=[pTB, pvB], writes=[yssdB])
            A.release(m1)

        def hyena_h2(L, h2v, h2B):
            m1 = A.mark()
            ft, (ftB,) = A.alloc('feats', L, F32)
            h1v, (h1B,) = A.alloc('h1', L, F32)
            tv, (tB,) = A.alloc('harg', L, F32)
            t2v, (t2B,) = A.alloc('harg2', L, F32)
            P.dma('sp', ft[0:33, :], feats_d[L], writes=[ftB], dkey=newkey('ft'))
            for (lw, K, src, srcB, dst, dstB, fbc) in ((hw1, 33, ft, ftB, h1v, h1B, 0), (hw2, 64, h1v, h1B, h2v, h2B, 1)):
                for t0 in range(0, L, 512):
                    n = min(512, L - t0)
                    pp, ppB = ps()
                    P.op('pe', lambda e, pp=pp, lw=lw, K=K, src=src, t0=t0, n=n: e.matmul(pp[0:64, 0:n], lw[0:K, 0:64], src[0:K, t0:t0 + n], start=True, stop=True),
                         reads=[hw1B, hw2B, srcB], writes=[ppB])
                    P.op('dve', lambda e, pp=pp, t0=t0, n=n, fbc=fbc: e.tensor_scalar(tv[0:64, t0:t0 + n], pp[0:64, 0:n], pv[0:64, PV_HFR:PV_HFR + 1], fb[0:64, fbc:fbc + 1], ALU.mult, ALU.add),
                         reads=[ppB, pvB, fbB], writes=[tB])
                for _ in range(2):
                    P.op('dve', lambda e: e.tensor_scalar(t2v[0:64, :], tv[0:64, :], PI, -2 * PI, ALU.is_gt, ALU.mult), reads=[tB], writes=[t2B])
                    P.op('dve', lambda e: e.tensor_tensor(tv[0:64, :], tv[0:64, :], t2v[0:64, :], ALU.add), reads=[tB, t2B], writes=[tB])
                    P.op('dve', lambda e: e.tensor_scalar(t2v[0:64, :], tv[0:64, :], -PI, 2 * PI, ALU.is_lt, ALU.mult), reads=[tB], writes=[t2B])
                    P.op('dve', lambda e: e.tensor_tensor(tv[0:64, :], tv[0:64, :], t2v[0:64, :], ALU.add), reads=[tB, t2B], writes=[tB])
                P.op('act', lambda e, dst=dst: e.activation(dst[0:64, :], tv[0:64, :], AF.Sin), reads=[tB], writes=[dstB])
            A.release(m1)

        def hyena_half(si, tok0, L, n_rows, q, h2v, h2B, yhy3, yhyB):
            nT = L // 128
            N2 = 2 * L
            m0_ = A.mark()
            u_tok = []
            for nm in ('v', 'x1', 'x2'):
                uv, uB = A.alloc('%s_tok' % nm, nT * 256, F32, nT)
                u_tok.append((uv.rearrange("p (c f) -> p c f", c=nT), uB))
            vb, vbB = A.alloc('vb', nT * 256, BF16, nT)
            vb3 = vb.rearrange("p (c f) -> p c f", c=nT)
            m1 = A.mark()
            wh = wh_ring
            raws = [A.alloc('hraw%d' % i, L, F32) for i in range(3)]
            cvs = [A.alloc('hcv%d' % i, L, F32) for i in range(3)]
            nr = 0
            for ui in range(3):
                wi = whn[0] % 2
                whn[0] += 1
                wv, (wB,) = wh[wi]
                wv3 = wv.rearrange("p (k n) -> p k n", k=8)
                col0 = 1296 + ui * 512 + q * 256
                load_w(wv3, col0, 256, wB, 'wh%d' % wi)
                u3, uB = u_tok[ui]
                for chrow in range(2):
                    rawv, (rawB,) = raws[nr % 3]
                    cvv, (cvB,) = cvs[nr % 3]
                    nr += 1
                    j = ui * 4 + q * 2 + chrow
                    inproj_row(wv3, chrow * 128, wB, tok0, L, rawv, rawB, cvv, cvB, PV_HCW + 12 + j, PV_HCB + j)
                    conv_row(rawv, rawB, cvv, cvB, L, n_rows, [PV_HCW + k * 12 + j for k in range(3)], PV_HCB + j)
                    for c0 in range(0, nT, 4):
                        nn = min(4, nT - c0)
                        pp, ppB = ps()
                        for i in range(nn):
                            P.op('pe', lambda e, pp=pp, cvv=cvv, c0=c0, i=i: e.transpose(pp[:, i * 128:(i + 1) * 128], cvv[:, (c0 + i) * 128:(c0 + i + 1) * 128], ident),
                                 reads=[cvB, identB], writes=[ppB], cost=0.3)
                        P.op('act', lambda e, pp=pp, c0=c0, nn=nn, chrow=chrow, u3=u3: e.copy(u3[:, c0:c0 + nn, chrow * 128:(chrow + 1) * 128],
                                                                                           pp[:, 0:nn * 128].rearrange("p (c f) -> p c f", c=nn)),
                             reads=[ppB], writes=[uB[c0 + i] for i in range(nn)], cost=0.7)
                        if ui == 0:
                            P.op('act', lambda e, pp=pp, c0=c0, nn=nn, chrow=chrow: e.copy(vb3[:, c0:c0 + nn, chrow * 128:(chrow + 1) * 128],
                                                                                        pp[:, 0:nn * 128].rearrange("p (c f) -> p c f", c=nn)),
                                 reads=[ppB], writes=[vbB[c0 + i] for i in range(nn)], cost=0.7)
            A.release(m1)
            import os
            hsub = int(os.environ.get('MK_HSUB', '9'))
            if hsub <= 1:
                A.release(m0_)
                return
            Ksp = []
            for o in range(2):
                kc_, kcB = A.alloc('Kc%d' % o, nT * 256, BF16, 1)
                ks_, ksB = A.alloc('Ks%d' % o, nT * 256, BF16, 1)
                Ksp.append((kc_.rearrange("p (c f) -> p c f", c=nT), kcB[0], ks_.rearrange("p (c f) -> p c f", c=nT), ksB[0]))
            m2 = A.mark()
            ksd = []
            for o in range(2):
                a_, aB_ = A.alloc('ksum%d' % o, nT * 256, BF16, 1)
                b_, bB_ = A.alloc('kdif%d' % o, nT * 256, BF16, 1)
                ksd.append((a_.rearrange("p (c f) -> p c f", c=nT), aB_[0], b_.rearrange("p (c f) -> p c f", c=nT), bB_[0]))
            wins = [A.alloc('win%d' % i, 1024, F32) for i in range(3)]
            ktmp = [A.alloc('ktmp%d' % i, 512, F32) for i in range(3)]
            for sc in range(nT):
                wv, (wB,) = wins[sc % 3]
                w4 = wv.rearrange("p (d o f) -> p d o f", d=2, o=2)
                P.dma('sp', wv.rearrange("p (a f) -> p a f", a=4),
                      win_d[L][sc * 128:(sc + 1) * 128, :].rearrange("p (a f) -> p a f", a=4)[:, :, q * 256:(q + 1) * 256],
                      writes=[wB], dkey='win%d' % (sc % 3))
                for o in range(2):
                    pk, pkB = ps()
                    for d_ in range(2):
                        col = d_ * 1024 + o * 512 + q * 256
                        P.op('pe', lambda e, pk=pk, d_=d_, col=col, sc=sc: e.matmul(pk[:, d_ * 256:(d_ + 1) * 256], h2v[0:64, sc * 128:(sc + 1) * 128],
                                                                                   hw3[0:64, col:col + 256], start=True, stop=True),
                             reads=[h2B, hw3B], writes=[pkB], cost=0.6)
                    kt, (ktB,) = ktmp[(sc * 2 + o) % 3]
                    P.op('dve', lambda e, kt=kt, pk=pk, w4=w4, o=o: e.tensor_tensor(kt.rearrange("p (d f) -> p d f", d=2), pk.rearrange("p (d f) -> p d f", d=2),
                                                                                     w4[:, :, o, :], ALU.mult), reads=[pkB, wB], writes=[ktB])
                    P.op('pool', lambda e, kt=kt, o=o, sc=sc: e.tensor_tensor(ksd[o][0][:, sc, :], kt[:, 0:256], kt[:, 256:512], ALU.add), reads=[ktB], writes=[ksd[o][1]], cost=0.75)
                    P.op('pool', lambda e, kt=kt, o=o, sc=sc: e.tensor_tensor(ksd[o][2][:, sc, :], kt[:, 256:512], kt[:, 0:256], ALU.subtract), reads=[ktB], writes=[ksd[o][3]], cost=0.75)
            if hsub <= 2:
                A.release(m0_)
                return
            tbs = [[A.alloc('tb%d_%d' % (k, i), nT * 128, BF16) for i in range(2)] for k in range(2)]
            ntb = [0, 0, 0]

            def load_tab(k, j):
                nb_ = len(tbs[k])
                tv_, (tB_,) = tbs[k][ntb[k] % nb_]
                key = 'tb%d_%d' % (k, ntb[k] % nb_)
                ntb[k] += 1
                P.dma('sp', tv_, tab_d[L][k][j], writes=[tB_], dkey=key)
                return tv_.rearrange("p (c f) -> p c f", c=nT), tB_

            for fc in range(nT):
                tC, tCB = load_tab(0, fc)
                tS, tSB = load_tab(1, fc)
                for o in range(2):
                    K3c, KcB, K3s, KsB = Ksp[o]
                    pc_, pcB_ = ps()
                    for sc in range(nT):
                        P.op('pe', lambda e, pc_=pc_, tC=tC, sc=sc, o=o: e.matmul(pc_[:, 0:256], tC[:, sc, :], ksd[o][0][:, sc, :], start=(sc == 0), stop=(sc == nT - 1)),
                             reads=[tCB, ksd[o][1]], writes=[pcB_])
                    for sc in range(nT):
                        P.op('pe', lambda e, pc_=pc_, tS=tS, sc=sc, o=o: e.matmul(pc_[:, 256:512], tS[:, sc, :], ksd[o][2][:, sc, :], start=(sc == 0), stop=(sc == nT - 1)),
                             reads=[tSB, ksd[o][3]], writes=[pcB_])
                    P.op('act', lambda e, pc_=pc_, K3c=K3c, fc=fc: e.activation(K3c[:, fc, :], pc_[:, 0:256], AF.Identity, scale=2.0 / N2), reads=[pcB_], writes=[KcB])
                    P.op('act', lambda e, pc_=pc_, K3s=K3s, fc=fc: e.activation(K3s[:, fc, :], pc_[:, 256:512], AF.Identity, scale=2.0 / N2), reads=[pcB_], writes=[KsB])
                    if fc == 0:
                        pn, pnB = ps()
                        for sc in range(nT):
                            P.op('pe', lambda e, pn=pn, tS=tS, sc=sc, o=o: e.matmul(pn[0:1, 0:256], tS[:, sc, 0:1], ksd[o][0][:, sc, :], start=(sc == 0), stop=(sc == nT - 1)),
                                 reads=[tSB, ksd[o][1]], writes=[pnB])
                        P.op('act', lambda e, pc_=pc_, K3c=K3c: e.activation(K3c[0:1, 0, :], pc_[0:1, 0:256], AF.Identity, scale=1.0 / N2), reads=[pcB_, KcB], writes=[KcB])
                        P.op('act', lambda e, pn=pn, K3s=K3s: e.activation(K3s[0:1, 0, :], pn[0:1, 0:256], AF.Identity, scale=1.0 / N2), reads=[pnB, KsB], writes=[KsB])
            A.release(m2)
            if hsub <= 3:
                A.release(m0_)
                return
            Pc, (PcB,) = A.alloc('Pc', nT * 256, BF16)
            Pq, (PqB,) = A.alloc('Pq', nT * 256, BF16)
            Pc3 = Pc.rearrange("p (c f) -> p c f", c=nT)
            Pq3 = Pq.rearrange("p (c f) -> p c f", c=nT)
            z1, z1B = A.alloc('zz1', nT * 256, F32, nT)
            z13 = z1.rearrange("p (c f) -> p c f", c=nT)
            z1b, z1bB = A.alloc('zz1b', nT * 256, BF16, nT)
            z1b3 = z1b.rearrange("p (c f) -> p c f", c=nT)
            pt_ = [A.alloc('ptm%d' % i, 256, F32) for i in range(6)]
            tbs = [[A.alloc('tc%d_%d' % (k, i), nT * 128, BF16) for i in range(3 if k < 2 else 2)] for k in range(3)]
            ntb = [0, 0, 0]
            npt = 0
            for o in range(2):
                K3c, KcB, K3s, KsB = Ksp[o]
                zin3, zinB = (vb3, vbB) if o == 0 else (z1b3, z1bB)
                zf3, zfB = u_tok[0] if o == 0 else (z13, z1B)
                g3, gB_ = u_tok[1 + o]
                for fc in range(nT):
                    tC, tCB = load_tab(0, fc)
                    tS, tSB = load_tab(1, fc)
                    pz_, pzB_ = ps()
                    for sc in range(nT):
                        P.op('pe', lambda e, pz_=pz_, tC=tC, sc=sc, zin3=zin3: e.matmul(pz_[:, 0:256], tC[:, sc, :], zin3[:, sc, :], start=(sc == 0), stop=(sc == nT - 1)),
                             reads=[tCB, zinB[sc]], writes=[pzB_])
                    for sc in range(nT):
                        P.op('pe', lambda e, pz_=pz_, tS=tS, sc=sc, zin3=zin3: e.matmul(pz_[:, 256:512], tS[:, sc, :], zin3[:, sc, :], start=(sc == 0), stop=(sc == nT - 1)),
                             reads=[tSB, zinB[sc]], writes=[pzB_])
                    tm = [pt_[(npt + i) % 6] for i in range(4)]
                    npt += 4
                    Zc = pz_[:, 0:256]
                    Zs = pz_[:, 256:512]
                    P.op('dve', lambda e, t=tm[0][0], Zc=Zc, K3c=K3c, fc=fc: e.tensor_tensor(t, Zc, K3c[:, fc, :], ALU.mult), reads=[pzB_, KcB], writes=[tm[0][1][0]])
                    P.op('dve', lambda e, t=tm[1][0], Zs=Zs, K3s=K3s, fc=fc: e.tensor_tensor(t, Zs, K3s[:, fc, :], ALU.mult), reads=[pzB_, KsB], writes=[tm[1][1][0]])
                    P.op('pool', lambda e, t0=tm[0][0], t1=tm[1][0], fc=fc: e.tensor_tensor(Pc3[:, fc, :], t0, t1, ALU.add), reads=[tm[0][1][0], tm[1][1][0]], writes=[PcB], cost=0.75)
                    P.op('dve', lambda e, t=tm[2][0], Zs=Zs, K3c=K3c, fc=fc: e.tensor_tensor(t, Zs, K3c[:, fc, :], ALU.mult), reads=[pzB_, KcB], writes=[tm[2][1][0]])
                    P.op('dve', lambda e, t=tm[3][0], Zc=Zc, K3s=K3s, fc=fc: e.tensor_tensor(t, Zc, K3s[:, fc, :], ALU.mult), reads=[pzB_, KsB], writes=[tm[3][1][0]])
                    P.op('pool', lambda e, t2=tm[2][0], t3=tm[3][0], fc=fc: e.tensor_tensor(Pq3[:, fc, :], t2, t3, ALU.subtract), reads=[tm[2][1][0], tm[3][1][0]], writes=[PqB], cost=0.75)
                    if fc == 0:
                        P.op('dve', lambda e, Zc=Zc, K3c=K3c: e.tensor_tensor(Pc3[0:1, 0, :], Zc[0:1, :], K3c[0:1, 0, :], ALU.mult), reads=[pzB_, KcB, PcB], writes=[PcB])
                        P.op('dve', lambda e, Zs=Zs, K3s=K3s: e.tensor_tensor(Pq3[0:1, 0, :], Zs[0:1, :], K3s[0:1, 0, :], ALU.mult), reads=[pzB_, KsB, PqB], writes=[PqB])
                for tc in range(nT):
                    tC, tCB = load_tab(0, tc)
                    tT, tTB = load_tab(2, tc)
                    pv_, pvB_ = ps()
                    for fc in range(nT):
                        P.op('pe', lambda e, pv_=pv_, tC=tC, fc=fc: e.matmul(pv_[:, 0:256], tC[:, fc, :], Pc3[:, fc, :], start=(fc == 0), stop=False),
                             reads=[tCB, PcB], writes=[pvB_])
                    for fc in range(nT):
                        P.op('pe', lambda e, pv_=pv_, tT=tT, fc=fc: e.matmul(pv_[:, 0:256], tT[:, fc, :], Pq3[:, fc, :], start=False, stop=(fc == nT - 1)),
                             reads=[tTB, PqB], writes=[pvB_])
                    t0v, (t0B,) = pt_[npt % 6]
                    npt += 1
                    so = o * 512 + q * 256
                    P.op('pool', lambda e, t0v=t0v, zf3=zf3, tc=tc, so=so: e.tensor_tensor(t0v, zf3[:, tc, :], skip[:, so:so + 256], ALU.mult),
                         reads=[zfB[tc], skipB], writes=[t0B])
                    P.op('dve', lambda e, t0v=t0v, pv_=pv_: e.tensor_tensor(t0v, t0v, pv_[:, 0:256], ALU.add), reads=[t0B, pvB_], writes=[t0B])
                    if o == 0:
                        P.op('dve', lambda e, t0v=t0v, g3=g3, tc=tc: e.tensor_tensor(z13[:, tc, :], g3[:, tc, :], t0v, ALU.mult), reads=[t0B, gB_[tc]], writes=[z1B[tc]])
                        P.op('act', lambda e, tc=tc: e.copy(z1b3[:, tc, :], z13[:, tc, :]), reads=[z1B[tc]], writes=[z1bB[tc]])
                    else:
                        P.op('dve', lambda e, t0v=t0v, g3=g3, tc=tc: e.tensor_tensor(t0v, g3[:, tc, :], t0v, ALU.mult), reads=[t0B, gB_[tc]], writes=[t0B])
                        pT, pTB = ps()
                        for j in range(2):
                            P.op('pe', lambda e, pT=pT, j=j, t0v=t0v: e.transpose(pT[:, j * 128:(j + 1) * 128], t0v[:, j * 128:(j + 1) * 128], ident),
                                 reads=[t0B, identB], writes=[pTB])
                        P.op('act', lambda e, pT=pT, tc=tc: e.copy(yhy3[:, q * 2:q * 2 + 2, tc * 128:(tc + 1) * 128], pT[:, 0:256].rearrange("p (j t) -> p j t", j=2)),
                             reads=[pTB], writes=[yhyB])
            A.release(m0_)

        wh_ring = []
        whn = [0]

        def mixer(stage=9):
            norm_stage(1)
            m1 = A.mark()
            wz3 = wzB = None
            wh_ring.extend(A.alloc('wh%d' % i, 8 * 256, BF16) for i in range(2))
            h2s = {}
            for L in (256, 1024):
                h2v, (h2B,) = A.alloc('h2_%d' % L, L, F32)
                hyena_h2(L, h2v, h2B)
                h2s[L] = (h2v, h2B)
            base0 = A.mark()
            wzp, (wzpB,) = A.alloc('wzp', 8 * 512, BF16)
            wzp3 = wzp.rearrange("p (k n) -> p k n", k=8)
            load_w(wzp3, 0, 512, wzpB, 'wz')
            for si, (tok0, L, n_rows, c) in enumerate(SEQS):
                if si == 0:
                    A.hw = A.top
                elif si == 1:
                    A.release(A.hw)
                else:
                    A.release(base0)
                m2 = A.mark()
                ps_pool[0] = [0, 1, 2, 3] if si == 0 else ([4, 5, 6, 7] if si == 1 else list(range(8)))
                yssd, (yssdB,) = A.alloc('yssd', 4 * L, BF16)
                yssd3 = yssd.rearrange("p (j t) -> p j t", j=4)
                yhy, (yhyB,) = A.alloc('yhy', 4 * L, BF16)
                yhy3 = yhy.rearrange("p (j t) -> p j t", j=4)
                if stage >= 3:
                    ssd_seq(si, tok0, L, n_rows, wzp3 if si < 2 else None, wzpB if si < 2 else None, yssd3, yssdB)
                if stage >= 4:
                    for q in range(2):
                        hyena_half(si, tok0, L, n_rows, q, h2s[L][0], h2s[L][1], yhy3, yhyB)
                if stage < 5:
                    A.release(m2)
                    continue
                wo, (woB,) = A.alloc('wo', 8 * 1024, BF16)
                wo3 = wo.rearrange("p (k n) -> p k n", k=8)
                P.dma('pool', wo3, w_out_d.rearrange("(k p) n -> p k n", p=128), writes=[woB], dkey=newkey('wo'))
                for dc in range(8):
                    for t0 in range(0, L, 512):
                        n = min(512, L - t0)
                        po, poB = ps()
                        for mc in range(8):
                            src3, srcB = (yssd3, yssdB) if mc < 4 else (yhy3, yhyB)
                            P.op('pe', lambda e, po=po, mc=mc, dc=dc, t0=t0, n=n, src3=src3, wo3=wo3: e.matmul(
                                po[:, 0:n], wo3[:, mc, dc * 128:(dc + 1) * 128], src3[:, mc % 4, t0:t0 + n], start=(mc == 0), stop=(mc == 7)),
                                reads=[woB, srcB], writes=[poB])
                        xsl = xT[:, dc, tok0 + t0:tok0 + t0 + n]
                        P.op('dve', lambda e, po=po, xsl=xsl, n=n, dc=dc, c=c: e.scalar_tensor_tensor(xsl, po[:, 0:n], gscap(1, dc, c), xsl, ALU.mult, ALU.add),
                             reads=[poB, gscBs[1]] + xb(dc, tok0 + t0, n), writes=xb(dc, tok0 + t0, n))
                A.release(m2)
            A.release(m1)

        import os
        stage = int(os.environ.get('MK_STAGE', '9'))
        if stage >= 1:
            ffn(0, 0)
        if stage >= 2:
            mixer(stage)
        ps_pool[0] = list(range(8))
        if stage >= 9:
            ffn(1, 2)
        m1 = A.mark()
        yo, yoB = A.alloc('yo', 8 * NTOK, F32, 24)
        yo3 = yo.rearrange("p (k t) -> p k t", k=8)
        norm_stage(0, final=True, outv=yo3, outB=yoB)
        yd3 = yT_d.rearrange("p (k t) -> p k t", k=8)
        for kc in range(8):
            P.dma('sp', yd3[:, kc, :], yo3[:, kc, :], reads=[yoB[kc * 3 + t] for t in range(3)], dkey='out', group=True)
        A.release(m1)
        P.emit(['out', 'nso'])
        build_program.peak = A.peak
        build_program.makespan = getattr(P, "makespan", None)
    return nc


_CACHE = {}


def _get_program():
    if 'nc' not in _CACHE:
        _CACHE['nc'] = build_program()
    return _CACHE['nc']


def _consts():
    if 'c' in _CACHE:
        return _CACHE['c']
    c = {"ident": np.eye(128, dtype=np.float32), "masks": _masks()}
    for L in (256, 1024):
        nT = L // 128
        tC, tS, tST = _dft_tables(L)
        c["tabC_%d" % L] = tC.reshape(nT, 128, nT * 128)
        c["tabS_%d" % L] = tS.reshape(nT, 128, nT * 128)
        c["tabST_%d" % L] = tST.reshape(nT, 128, nT * 128)
        ft, wf, wb = _filter_consts(L)
        c["featsT_%d" % L] = ft
        c["win_%d" % L] = np.ascontiguousarray(np.concatenate([wf.reshape(L, 1024), wb.reshape(L, 1024)], axis=1))
    _CACHE['c'] = c
    return c


def _fm(v):
    v = np.asarray(v, np.float32).reshape(-1, 128)
    return np.ascontiguousarray(v.T)


def kernel(x_prompt, x_sample, state_ssd, c, c_ctx, w_ada, b_ada, norm_ffn1, ffn1_w_gate, ffn1_w_up,
           ffn1_w_down, norm_mix, w_in, w_out, ssd_conv_w, ssd_conv_b, ssd_dt_bias, ssd_a_log, ssd_d,
           ssd_norm_w, hy_conv_w, hy_conv_b, hy_w1, hy_b1, hy_freq, hy_w2, hy_b2, hy_w3, hy_skip,
           norm_ffn2, ffn2_w_gate, ffn2_w_up, ffn2_w_down, norm_final):
    f = lambda a: np.ascontiguousarray(np.asarray(a, dtype=np.float32))
    x_prompt, x_sample, state_ssd, c, c_ctx = f(x_prompt), f(x_sample), f(state_ssd), f(c), f(c_ctx)
    nc = _get_program()
    pvm = np.zeros((128, NPV), np.float32)
    pvm[:, PV_BADA:PV_BADA + 72] = _fm(f(b_ada)[0])
    pvm[:, PV_N1:PV_N1 + 8] = _fm(f(norm_ffn1)[0])
    pvm[:, PV_NM:PV_NM + 8] = _fm(f(norm_mix)[0])
    pvm[:, PV_N2:PV_N2 + 8] = _fm(f(norm_ffn2)[0])
    pvm[:, PV_NF:PV_NF + 8] = _fm(f(norm_final))
    for k in range(3):
        pvm[:, PV_SCW + k * 6:PV_SCW + (k + 1) * 6] = _fm(f(ssd_conv_w)[0, k])
        pvm[:, PV_HCW + k * 12:PV_HCW + (k + 1) * 12] = _fm(f(hy_conv_w)[0, k])
    pvm[:, PV_SCB:PV_SCB + 6] = _fm(f(ssd_conv_b)[0])
    pvm[:, PV_HCB:PV_HCB + 12] = _fm(f(hy_conv_b)[0])
    pvm[:, PV_SNW:PV_SNW + 4] = _fm(f(ssd_norm_w)[0])
    pvm[0:64, PV_HB1] = f(hy_b1)[0]
    pvm[0:64, PV_HFR] = f(hy_freq)[0]
    pvm[0:64, PV_HB2] = f(hy_b2)[0]
    pvm[0:64, PV_GM0] = 1.0
    pvm[64:128, PV_GM1] = 1.0
    shared = dict(_consts())
    shared.update({
        "w_ada": f(w_ada)[0], "ffn1_w_gate": f(ffn1_w_gate)[0], "ffn1_w_up": f(ffn1_w_up)[0], "ffn1_w_down": f(ffn1_w_down)[0],
        "ffn2_w_gate": f(ffn2_w_gate)[0], "ffn2_w_up": f(ffn2_w_up)[0], "ffn2_w_down": f(ffn2_w_down)[0],
        "w_in": f(w_in)[0], "w_out": f(w_out)[0], "pv": pvm,
        "dt_bias": f(ssd_dt_bias)[0].reshape(16), "a_log": f(ssd_a_log)[0].reshape(16), "ssd_d": f(ssd_d)[0].reshape(8),
        "hy_skip": f(hy_skip)[0].reshape(1024), "hy_w1": f(hy_w1)[0], "hy_w2": f(hy_w2)[0], "hy_w3": f(hy_w3)[0],
    })
    in_maps = []
    for ci in range(8):
        xt = np.concatenate([x_prompt[2 * ci], x_prompt[2 * ci + 1], x_sample[ci]], axis=0)
        xTm = np.ascontiguousarray(xt.reshape(NTOK, 8, 128).transpose(2, 1, 0)).reshape(128, 8 * NTOK)
        cond = np.stack([c_ctx, c[ci]], axis=0)
        condT = np.ascontiguousarray(cond.reshape(2, 8, 128).transpose(2, 1, 0)).reshape(128, 16)
        s = state_ssd[ci, 0]
        stT = np.ascontiguousarray(s.reshape(2, 2, 4, 64, 64).transpose(0, 1, 4, 2, 3)).reshape(2, 128, 256)
        m = dict(shared)
        m.update({"xT": xTm, "condT": condT, "stT": stT})
        in_maps.append(m)
    res = run_bass_kernel_spmd(nc, in_maps, core_ids=list(range(8)))
    y_prompt = np.empty((16, 256, D), np.float32)
    y_sample = np.empty((8, 1024, D), np.float32)
    new_state = np.empty((16, 1, 2, 8, 64, 64), np.float32)
    for ci in range(8):
        r = res.results[ci]
        yt = np.asarray(r["yT"]).reshape(128, 8, NTOK).transpose(2, 1, 0).reshape(NTOK, D)
        y_prompt[2 * ci] = yt[0:256]
        y_prompt[2 * ci + 1] = yt[256:512]
        y_sample[ci] = yt[512:]
        ns = np.asarray(r["nsT"]).reshape(2, 2, 2, 64, 4, 64)
        new_state[2 * ci:2 * ci + 2, 0] = ns.transpose(0, 1, 2, 4, 5, 3).reshape(2, 2, 8, 64, 64)
    return (y_prompt, y_sample, new_state)
```
